# Optimizing a Trainium2 kernel written in Bass

```python
import math
import jax, jax.numpy as jnp
from jax import lax
import numpy as np

D_MODEL = 1024
BATCH = 16
SEQ = 2048
DEPTH = 2

H_A = 8
D_A = 64
H_I = 4
D_I = 64
TOPK_MAX = 256
H_B = 4
D_B = 64
D_FF = 2816
CONV_W = 3
PLE_DIM = 256
ROPE_THETA = 10000.0
Q_BLOCK = 128
EPS = 1e-6

W_A = H_A * D_A
W_B = H_B * 2 * D_B
N_IN = W_A + D_A + D_A + H_I * D_I + D_I + H_I + 2 * (H_B * 2 * D_B) + W_B + 2 * D_MODEL

kernel_name = 'hybrid_dsa_diffattn_convffn_ple'


def _split_points():
    sizes = [W_A, D_A, D_A, H_I * D_I, D_I, H_I, H_B * 2 * D_B, H_B * 2 * D_B, W_B, D_MODEL, D_MODEL]
    pts, acc = [], 0
    for s in sizes[:-1]:
        acc += s
        pts.append(acc)
    return pts


def rms_norm(x, g):
    xf = x.astype(jnp.float32)
    y = xf * lax.rsqrt(jnp.mean(xf * xf, axis=-1, keepdims=True) + EPS)
    return (y * g.astype(jnp.float32)).astype(x.dtype)


def rope_tables(length, dim):
    inv = 1.0 / (ROPE_THETA ** (jnp.arange(0, dim, 2, dtype=jnp.float32) / dim))
    ang = jnp.arange(length, dtype=jnp.float32)[:, None] * inv[None, :]
    return jnp.cos(ang), jnp.sin(ang)


def apply_rope(x, cos, sin):
    half = x.shape[-1] // 2
    shape = (cos.shape[0],) + (1,) * (x.ndim - 3) + (half,)
    c, s = cos.reshape(shape), sin.reshape(shape)
    xf = x.astype(jnp.float32)
    x1, x2 = xf[..., :half], xf[..., half:]
    return jnp.concatenate([x1 * c - x2 * s, x2 * c + x1 * s], axis=-1).astype(x.dtype)


def gather_rows(t, idx):
    return jax.vmap(lambda tb, ib: tb[ib])(t, idx)


def token_mixers(u, w_in, g_qa, g_ka, g_qb, g_kb, lam_q1, lam_k1, lam_q2, lam_k2,
                 g_subln, w_branch_a, w_branch_b, w_out, lam_init, cos, sin):
    bsz, length, _ = u.shape
    k_top = min(TOPK_MAX, length // 4)
    proj = u @ w_in
    qa, ka, va, qi, ki, wi, qb, kb, vb, gate_a, gate_b = jnp.split(proj, _split_points(), axis=-1)

    qa = apply_rope(rms_norm(qa.reshape(bsz, length, H_A, D_A), g_qa), cos, sin)
    ka = apply_rope(rms_norm(ka, g_ka), cos, sin)
    qi = apply_rope(qi.reshape(bsz, length, H_I, D_I), cos, sin)
    ki = apply_rope(ki, cos, sin)
    wi = wi * (H_I ** -0.5 * D_I ** -0.5)

    qb = apply_rope(rms_norm(qb.reshape(bsz, length, H_B, 2, D_B), g_qb), cos, sin)
    kb = apply_rope(rms_norm(kb.reshape(bsz, length, H_B, 2, D_B), g_kb), cos, sin)
    q1, q2 = qb[..., 0, :], qb[..., 1, :]
    k1, k2 = kb[..., 0, :], kb[..., 1, :]
    vb = vb.reshape(bsz, length, H_B, 2 * D_B)
    lam = (jnp.exp(jnp.sum(lam_q1.astype(jnp.float32) * lam_k1.astype(jnp.float32)))
           - jnp.exp(jnp.sum(lam_q2.astype(jnp.float32) * lam_k2.astype(jnp.float32))) + lam_init)

    scale_a = D_A ** -0.5
    scale_b = D_B ** -0.5
    s_pos = jnp.arange(length)

    def qslice(t, start):
        return lax.dynamic_slice_in_dim(t, start, Q_BLOCK, axis=1)

    def block(start):
        t_pos = start + jnp.arange(Q_BLOCK)
        causal = s_pos[None, :] <= t_pos[:, None]
        rel = jax.nn.relu(jnp.einsum('bthd,bsd->bths', qslice(qi, start), ki).astype(jnp.float32))
        iscore = jnp.einsum('bths,bth->bts', rel, qslice(wi, start).astype(jnp.float32))
        iscore = jnp.where(causal[None], iscore, -jnp.inf)
        _, sel = lax.top_k(iscore, k_top)
        k_sel = gather_rows(ka, sel)
        v_sel = gather_rows(va, sel)
        logit = jnp.einsum('bthd,btkd->bthk', qslice(qa, start), k_sel).astype(jnp.float32) * scale_a
        valid = (sel <= t_pos[None, :, None])[:, :, None, :]
        attn_a = jax.nn.softmax(jnp.where(valid, logit, -jnp.inf), axis=-1)
        o_a = jnp.einsum('bthk,btkd->bthd', attn_a.astype(va.dtype), v_sel)
        s1 = jnp.einsum('bthd,bshd->bhts', qslice(q1, start), k1).astype(jnp.float32) * scale_b
        s2 = jnp.einsum('bthd,bshd->bhts', qslice(q2, start), k2).astype(jnp.float32) * scale_b
        a1 = jax.nn.softmax(jnp.where(causal, s1, -jnp.inf), axis=-1)
        a2 = jax.nn.softmax(jnp.where(causal, s2, -jnp.inf), axis=-1)
        o_b = jnp.einsum('bhts,bshe->bthe', (a1 - lam * a2).astype(vb.dtype), vb)
        return o_a, o_b

    starts = jnp.arange(0, length, Q_BLOCK, dtype=jnp.int32)
    o_a, o_b = lax.map(block, starts)
    o_a = jnp.moveaxis(o_a, 0, 1).reshape(bsz, length, W_A)
    o_b = jnp.moveaxis(o_b, 0, 1).reshape(bsz, length, H_B, 2 * D_B)
    o_b = (rms_norm(o_b, g_subln) * (1.0 - lam_init)).reshape(bsz, length, W_B)

    mix = jax.nn.sigmoid(gate_a) * (o_a @ w_branch_a) + jax.nn.sigmoid(gate_b) * (o_b @ w_branch_b)
    return mix @ w_out


def conv_ffn(u, w_up, conv_w, conv_b, w_down):
    up = u @ w_up
    pad = jnp.pad(up, ((0, 0), (CONV_W - 1, 0), (0, 0)))
    length = up.shape[1]
    conv = conv_b + sum(pad[:, j:j + length] * conv_w[j] for j in range(CONV_W))
    gate, val = jnp.split(conv, 2, axis=-1)
    return (jax.nn.gelu(gate, approximate=True) * val) @ w_down


def setup_inputs(seed: int = 0) -> dict:
    key = jax.random.key(seed)
    ks = jax.random.split(key, 24)
    f32 = jnp.float32

    def nrm(k, shape, scale):
        return jax.random.normal(k, shape, f32) * scale

    def gain(k, shape):
        return 1.0 + 0.05 * jax.random.normal(k, shape, f32)

    return {
        'x': nrm(ks[0], (BATCH, SEQ, D_MODEL), 1.0),
        'p': nrm(ks[1], (DEPTH, BATCH, SEQ, PLE_DIM), 1.0),
        'g_mix_norm': gain(ks[2], (DEPTH, D_MODEL)),
        'w_in': nrm(ks[3], (DEPTH, D_MODEL, N_IN), D_MODEL ** -0.5),
        'g_qa': gain(ks[4], (DEPTH, D_A)),
        'g_ka': gain(ks[5], (DEPTH, D_A)),
        'g_qb': gain(ks[6], (DEPTH, D_B)),
        'g_kb': gain(ks[7], (DEPTH, D_B)),
        'lam_q1': nrm(ks[8], (DEPTH, D_B), 0.1),
        'lam_k1': nrm(ks[9], (DEPTH, D_B), 0.1),
        'lam_q2': nrm(ks[10], (DEPTH, D_B), 0.1),
        'lam_k2': nrm(ks[11], (DEPTH, D_B), 0.1),
        'g_subln': gain(ks[12], (DEPTH, 2 * D_B)),
        'w_branch_a': nrm(ks[13], (DEPTH, W_A, D_MODEL), W_A ** -0.5),
        'w_branch_b': nrm(ks[14], (DEPTH, W_B, D_MODEL), W_B ** -0.5),
        'w_out': nrm(ks[15], (DEPTH, D_MODEL, D_MODEL), D_MODEL ** -0.5),
        'g_ffn_norm': gain(ks[16], (DEPTH, D_MODEL)),
        'w_up': nrm(ks[17], (DEPTH, D_MODEL, 2 * D_FF), D_MODEL ** -0.5),
        'conv_w': nrm(ks[18], (DEPTH, CONV_W, 2 * D_FF), CONV_W ** -0.5),
        'conv_b': nrm(ks[19], (DEPTH, 2 * D_FF), 0.02),
        'w_down': nrm(ks[20], (DEPTH, D_FF, D_MODEL), D_FF ** -0.5),
        'g_ple_norm': gain(ks[21], (DEPTH, D_MODEL)),
        'w_ple_gate': nrm(ks[22], (DEPTH, D_MODEL, D_MODEL), D_MODEL ** -0.5),
        'w_ple_proj': nrm(ks[23], (DEPTH, PLE_DIM, D_MODEL), PLE_DIM ** -0.5),
    }


def reference(x, p, g_mix_norm, w_in, g_qa, g_ka, g_qb, g_kb, lam_q1, lam_k1, lam_q2, lam_k2,
              g_subln, w_branch_a, w_branch_b, w_out, g_ffn_norm, w_up, conv_w, conv_b, w_down,
              g_ple_norm, w_ple_gate, w_ple_proj):
    length = x.shape[1]
    cos, sin = rope_tables(length, D_A)
    h = x
    for i in range(DEPTH):
        lam_init = 0.8 - 0.6 * math.exp(-0.3 * i)
        u = rms_norm(h, g_mix_norm[i])
        h = h + token_mixers(u, w_in[i], g_qa[i], g_ka[i], g_qb[i], g_kb[i], lam_q1[i], lam_k1[i],
                             lam_q2[i], lam_k2[i], g_subln[i], w_branch_a[i], w_branch_b[i], w_out[i],
                             lam_init, cos, sin)
        h = h + conv_ffn(rms_norm(h, g_ffn_norm[i]), w_up[i], conv_w[i], conv_b[i], w_down[i])
        ple_gate = jax.nn.sigmoid(rms_norm(h, g_ple_norm[i]) @ w_ple_gate[i])
        h = h + ple_gate * (p[i].astype(h.dtype) @ w_ple_proj[i])
    return h
```

```python
import math
import numpy as np
from contextlib import ExitStack
import concourse.bass as bass
import concourse.mybir as mybir
from concourse.bass_utils import run_bass_kernel_spmd

F32 = mybir.dt.float32
BF16 = mybir.dt.bfloat16
AF = mybir.ActivationFunctionType
ALU = mybir.AluOpType
AX = mybir.AxisListType

D = 1024
NIN = 4548
DFF = 2816
NFC = 22
PLE = 256
EPS = 1e-6
KBIS = 20
NEG = -1.0e30
COMPUTE = ('pe', 'act', 'dve', 'pool')
ENGS = ('sp', 'pool', 'act', 'dve', 'pe')


class Op(object):
    __slots__ = ('eng', 'fn', 'deps', 'dma', 'sem', 'semval', 'ms', 'waited')


class Prog(object):
    def __init__(self, nc, es, n_sp=16, n_pq=8):
        self.nc = nc
        self.esem = {e: es.enter_context(nc.semaphore('sem_' + e)) for e in COMPUTE}
        self.dsem = {'sp': [es.enter_context(nc.semaphore('dsp%d' % i)) for i in range(n_sp)],
                     'pool': [es.enter_context(nc.semaphore('dpq%d' % i)) for i in range(n_pq)]}
        self.duse = {q: [0] * len(v) for q, v in self.dsem.items()}
        self.dlast = {q: [None] * len(v) for q, v in self.dsem.items()}
        self.dnext = {q: 0 for q in self.dsem}
        self.mscount = {e: 0 for e in COMPUTE}
        self.waitedv = {e: {} for e in ENGS}
        self.state = {}
        self.ops = {e: [] for e in ENGS}
        self.last = {e: None for e in ENGS}
        self.cnt = 0

    def sb(self, ctx, shape, dt):
        self.cnt += 1
        return ctx.enter_context(self.nc.sbuf_tensor('sb%d' % self.cnt, list(shape), dt))

    def ps(self, ctx, shape, dt):
        self.cnt += 1
        return ctx.enter_context(self.nc.psum_tensor('ps%d' % self.cnt, list(shape), dt))

    def op(self, eng, fn, r=(), w=(), dma=False):
        o = Op()
        o.eng = eng; o.fn = fn; o.dma = dma; o.waited = False; o.ms = 0; o.sem = None; o.semval = 0
        deps = {}
        for k in r:
            st = self.state.get(k)
            if st is not None and st[0] is not None:
                deps[st[0]] = 'raw'
        for k in w:
            st = self.state.get(k)
            if st is not None:
                if st[0] is not None:
                    deps.setdefault(st[0], 'waw')
                for ro in st[1].values():
                    deps.setdefault(ro, 'war')
                for ro in st[2]:
                    deps.setdefault(ro, 'war')
        if dma:
            n = len(self.dsem[eng])
            slot = self.dnext[eng]
            self.dnext[eng] = (slot + 1) % n
            prev = self.dlast[eng][slot]
            if prev is not None:
                deps.setdefault(prev, 'slot')
            self.duse[eng][slot] += 1
            o.sem = self.dsem[eng][slot]
            o.semval = 16 * self.duse[eng][slot]
            self.dlast[eng][slot] = o
        fd = []
        for d, kind in deps.items():
            if d is o:
                continue
            if (not d.dma) and (not dma) and d.eng == eng:
                if eng == 'pe':
                    continue
            fd.append(d)
            d.waited = True
        o.deps = fd
        for k in r:
            st = self.state.setdefault(k, [None, {}, []])
            if dma:
                st[2].append(o)
            else:
                st[1][eng] = o
        for k in w:
            self.state[k] = [o, {}, []]
        self.ops[eng].append(o)
        if not dma:
            self.last[eng] = o
        return o

    def barrier(self):
        targets = [self.last[e] for e in COMPUTE if self.last[e] is not None]
        for q in self.dlast:
            targets += [d for d in self.dlast[q] if d is not None]
        for e in ENGS:
            o = Op()
            o.eng = e; o.fn = None; o.dma = False; o.waited = False; o.ms = 0; o.sem = None; o.semval = 0
            o.deps = [t for t in targets if t.dma or t.eng != e]
            for t in o.deps:
                t.waited = True
            self.ops[e].append(o)
        self.state = {}

    def emit(self):
        for e in COMPUTE:
            for o in self.ops[e]:
                if o.fn is not None and (not o.dma) and o.waited:
                    self.mscount[e] += 1
                    o.ms = self.mscount[e]
        nc = self.nc
        with nc.Block() as block:
            decos = {'sp': block.sync, 'pool': block.gpsimd, 'act': block.scalar, 'dve': block.vector,
                     'pe': block.tensor}
            for e in ENGS:
                ops = self.ops[e]
                if not ops:
                    continue

                def body(eo, ops=ops, e=e):
                    wt = self.waitedv[e]
                    for o in ops:
                        for d in o.deps:
                            if d.dma:
                                sem, val = d.sem, d.semval
                            else:
                                sem, val = self.esem[d.eng], d.ms
                            key = id(sem)
                            if wt.get(key, 0) < val:
                                eo.wait_ge(sem, val)
                                wt[key] = val
                        if o.fn is None:
                            continue
                        ins = o.fn(eo)
                        if o.dma:
                            ins.then_inc(o.sem, 16)
                        elif o.waited:
                            ins.then_inc(self.esem[e], 1)
                decos[e](body)
        self.ops = {e: [] for e in ENGS}

    def dma(self, q, out, in_, r=(), w=()):
        return self.op(q, lambda e: e.dma_start(out=out, in_=in_), r, w, dma=True)

    def tt(self, eng, out, in0, in1, op, r=(), w=()):
        return self.op(eng, lambda e: e.tensor_tensor(out=out, in0=in0, in1=in1, op=op), r, w)

    def ts(self, eng, out, in0, s1, op0, s2=None, op1=None, accum=None, r=(), w=()):
        if op1 is None:
            return self.op(eng, lambda e: e.tensor_scalar(out=out, in0=in0, scalar1=s1, scalar2=None, op0=op0), r, w)
        if accum is None:
            return self.op(eng, lambda e: e.tensor_scalar(out=out, in0=in0, scalar1=s1, scalar2=s2, op0=op0,
                                                           op1=op1), r, w)
        return self.op(eng, lambda e: e.tensor_scalar(out=out, in0=in0, scalar1=s1, scalar2=s2, op0=op0,
                                                       op1=op1, accum_out=accum), r, w)

    def stt(self, eng, out, in0, scalar, in1, op0, op1, r=(), w=()):
        return self.op(eng, lambda e: e.scalar_tensor_tensor(out=out, in0=in0, scalar=scalar, in1=in1,
                                                              op0=op0, op1=op1), r, w)

    def act(self, out, in_, func, bias=None, scale=None, accum=None, r=(), w=()):
        kw = {}
        if bias is not None:
            kw['bias'] = bias
        if scale is not None:
            kw['scale'] = scale
        if accum is not None:
            kw['accum_out'] = accum
        return self.op('act', lambda e: e.activation(out=out, in_=in_, func=func, **kw), r, w)

    def cp(self, eng, out, in_, r=(), w=()):
        if eng == 'act':
            return self.op('act', lambda e: e.copy(out=out, in_=in_), r, w)
        return self.op(eng, lambda e: e.tensor_copy(out=out, in_=in_), r, w)

    def mm(self, out, lhsT, rhs, start, stop, r=(), w=(), acc0=False):
        if acc0:
            return self.op('pe', lambda e: e.matmul(out, lhsT=lhsT, rhs=rhs, start=False, stop=False,
                                                    skip_group_check=True), r, w)
        return self.op('pe', lambda e: e.matmul(out, lhsT=lhsT, rhs=rhs, start=start, stop=stop), r, w)

    def tr(self, out, in_, ident, r=(), w=()):
        return self.op('pe', lambda e: e.transpose(out=out, in_=in_, identity=ident), r, w)

    def red(self, out, in_, op, r=(), w=()):
        return self.op('dve', lambda e: e.tensor_reduce(out=out, in_=in_, axis=AX.X, op=op), r, w)

    def recip(self, out, in_, r=(), w=()):
        return self.op('dve', lambda e: e.reciprocal(out=out, in_=in_), r, w)

    def rsqrt(self, out, in_, addc, r=(), w=()):
        self.op('act', lambda e: e.activation(out=out, in_=in_, func=AF.Sqrt, bias=addc, scale=1.0), r, w)
        return self.op('dve', lambda e: e.reciprocal(out=out, in_=out), w, w)

    def memset(self, eng, ap, val, r=(), w=()):
        return self.op(eng, lambda e: e.memset(ap, val), r, w)


class Rot(object):
    def __init__(self, P, ctx, n, shape, dt, name, psum=False):
        self.bufs = [(P.ps if psum else P.sb)(ctx, shape, dt) for _ in range(n)]
        self.name = name
        self.i = -1

    def next(self):
        self.i = (self.i + 1) % len(self.bufs)
        return self.bufs[self.i], (self.name, self.i)


def bc_mid(ap, n):
    return ap.unsqueeze(1).to_broadcast([ap.shape[0], n, ap.shape[1]])


def bc_last(ap, n):
    return ap.unsqueeze(2).to_broadcast([ap.shape[0], ap.shape[1], n])


def build(L, NSEQ, DEPTH, debug=False):
    NT = L // 128
    T = NSEQ * L
    NTT = T // 128
    KTOP = min(256, L // 4)
    KT = KTOP // 128
    NG = T // 512
    nc = bass.Bass("TRN2", target_bir_lowering=False)

    def din(name, shape, dt=F32):
        return nc.dram_tensor(name, list(shape), dt, kind="ExternalInput").ap()

    x = din("x", [T, D])
    p_in = din("p", [DEPTH, T, PLE])
    g_mix = din("g_mix_norm", [DEPTH, D])
    w_in = din("w_in", [DEPTH, D, NIN])
    g_q = {f: din(f, [DEPTH, 64]) for f in ("g_qa", "g_ka", "g_qb", "g_kb")}
    lamv = {f: din(f, [DEPTH, 64]) for f in ("lam_q1", "lam_k1", "lam_q2", "lam_k2")}
    g_sub = din("g_subln", [DEPTH, 128])
    w_bra = din("w_branch_a", [DEPTH, 512, D])
    w_brb = din("w_branch_b", [DEPTH, 512, D])
    w_out = din("w_out", [DEPTH, D, D])
    g_ffn = din("g_ffn_norm", [DEPTH, D])
    w_up = din("w_up", [DEPTH, D, 2 * DFF])
    conv_w = din("conv_w", [DEPTH, 3, 2 * DFF])
    conv_b = din("conv_b", [DEPTH, 2 * DFF])
    w_down = din("w_down", [DEPTH, DFF, D])
    g_ple = din("g_ple_norm", [DEPTH, D])
    w_pg = din("w_ple_gate", [DEPTH, D, D])
    w_pp = din("w_ple_proj", [DEPTH, PLE, D])
    cs_in = din("cs_tab", [128, NT, 64])
    y = nc.dram_tensor("y", [T, D], F32, kind="ExternalOutput").ap()

    skind = "ExternalOutput" if debug else "Internal"

    def dscr(name, shape, dt):
        return nc.dram_tensor(name, list(shape), dt, kind=skind).ap()

    h_s = dscr("h_s", [T, D], F32)
    fm_s = dscr("fm_s", [NSEQ, 128, 16, L], BF16)
    va_s = dscr("va_s", [NSEQ, NT, 128, 64], BF16)
    vb_s = dscr("vb_s", [NSEQ, NT, 128, 512], BF16)
    wi_s = dscr("wi_s", [NSEQ, NT, 128, 4], F32)
    sg_s = dscr("sg_s", [T, 2048], BF16)
    pr_s = dscr("pr_s", [128, NFC, T], BF16)

    with ExitStack() as es:
        P = Prog(nc, es)
        ident = P.sb(es, [128, 128], BF16)
        identf = P.sb(es, [128, 128], F32)
        caus01T = P.sb(es, [128, 128], BF16)
        negmask = P.sb(es, [128, 128], F32)
        cs_sb = P.sb(es, [128, NT, 64], F32)
        pow2 = P.sb(es, [128, KBIS + 1], F32)
        lam_sb = P.sb(es, [128, DEPTH], F32)
        nlam_sb = P.sb(es, [128, DEPTH], F32)
        ones_bf = P.sb(es, [128, 128], BF16)
        zer_f = P.sb(es, [128, 128], F32)

        with ExitStack() as ph:
            P.memset('pool', ident[:], 0.0, w=['ident'])
            P.op('pool', lambda e: e.affine_select(out=ident[:], in_=ident[:], pattern=[[-1, 128]],
                                                   compare_op=ALU.not_equal, fill=1.0, base=0,
                                                   channel_multiplier=1), r=['ident'], w=['ident'])
            P.memset('pool', identf[:], 0.0, w=['identf'])
            P.op('pool', lambda e: e.affine_select(out=identf[:], in_=identf[:], pattern=[[-1, 128]],
                                                   compare_op=ALU.not_equal, fill=1.0, base=0,
                                                   channel_multiplier=1), r=['identf'], w=['identf'])
            P.memset('pool', ones_bf[:], 1.0, w=['ones'])
            P.memset('pool', zer_f[:], 0.0, w=['zer'])
            P.op('pool', lambda e: e.affine_select(out=caus01T[:], in_=ones_bf[:], pattern=[[1, 128]],
                                                   compare_op=ALU.is_ge, fill=0.0, base=0,
                                                   channel_multiplier=-1), r=['ones'], w=['caus'])
            P.op('pool', lambda e: e.affine_select(out=negmask[:], in_=zer_f[:], pattern=[[-1, 128]],
                                                   compare_op=ALU.is_ge, fill=NEG, base=0,
                                                   channel_multiplier=1), r=['zer'], w=['negm'])
            P.dma('sp', cs_sb[:], cs_in[:, :, :], w=['cs'])
            for k in range(KBIS + 1):
                P.memset('dve', pow2[:, k:k + 1], float(2.0 ** (-k)), w=['pow2'])
            lt = {f: P.sb(ph, [128, 64], F32) for f in lamv}
            ltmp = P.sb(ph, [128, 64], F32)
            ld = P.sb(ph, [128, 4], F32)
            for l in range(DEPTH):
                lam_init = 0.8 - 0.6 * math.exp(-0.3 * l)
                for f in lamv:
                    P.dma('sp', lt[f][:], lamv[f][l:l + 1, :].to_broadcast([128, 64]), w=[('lt', f)])
                P.tt('dve', ltmp[:], lt['lam_q1'][:], lt['lam_k1'][:], ALU.mult, r=[('lt', 'lam_q1'), ('lt', 'lam_k1')], w=['ltmp'])
                P.red(ld[:, 0:1], ltmp[:], ALU.add, r=['ltmp'], w=['ld0'])
                P.tt('dve', ltmp[:], lt['lam_q2'][:], lt['lam_k2'][:], ALU.mult, r=[('lt', 'lam_q2'), ('lt', 'lam_k2')], w=['ltmp'])
                P.red(ld[:, 1:2], ltmp[:], ALU.add, r=['ltmp'], w=['ld1'])
                P.act(ld[:, 2:4], ld[:, 0:2], AF.Exp, r=['ld0', 'ld1'], w=['ld23'])
                P.stt('dve', lam_sb[:, l:l + 1], ld[:, 2:3], lam_init, ld[:, 3:4], ALU.add, ALU.subtract,
                      r=['ld23'], w=[('lam', l)])
                P.ts('dve', nlam_sb[:, l:l + 1], lam_sb[:, l:l + 1], -1.0, ALU.mult, r=[('lam', l)], w=[('nlam', l)])
            P.barrier()
            P.emit()

        for l in range(DEPTH):
            lam_init = 0.8 - 0.6 * math.exp(-0.3 * l)
            hsrc = x if l == 0 else h_s
            with ExitStack() as ph:
                w_sb = P.sb(ph, [128, 8, NIN], BF16)
                for k in range(8):
                    P.dma('pool', w_sb[:, k, :], w_in[l, k * 128:(k + 1) * 128, :], w=[('w', k)])
                gs_bc = P.sb(ph, [128, D], F32)
                P.dma('sp', gs_bc[:], g_mix[l:l + 1, :].to_broadcast([128, D]), w=['gs0'])
                P.ts('dve', gs_bc[:], gs_bc[:], float(math.sqrt(D)), ALU.mult, r=['gs0'], w=['gs'])
                tabs = {}
                for f in g_q:
                    g8 = P.sb(ph, [128, 64], F32)
                    P.dma('sp', g8[:], g_q[f][l:l + 1, :].to_broadcast([128, 64]), w=[('g8', f)])
                    P.ts('dve', g8[:], g8[:], 8.0, ALU.mult, r=[('g8', f)], w=[('g8s', f)])
                    tb = P.sb(ph, [128, NT, 4, 32], F32)
                    for j, (co, go) in enumerate(((0, 0), (32, 32), (0, 32), (32, 0))):
                        P.tt('dve', tb[:, :, j, :], cs_sb[:, :, co:co + 32], bc_mid(g8[:, go:go + 32], NT), ALU.mult,
                             r=['cs', ('g8s', f)], w=[('tab', f, j)])
                    tabs[f] = tb
                hR = Rot(P, ph, 2, [128, D], F32, 'h')
                uR = Rot(P, ph, 2, [128, D], BF16, 'u')
                uTR = Rot(P, ph, 2, [128, 8, 128], BF16, 'uT')
                ssR = Rot(P, ph, 2, [128, 2], F32, 'ss')
                junkA = P.sb(ph, [128, D], BF16)
                sqR = Rot(P, ph, 2, [128, 512], F32, 'sq')
                xnR = Rot(P, ph, 2, [128, 512], F32, 'xn')
                smR = Rot(P, ph, 2, [128, 16], F32, 'sm')
                tR = [Rot(P, ph, 2, [128, 8, 32], F32, 't%d' % i) for i in range(4)]
                rqR = Rot(P, ph, 2, [128, 16, 128], BF16, 'rq')
                stR = Rot(P, ph, 2, [128, 16, 128], BF16, 'stage')
                vaR = Rot(P, ph, 2, [128, 64], BF16, 'va')
                vbR = Rot(P, ph, 2, [128, 512], BF16, 'vb')
                wiR = Rot(P, ph, 2, [128, 4], F32, 'wi')
                sgR = Rot(P, ph, 2, [128, 2048], BF16, 'sg')
                ptrR = Rot(P, ph, 2, [128, 8, 128], BF16, 'ptr', psum=True)
                ppR = Rot(P, ph, 4, [128, 512], F32, 'pp', psum=True)

                def normrope(src, srckey, H, fam, pos, rq, rqkey, blk0, half0, normed):
                    sv = src.rearrange("p (h d) -> p h d", d=64)
                    if normed:
                        sq, sqk = sqR.next()
                        P.act(sq[:, 0:H * 64], src, AF.Square, r=[srckey], w=[sqk])
                        sm, smk = smR.next()
                        P.red(sm[:, 0:H], sq[:, 0:H * 64].rearrange("p (h d) -> p h d", d=64), ALU.add, r=[sqk], w=[smk])
                        P.rsqrt(sm[:, 8:8 + H], sm[:, 0:H], 64.0 * EPS, r=[smk], w=[smk])
                        xn, xnk = xnR.next()
                        xv = xn[:, 0:H * 64].rearrange("p (h d) -> p h d", d=64)
                        P.tt('dve', xv, sv, bc_last(sm[:, 8:8 + H], 64), ALU.mult, r=[srckey, smk], w=[xnk])
                        xk = xnk
                        tb = tabs[fam]
                        C1, S2, C2, S1 = (tb[:, pos, j, :] for j in range(4))
                        tr_ = [('tab', fam, j) for j in range(4)]
                    else:
                        xv, xk = sv, srckey
                        C1 = C2 = cs_sb[:, pos, 0:32]
                        S1 = S2 = cs_sb[:, pos, 32:64]
                        tr_ = ['cs'] * 4
                    x1 = xv[:, :, 0:32]
                    x2 = xv[:, :, 32:64]
                    ov = rq[:].rearrange("p b c -> p (b c)")[:, blk0 * 128 + half0: blk0 * 128 + half0 + H * 64]
                    ov = ov.rearrange("p (h d) -> p h d", d=64)
                    tb_ = [r_.next() for r_ in tR]
                    (t1, k1), (t2, k2), (t3, k3), (t4, k4) = tb_
                    P.tt('dve', t1[:, 0:H, :], x1, bc_mid(C1, H), ALU.mult, r=[xk, tr_[0]], w=[k1])
                    P.tt('dve', t2[:, 0:H, :], x2, bc_mid(S2, H), ALU.mult, r=[xk, tr_[1]], w=[k2])
                    P.tt('dve', ov[:, :, 0:32], t1[:, 0:H, :], t2[:, 0:H, :], ALU.subtract, r=[k1, k2], w=[rqkey])
                    P.tt('dve', t3[:, 0:H, :], x2, bc_mid(C2, H), ALU.mult, r=[xk, tr_[2]], w=[k3])
                    P.tt('dve', t4[:, 0:H, :], x1, bc_mid(S1, H), ALU.mult, r=[xk, tr_[3]], w=[k4])
                    P.tt('dve', ov[:, :, 32:64], t3[:, 0:H, :], t4[:, 0:H, :], ALU.add, r=[k3, k4], w=[rqkey])

                groups = [(0, 512), (512, 964), (964, 1476), (1476, 1988), (1988, 2500),
                          (2500, 3012), (3012, 3524), (3524, 4036), (4036, 4548)]
                for tt in range(NTT):
                    sq_i, pos = tt // NT, tt % NT
                    h_t, hk = hR.next()
                    P.dma('sp', h_t[:], hsrc[tt * 128:(tt + 1) * 128, :], w=[hk])
                    ss, ssk = ssR.next()
                    P.act(junkA[:], h_t[:], AF.Square, accum=ss[:, 0:1], r=[hk], w=[ssk, 'junkA'])
                    P.rsqrt(ss[:, 1:2], ss[:, 0:1], D * EPS, r=[ssk], w=[ssk])
                    u, uk = uR.next()
                    P.stt('dve', u[:], h_t[:], ss[:, 1:2], gs_bc[:], ALU.mult, ALU.mult, r=[hk, ssk, 'gs'], w=[uk])
                    ptr, pk = ptrR.next()
                    for k in range(8):
                        P.tr(ptr[:, k, :], u[:, k * 128:(k + 1) * 128], ident[:], r=[uk, 'ident'], w=[pk])
                    uT, uTk = uTR.next()
                    P.cp('act', uT[:], ptr[:], r=[pk], w=[uTk])
                    rq, rqk = rqR.next()
                    sg, sgk = sgR.next()
                    for gi, (c0, c1) in enumerate(groups):
                        pp, ppk = ppR.next()
                        for k in range(8):
                            P.mm(pp[:, 0:c1 - c0], uT[:, k, :], w_sb[:, k, c0:c1], k == 0, k == 7,
                                 r=[uTk, ('w', k)], w=[ppk])
                        if gi == 0:
                            normrope(pp[:, 0:512], ppk, 8, 'g_qa', pos, rq, rqk, 0, 0, True)
                        elif gi == 1:
                            normrope(pp[:, 0:64], ppk, 1, 'g_ka', pos, rq, rqk, 7, 0, True)
                            P.cp('dve', rq[:, 7, 64:128], rq[:, 7, 0:64], r=[rqk], w=[rqk])
                            va, vak = vaR.next()
                            P.cp('act', va[:], pp[:, 64:128], r=[ppk], w=[vak])
                            P.dma('sp', va_s[sq_i, pos, :, :], va[:], r=[vak])
                            normrope(pp[:, 128:448], ppk, 5, None, pos, rq, rqk, 4, 0, False)
                            P.cp('dve', rq[:, 6, 64:128], rq[:, 6, 0:64], r=[rqk], w=[rqk])
                            wi, wik = wiR.next()
                            P.ts('dve', wi[:], pp[:, 448:452], 1.0 / 16.0, ALU.mult, r=[ppk], w=[wik])
                            P.dma('sp', wi_s[sq_i, pos, :, :], wi[:], r=[wik])
                        elif gi == 2:
                            normrope(pp[:, 0:512], ppk, 8, 'g_qb', pos, rq, rqk, 8, 0, True)
                        elif gi == 3:
                            normrope(pp[:, 0:512], ppk, 8, 'g_kb', pos, rq, rqk, 12, 0, True)
                        elif gi == 4:
                            vb, vbk = vbR.next()
                            P.cp('act', vb[:], pp[:, 0:512], r=[ppk], w=[vbk])
                            P.dma('sp', vb_s[sq_i, pos, :, :], vb[:], r=[vbk])
                        else:
                            q = gi - 5
                            P.act(sg[:, q * 512:(q + 1) * 512], pp[:, 0:512], AF.Sigmoid, r=[ppk], w=[sgk])
                    P.dma('sp', sg_s[tt * 128:(tt + 1) * 128, :], sg[:], r=[sgk])
                    stg, stk = stR.next()
                    for hb in range(2):
                        ptr, pk = ptrR.next()
                        for b in range(8):
                            P.tr(ptr[:, b, :], rq[:, hb * 8 + b, :], ident[:], r=[rqk, 'ident'], w=[pk])
                        P.cp('act', stg[:, hb * 8:(hb + 1) * 8, :], ptr[:], r=[pk], w=[stk])
                    P.dma('sp', fm_s[sq_i, :, :, pos * 128:(pos + 1) * 128], stg[:], r=[stk])
                P.barrier()
                P.emit()
            if debug == 1:
                break
            with ExitStack() as ph:
                wA = P.sb(ph, [128, 4, D], BF16)
                wB = P.sb(ph, [128, 4, D], BF16)
                wO = P.sb(ph, [128, 8, D], BF16)
                for k in range(4):
                    P.dma('pool', wA[:, k, :], w_bra[l, k * 128:(k + 1) * 128, :], w=[('wA', k)])
                    P.dma('pool', wB[:, k, :], w_brb[l, k * 128:(k + 1) * 128, :], w=[('wB', k)])
                for k in range(8):
                    P.dma('pool', wO[:, k, :], w_out[l, k * 128:(k + 1) * 128, :], w=[('wO', k)])
                gsub = P.sb(ph, [128, 128], F32)
                P.dma('sp', gsub[:], g_sub[l:l + 1, :].to_broadcast([128, 128]), w=['gsub0'])
                P.ts('dve', gsub[:], gsub[:], float((1.0 - lam_init) * math.sqrt(128.0)), ALU.mult, r=['gsub0'], w=['gsub'])
                FM = P.sb(ph, [128, 16, L], BF16)
                vaA = P.sb(ph, [128, NT, 65], BF16)
                vbA = P.sb(ph, [128, NT, 4, 129], BF16)
                wiS = P.sb(ph, [128, NT, 4], F32)
                isc = P.sb(ph, [128, L], F32)
                junkD = P.sb(ph, [128, L], BF16)
                Mk = P.sb(ph, [128, L], BF16)
                MT = P.sb(ph, [128, NT, 128], BF16)
                rlR = Rot(P, ph, 2, [128, 512], F32, 'rl')
                bis = P.sb(ph, [128, 8], F32)
                stepsX = P.sb(ph, [128, KBIS + 1], F32)
                ER = Rot(P, ph, 2, [128, 2, 4, 128], BF16, 'E')
                PR = Rot(P, ph, 2, [128, 2, 4, 128], BF16, 'Pm')
                rsA = P.sb(ph, [128, 8], F32)
                rsB = P.sb(ph, [128, 16], F32)
                sB2 = P.sb(ph, [128, 8], F32)
                oa_bf = P.sb(ph, [128, 512], BF16)
                ob_f = P.sb(ph, [128, 4, 128], F32)
                ob_t = P.sb(ph, [128, 4, 128], F32)
                ob_n = P.sb(ph, [128, 4, 128], F32)
                ob_sq = P.sb(ph, [128, 512], F32)
                ob_bf = P.sb(ph, [128, 512], BF16)
                oT = P.sb(ph, [128, 8, 128], BF16)
                m1 = P.sb(ph, [128, D], F32)
                m2 = P.sb(ph, [128, D], F32)
                mix_bf = P.sb(ph, [128, D], BF16)
                mT = P.sb(ph, [128, 8, 128], BF16)
                hR = Rot(P, ph, 2, [128, D], F32, 'h')
                h2R = Rot(P, ph, 2, [128, D], F32, 'h2')
                sgR = Rot(P, ph, 2, [128, 2048], BF16, 'sg')
                ps_idx = P.ps(ph, [128, 512], F32)
                ps_tr = P.ps(ph, [128, 8, 128], BF16)
                ps_S = P.ps(ph, [128, 2, 4, 128], F32)
                ps_OA = P.ps(ph, [128, 512], F32)
                ps_OB = P.ps(ph, [128, 3, 512], F32)

                def oa_ap(h):
                    if h < 7:
                        return ps_OA[:, h * 65:h * 65 + 65]
                    return ps_OB[:, 2, 258:323]

                def ob_ap(q):
                    return ps_OB[:, q // 3, (q % 3) * 129:(q % 3) * 129 + 129]

                for s_i in range(NSEQ):
                    for b in range(16):
                        P.dma('sp', FM[:, b, :], fm_s[s_i, :, b, :], w=[('FM', b)])
                    P.memset('pool', vaA[:], 1.0, w=['vaA'])
                    P.memset('pool', vbA[:], 1.0, w=['vbA'])
                    P.dma('sp', vaA[:, :, 0:64], va_s[s_i].rearrange("n p d -> p n d"), w=['vaA'])
                    for hh in range(4):
                        P.dma('sp', vbA[:, :, hh, 0:128], vb_s[s_i, :, :, hh * 128:(hh + 1) * 128].rearrange("n p d -> p n d"),
                              w=['vbA'])
                    P.dma('sp', wiS[:], wi_s[s_i].rearrange("n p d -> p n d"), w=['wiS'])
                    for j in range(NT):
                        tt = s_i * NT + j
                        S = 128 * (j + 1)
                        tsl = slice(j * 128, (j + 1) * 128)
                        for c0 in range(0, S, 512):
                            cw = min(512, S - c0)
                            for hh in range(4):
                                r0 = 64 * (hh % 2)
                                P.mm(ps_idx[:, 0:cw], FM[r0:r0 + 64, 4 + hh // 2, tsl], FM[r0:r0 + 64, 6, c0:c0 + cw],
                                     True, True, r=[('FM', 4 + hh // 2), ('FM', 6)], w=['ps_idx'])
                                rl, rlk = rlR.next()
                                P.act(rl[:, 0:cw], ps_idx[:, 0:cw], AF.Relu, r=['ps_idx'], w=[rlk])
                                if hh == 0:
                                    P.ts('dve', isc[:, c0:c0 + cw], rl[:, 0:cw], wiS[:, j, 0:1], ALU.mult,
                                         r=[rlk, 'wiS'], w=['isc'])
                                else:
                                    P.stt('dve', isc[:, c0:c0 + cw], rl[:, 0:cw], wiS[:, j, hh:hh + 1], isc[:, c0:c0 + cw],
                                          ALU.mult, ALU.add, r=[rlk, 'wiS', 'isc'], w=['isc'])
                        if j >= KT:
                            P.red(bis[:, 0:1], isc[:, 0:S], ALU.min, r=['isc'], w=['mn'])
                        P.tt('dve', isc[:, S - 128:S], isc[:, S - 128:S], negmask[:], ALU.add, r=['isc', 'negm', 'mn'], w=['isc'])
                        if j >= KT:
                            P.red(bis[:, 1:2], isc[:, 0:S], ALU.max, r=['isc'], w=['mx'])
                            P.tt('dve', bis[:, 2:3], bis[:, 1:2], bis[:, 0:1], ALU.subtract, r=['mn', 'mx'], w=['w0'])
                            P.ts('dve', stepsX[:], pow2[:], bis[:, 2:3], ALU.mult, r=['w0', 'pow2'], w=['steps'])
                            P.tt('dve', bis[:, 3:4], bis[:, 0:1], stepsX[:, 1:2], ALU.add, r=['mn', 'steps'], w=['mid'])
                            for k in range(KBIS):
                                P.ts('dve', junkD[:, 0:S], isc[:, 0:S], bis[:, 3:4], ALU.is_ge, 0.0, ALU.add,
                                     accum=bis[:, 4:5], r=['isc', 'mid'], w=['cnt', 'junkD'])
                                P.ts('dve', bis[:, 5:6], bis[:, 4:5], float(KTOP) - 0.5, ALU.is_ge, -0.5, ALU.add,
                                     r=['cnt'], w=['sgn'])
                                P.stt('dve', bis[:, 3:4], bis[:, 5:6], stepsX[:, k + 1:k + 2], bis[:, 3:4], ALU.mult, ALU.add,
                                      r=['sgn', 'steps', 'mid'], w=['mid'])
                            P.stt('dve', bis[:, 6:7], stepsX[:, KBIS:KBIS + 1], -0.5, bis[:, 3:4], ALU.mult, ALU.add,
                                  r=['steps', 'mid'], w=['thr'])
                            P.ts('dve', Mk[:, 0:S], isc[:, 0:S], bis[:, 6:7], ALU.is_ge, r=['isc', 'thr'], w=['Mk'])
                        else:
                            P.ts('dve', Mk[:, 0:S], isc[:, 0:S], -1.0e29, ALU.is_ge, r=['isc'], w=['Mk'])
                        for i0 in range(0, j + 1, 8):
                            n8 = min(8, j + 1 - i0)
                            for ii in range(n8):
                                i = i0 + ii
                                P.tr(ps_tr[:, ii, :], Mk[:, i * 128:(i + 1) * 128], ident[:], r=['Mk', 'ident'], w=['ps_tr'])
                            P.cp('act', MT[:, i0:i0 + n8, :], ps_tr[:, 0:n8, :], r=['ps_tr'], w=['MT'])
                        P.memset('dve', ps_OA[:], 0.0, w=['OA'])
                        P.memset('dve', ps_OB[:], 0.0, w=['OB', 'OB2'])
                        for i in range(j + 1):
                            ssl = slice(i * 128, (i + 1) * 128)
                            for e_ in range(2):
                                r0 = 64 * e_
                                P.mm(ps_S[:, e_, :, :], FM[r0:r0 + 64, 7, ssl], FM[r0:r0 + 64, 0:4, tsl], True, True,
                                     r=[('FM', 7), ('FM', 0), ('FM', 1), ('FM', 2), ('FM', 3)], w=['ps_S'])
                            E, Ek = ER.next()
                            P.act(E[:], ps_S[:], AF.Exp, scale=0.125, r=['ps_S'], w=[Ek])
                            Pm, Pk = PR.next()
                            P.tt('dve', Pm[:].rearrange("p e a t -> p (e a) t"), E[:].rearrange("p e a t -> p (e a) t"),
                                 bc_mid(MT[:, i, :], 8), ALU.mult, r=[Ek, 'MT'], w=[Pk])
                            for e_ in range(2):
                                for pr in range(4):
                                    hh = 2 * pr + e_
                                    P.mm(oa_ap(hh), Pm[:, e_, pr, :], vaA[:, i, :], i == 0, i == j, r=[Pk, 'vaA'],
                                         w=['OA', 'OB2'] if hh == 7 else ['OA'], acc0=True)
                        v7 = ps_OA[:, 0:455].rearrange("p (h c) -> p h c", c=65)
                        P.recip(rsA[:, 0:7], v7[:, :, 64], r=['OA'], w=['rsA'])
                        P.recip(rsA[:, 7:8], ps_OB[:, 2, 322:323], r=['OA', 'OB2'], w=['rsA'])
                        P.tt('dve', oa_bf[:, 0:448].rearrange("p (h d) -> p h d", d=64), v7[:, :, 0:64], bc_last(rsA[:, 0:7], 64),
                             ALU.mult, r=['OA', 'rsA'], w=['oa'])
                        P.ts('dve', oa_bf[:, 448:512], ps_OB[:, 2, 258:322], rsA[:, 7:8], ALU.mult, r=['OA', 'OB2', 'rsA'], w=['oa'])
                        for i in range(j + 1):
                            ssl = slice(i * 128, (i + 1) * 128)
                            for hh in range(4):
                                for m_ in range(2):
                                    r0 = 64 * m_
                                    P.mm(ps_S[:, m_, hh, :], FM[r0:r0 + 64, 12 + hh, ssl], FM[r0:r0 + 64, 8 + hh, tsl], True, True,
                                         r=[('FM', 12 + hh), ('FM', 8 + hh)], w=['ps_S'])
                            E, Ek = ER.next()
                            P.act(E[:], ps_S[:], AF.Exp, scale=0.125, r=['ps_S'], w=[Ek])
                            if i == j:
                                P.tt('dve', E[:].rearrange("p e a t -> p (e a) t"), E[:].rearrange("p e a t -> p (e a) t"),
                                     bc_mid(caus01T[:], 8), ALU.mult, r=[Ek, 'caus'], w=[Ek])
                            for hh in range(4):
                                for m_ in range(2):
                                    P.mm(ob_ap(2 * hh + m_), E[:, m_, hh, :], vbA[:, i, hh, :], i == 0, i == j,
                                         r=[Ek, 'vbA'], w=['OB', 'OB2'] if hh == 3 else ['OB'], acc0=True)
                        for b in range(3):
                            nq = 3 if b < 2 else 2
                            vq = ps_OB[:, b, 0:nq * 129].rearrange("p (q c) -> p q c", c=129)
                            P.recip(rsB[:, 3 * b:3 * b + nq], vq[:, :, 128], r=['OB', 'OB2'], w=['rsB'])
                        P.ts('dve', rsB[:, 8:16], rsB[:, 0:8], nlam_sb[:, l:l + 1], ALU.mult, r=['rsB', ('nlam', l)], w=['rsB2'])
                        for hh in range(4):
                            P.ts('dve', ob_t[:, hh, :], ob_ap(2 * hh)[:, 0:128], rsB[:, 2 * hh:2 * hh + 1], ALU.mult,
                                 r=['OB', 'OB2', 'rsB'], w=['ob_t'])
                            P.stt('dve', ob_f[:, hh, :], ob_ap(2 * hh + 1)[:, 0:128], rsB[:, 8 + 2 * hh + 1:8 + 2 * hh + 2],
                                  ob_t[:, hh, :], ALU.mult, ALU.add, r=['OB', 'OB2', 'rsB2', 'ob_t'], w=['ob_f'])
                        P.act(ob_sq[:], ob_f[:].rearrange("p h d -> p (h d)"), AF.Square, r=['ob_f'], w=['ob_sq'])
                        P.red(sB2[:, 0:4], ob_sq[:].rearrange("p (h d) -> p h d", d=128), ALU.add, r=['ob_sq'], w=['ssb'])
                        P.rsqrt(sB2[:, 4:8], sB2[:, 0:4], 128.0 * EPS, r=['ssb'], w=['rsb'])
                        P.tt('dve', ob_n[:], ob_f[:], bc_last(sB2[:, 4:8], 128), ALU.mult, r=['ob_f', 'rsb'], w=['ob_n'])
                        P.tt('dve', ob_bf[:].rearrange("p (h d) -> p h d", d=128), ob_n[:], bc_mid(gsub[:], 4), ALU.mult,
                             r=['ob_n', 'gsub'], w=['ob'])
                        for b in range(4):
                            P.tr(ps_tr[:, b, :], oa_bf[:, b * 128:(b + 1) * 128], ident[:], r=['oa', 'ident'], w=['ps_tr'])
                            P.tr(ps_tr[:, 4 + b, :], ob_bf[:, b * 128:(b + 1) * 128], ident[:], r=['ob', 'ident'], w=['ps_tr'])
                        P.cp('act', oT[:], ps_tr[:], r=['ps_tr'], w=['oT'])
                        sg, sgk = sgR.next()
                        P.dma('sp', sg[:], sg_s[tt * 128:(tt + 1) * 128, :], w=[sgk])
                        h_t, hk = hR.next()
                        P.dma('sp', h_t[:], hsrc[tt * 128:(tt + 1) * 128, :], w=[hk])
                        psv = ps_S[:].rearrange("p e a t -> p e (a t)")
                        for n2 in range(2):
                            nsl = slice(n2 * 512, (n2 + 1) * 512)
                            for k in range(4):
                                P.mm(psv[:, n2, :], oT[:, k, :], wA[:, k, nsl], k == 0, k == 3, r=['oT', ('wA', k)], w=['ps_S'])
                            P.tt('dve', m1[:, nsl], psv[:, n2, :], sg[:, nsl], ALU.mult, r=['ps_S', sgk], w=['m1'])
                        for n2 in range(2):
                            nsl = slice(n2 * 512, (n2 + 1) * 512)
                            for k in range(4):
                                P.mm(psv[:, n2, :], oT[:, 4 + k, :], wB[:, k, nsl], k == 0, k == 3, r=['oT', ('wB', k)], w=['ps_S'])
                            P.tt('dve', m2[:, nsl], psv[:, n2, :], sg[:, 1024 + n2 * 512:1024 + (n2 + 1) * 512], ALU.mult,
                                 r=['ps_S', sgk], w=['m2'])
                        P.tt('dve', mix_bf[:], m1[:], m2[:], ALU.add, r=['m1', 'm2'], w=['mix'])
                        for k in range(8):
                            P.tr(ps_tr[:, k, :], mix_bf[:, k * 128:(k + 1) * 128], ident[:], r=['mix', 'ident'], w=['ps_tr'])
                        P.cp('act', mT[:], ps_tr[:], r=['ps_tr'], w=['mT'])
                        h2, h2k = h2R.next()
                        for n2 in range(2):
                            nsl = slice(n2 * 512, (n2 + 1) * 512)
                            for k in range(8):
                                P.mm(psv[:, n2, :], mT[:, k, :], wO[:, k, nsl], k == 0, k == 7, r=['mT', ('wO', k)], w=['ps_S'])
                            P.tt('dve', h2[:, nsl], psv[:, n2, :], h_t[:, nsl], ALU.add, r=['ps_S', hk], w=[h2k])
                        P.dma('sp', h_s[tt * 128:(tt + 1) * 128, :], h2[:], r=[h2k])
                P.barrier()
                P.emit()
            if debug == 2:
                break
            with ExitStack() as ph:
                wU = P.sb(ph, [128, 8, 2 * DFF], BF16)
                for k in range(8):
                    P.dma('pool', wU[:, k, :], w_up[l, k * 128:(k + 1) * 128, :], w=[('wU', k)])
                gs_bc = P.sb(ph, [128, D], F32)
                P.dma('sp', gs_bc[:], g_ffn[l:l + 1, :].to_broadcast([128, D]), w=['gs0'])
                P.ts('dve', gs_bc[:], gs_bc[:], float(math.sqrt(D)), ALU.mult, r=['gs0'], w=['gs'])
                cwr = P.sb(ph, [44, 4, 128], F32)
                for jj in range(3):
                    P.dma('sp', cwr[:, jj, :], conv_w[l, jj, :].rearrange("(c p) -> c p", p=128), w=['cwr'])
                P.dma('sp', cwr[:, 3, :], conv_b[l, :].rearrange("(c p) -> c p", p=128), w=['cwr'])
                cw = P.sb(ph, [128, 4, 44], F32)
                ps_c = P.ps(ph, [128, 4, 44], F32)
                for jj in range(4):
                    P.tr(ps_c[:, jj, :], cwr[:, jj, :], identf[0:44, 0:44], r=['cwr', 'identf'], w=['ps_c'])
                P.cp('dve', cw[:], ps_c[:], r=['ps_c'], w=['cw'])
                halo = P.sb(ph, [128, 44, 2], F32)
                hG = Rot(P, ph, 1, [128, 4, D], F32, 'hG')
                uG = Rot(P, ph, 1, [128, 4, D], BF16, 'uG')
                uTG = Rot(P, ph, 2, [128, 8, 512], BF16, 'uTG')
                ssR = Rot(P, ph, 2, [128, 8], F32, 'ss')
                junkA = P.sb(ph, [128, D], BF16)
                xsR = Rot(P, ph, 3, [128, 514], F32, 'xs')
                acR = Rot(P, ph, 3, [128, 512], F32, 'acc')
                glR = Rot(P, ph, 2, [128, 512], F32, 'gl')
                prR = Rot(P, ph, 2, [128, NFC, 512], BF16, 'prod')
                ptrR = Rot(P, ph, 2, [128, 8, 128], BF16, 'ptr', psum=True)
                ppR = Rot(P, ph, 4, [128, 512], F32, 'pp', psum=True)
                for g in range(NG):
                    if (g * 512) % L == 0:
                        P.memset('dve', halo[:], 0.0, w=['halo'])
                    hg, hgk = hG.next()
                    P.dma('sp', hg[:], h_s[g * 512:(g + 1) * 512, :].rearrange("(a p) d -> p a d", p=128), w=[hgk])
                    ss, ssk = ssR.next()
                    for a in range(4):
                        P.act(junkA[:], hg[:, a, :], AF.Square, accum=ss[:, a:a + 1], r=[hgk], w=[ssk, 'junkA'])
                    P.rsqrt(ss[:, 4:8], ss[:, 0:4], D * EPS, r=[ssk], w=[ssk])
                    ug, ugk = uG.next()
                    uT, uTk = uTG.next()
                    for a in range(4):
                        P.stt('dve', ug[:, a, :], hg[:, a, :], ss[:, 4 + a:5 + a], gs_bc[:], ALU.mult, ALU.mult,
                              r=[hgk, ssk, 'gs'], w=[ugk])
                        ptr, pk = ptrR.next()
                        for k in range(8):
                            P.tr(ptr[:, k, :], ug[:, a, k * 128:(k + 1) * 128], ident[:], r=[ugk, 'ident'], w=[pk])
                        P.cp('act', uT[:, :, a * 128:(a + 1) * 128], ptr[:], r=[pk], w=[uTk])
                    prod, prk = prR.next()
                    for q in range(NFC):
                        res = []
                        for fc in (q, q + NFC):
                            pp, ppk = ppR.next()
                            for k in range(8):
                                P.mm(pp[:], wU[:, k, fc * 128:(fc + 1) * 128], uT[:, k, :], k == 0, k == 7,
                                     r=[uTk, ('wU', k)], w=[ppk])
                            xs, xsk = xsR.next()
                            P.cp('act', xs[:, 2:514], pp[:], r=[ppk], w=[xsk])
                            P.cp('pool', xs[:, 0:2], halo[:, fc, :], r=['halo'], w=[xsk])
                            acc, ack = acR.next()
                            P.act(acc[:], pp[:], AF.Identity, bias=cw[:, 3, fc:fc + 1], scale=cw[:, 2, fc:fc + 1],
                                  r=[ppk, 'cw'], w=[ack])
                            P.stt('dve', acc[:], xs[:, 1:513], cw[:, 1, fc:fc + 1], acc[:], ALU.mult, ALU.add,
                                  r=[xsk, 'cw', ack], w=[ack])
                            P.stt('dve', acc[:], xs[:, 0:512], cw[:, 0, fc:fc + 1], acc[:], ALU.mult, ALU.add,
                                  r=[xsk, 'cw', ack], w=[ack])
                            P.cp('pool', halo[:, fc, :], xs[:, 512:514], r=[xsk], w=['halo'])
                            res.append((acc, ack))
                        gl, glk = glR.next()
                        P.act(gl[:], res[0][0][:], AF.Gelu_apprx_tanh, r=[res[0][1]], w=[glk])
                        P.tt('dve', prod[:, q, :], gl[:], res[1][0][:], ALU.mult, r=[glk, res[1][1]], w=[prk])
                    P.dma('sp', pr_s[:, :, g * 512:(g + 1) * 512], prod[:], r=[prk])
                P.barrier()
                P.emit()
            if debug == 3:
                break
            with ExitStack() as ph:
                wD = P.sb(ph, [128, NFC, D], BF16)
                for fc in range(NFC):
                    P.dma('pool', wD[:, fc, :], w_down[l, fc * 128:(fc + 1) * 128, :], w=[('wD', fc)])
                wG = P.sb(ph, [128, 8, D], BF16)
                for k in range(8):
                    P.dma('pool', wG[:, k, :], w_pg[l, k * 128:(k + 1) * 128, :], w=[('wG', k)])
                wPp = P.sb(ph, [128, 2, D], BF16)
                for k in range(2):
                    P.dma('pool', wPp[:, k, :], w_pp[l, k * 128:(k + 1) * 128, :], w=[('wPp', k)])
                gs_bc = P.sb(ph, [128, D], F32)
                P.dma('sp', gs_bc[:], g_ple[l:l + 1, :].to_broadcast([128, D]), w=['gs0'])
                P.ts('dve', gs_bc[:], gs_bc[:], float(math.sqrt(D)), ALU.mult, r=['gs0'], w=['gs'])
                hR = Rot(P, ph, 2, [128, D], F32, 'h')
                h3R = Rot(P, ph, 2, [128, D], F32, 'h3')
                h4R = Rot(P, ph, 2, [128, D], F32, 'h4')
                prR = Rot(P, ph, 2, [128, NFC, 128], BF16, 'pr')
                pbR = Rot(P, ph, 2, [128, PLE], BF16, 'pb')
                pTR = Rot(P, ph, 2, [128, 2, 128], BF16, 'pT')
                ssR = Rot(P, ph, 2, [128, 2], F32, 'ss')
                uR = Rot(P, ph, 2, [128, D], BF16, 'u')
                uTR = Rot(P, ph, 2, [128, 8, 128], BF16, 'uT')
                sgR = Rot(P, ph, 2, [128, D], F32, 'sgt')
                tmR = Rot(P, ph, 2, [128, D], F32, 'tm')
                junkA = P.sb(ph, [128, D], BF16)
                psD = P.ps(ph, [128, 2, 512], F32)
                psG = P.ps(ph, [128, 2, 512], F32)
                psP = P.ps(ph, [128, 2, 512], F32)
                ptrR = Rot(P, ph, 2, [128, 8, 128], BF16, 'ptr', psum=True)
                hdst = y if l == DEPTH - 1 else h_s
                for tt in range(NTT):
                    rows = slice(tt * 128, (tt + 1) * 128)
                    h_t, hk = hR.next()
                    P.dma('sp', h_t[:], h_s[rows, :], w=[hk])
                    pr, prk = prR.next()
                    P.dma('sp', pr[:], pr_s[:, :, rows], w=[prk])
                    pb, pbk = pbR.next()
                    P.dma('pool', pb[:], p_in[l, rows, :], w=[pbk])
                    h3, h3k = h3R.next()
                    for n2 in range(2):
                        nsl = slice(n2 * 512, (n2 + 1) * 512)
                        for fc in range(NFC):
                            P.mm(psD[:, n2, :], pr[:, fc, :], wD[:, fc, nsl], fc == 0, fc == NFC - 1,
                                 r=[prk, ('wD', fc)], w=[('psD', n2)])
                        P.tt('dve', h3[:, nsl], psD[:, n2, :], h_t[:, nsl], ALU.add, r=[('psD', n2), hk], w=[h3k])
                    ss, ssk = ssR.next()
                    P.act(junkA[:], h3[:], AF.Square, accum=ss[:, 0:1], r=[h3k], w=[ssk, 'junkA'])
                    P.rsqrt(ss[:, 1:2], ss[:, 0:1], D * EPS, r=[ssk], w=[ssk])
                    u, uk = uR.next()
                    P.stt('dve', u[:], h3[:], ss[:, 1:2], gs_bc[:], ALU.mult, ALU.mult, r=[h3k, ssk, 'gs'], w=[uk])
                    ptr, pk = ptrR.next()
                    for k in range(8):
                        P.tr(ptr[:, k, :], u[:, k * 128:(k + 1) * 128], ident[:], r=[uk, 'ident'], w=[pk])
                    uT, uTk = uTR.next()
                    P.cp('act', uT[:], ptr[:], r=[pk], w=[uTk])
                    ptr, pk = ptrR.next()
                    for k in range(2):
                        P.tr(ptr[:, k, :], pb[:, k * 128:(k + 1) * 128], ident[:], r=[pbk, 'ident'], w=[pk])
                    pT, pTk = pTR.next()
                    P.cp('act', pT[:], ptr[:, 0:2, :], r=[pk], w=[pTk])
                    sgt, sgk = sgR.next()
                    tm, tmk = tmR.next()
                    h4, h4k = h4R.next()
                    for n2 in range(2):
                        nsl = slice(n2 * 512, (n2 + 1) * 512)
                        for k in range(8):
                            P.mm(psG[:, n2, :], uT[:, k, :], wG[:, k, nsl], k == 0, k == 7, r=[uTk, ('wG', k)], w=[('psG', n2)])
                        P.act(sgt[:, nsl], psG[:, n2, :], AF.Sigmoid, r=[('psG', n2)], w=[sgk])
                        for k in range(2):
                            P.mm(psP[:, n2, :], pT[:, k, :], wPp[:, k, nsl], k == 0, k == 1, r=[pTk, ('wPp', k)], w=[('psP', n2)])
                        P.tt('dve', tm[:, nsl], psP[:, n2, :], sgt[:, nsl], ALU.mult, r=[('psP', n2), sgk], w=[tmk])
                        P.tt('dve', h4[:, nsl], tm[:, nsl], h3[:, nsl], ALU.add, r=[tmk, h3k], w=[h4k])
                    P.dma('sp', hdst[rows, :], h4[:], r=[h4k])
                P.barrier()
                P.emit()
    return nc


def rope_table(L):
    NT = L // 128
    inv = 1.0 / (10000.0 ** (np.arange(0, 64, 2, dtype=np.float32) / np.float32(64.0)))
    ang = np.arange(L, dtype=np.float32)[:, None] * inv[None, :].astype(np.float32)
    cs = np.concatenate([np.cos(ang), np.sin(ang)], axis=1).astype(np.float32)
    return np.ascontiguousarray(cs.reshape(NT, 128, 64).transpose(1, 0, 2))


_CACHE = {}


def run(inputs, L, NSEQ, DEPTH, ncores, debug=False):
    key = (L, NSEQ, DEPTH, debug)
    if key not in _CACHE:
        _CACHE[key] = build(L, NSEQ, DEPTH, debug)
    nc = _CACHE[key]
    T = NSEQ * L
    xs = np.ascontiguousarray(inputs['x'], dtype=np.float32).reshape(ncores, T, D)
    ps = np.ascontiguousarray(inputs['p'], dtype=np.float32).reshape(DEPTH, ncores, T, PLE)
    cs = rope_table(L)
    in_maps = []
    for c in range(ncores):
        m = {k: np.ascontiguousarray(v, dtype=np.float32) for k, v in inputs.items() if k not in ('x', 'p')}
        m['x'] = xs[c]
        m['p'] = np.ascontiguousarray(ps[:, c])
        m['cs_tab'] = cs
        in_maps.append(m)
    res = run_bass_kernel_spmd(nc, in_maps, core_ids=list(range(ncores)))
    return res


def kernel(**inputs):
    B, L, _ = inputs['x'].shape
    DEPTH = inputs['p'].shape[0]
    ncores = 8
    NSEQ = B // ncores
    res = run(inputs, L, NSEQ, DEPTH, ncores)
    out = np.stack([np.asarray(r['y'], dtype=np.float32) for r in res.results], axis=0)
    return out.reshape(B, L, D)
```

```python
import math
import numpy as np
from contextlib import ExitStack
import concourse.bass as bass
import concourse.mybir as mybir
from concourse.bass_utils import run_bass_kernel_spmd

F32 = mybir.dt.float32
BF16 = mybir.dt.bfloat16
AF = mybir.ActivationFunctionType
ALU = mybir.AluOpType
AX = mybir.AxisListType

D = 1024
NIN = 4548
DFF = 2816
NFC = 22
PLE = 256
EPS = 1e-6
KBIS = 16
NEG = -1.0e30
COMPUTE = ('pe', 'act', 'dve', 'pool')
ENGS = ('sp', 'pool', 'act', 'dve', 'pe')


class Op(object):
    __slots__ = ('eng', 'fn', 'deps', 'dma', 'sem', 'semval', 'ms', 'waited')


class Prog(object):
    def __init__(self, nc, es, n_sp=16, n_pq=8):
        self.nc = nc
        self.esem = {e: es.enter_context(nc.semaphore('sem_' + e)) for e in COMPUTE}
        self.dsem = {'sp': [es.enter_context(nc.semaphore('dsp%d' % i)) for i in range(n_sp)],
                     'pool': [es.enter_context(nc.semaphore('dpq%d' % i)) for i in range(n_pq)]}
        self.duse = {q: [0] * len(v) for q, v in self.dsem.items()}
        self.dlast = {q: [None] * len(v) for q, v in self.dsem.items()}
        self.dnext = {q: 0 for q in self.dsem}
        self.mscount = {e: 0 for e in COMPUTE}
        self.waitedv = {e: {} for e in ENGS}
        self.state = {}
        self.ops = {e: [] for e in ENGS}
        self.last = {e: None for e in ENGS}
        self.cnt = 0

    def sb(self, ctx, shape, dt):
        self.cnt += 1
        return ctx.enter_context(self.nc.sbuf_tensor('sb%d' % self.cnt, list(shape), dt))

    def ps(self, ctx, shape, dt):
        self.cnt += 1
        return ctx.enter_context(self.nc.psum_tensor('ps%d' % self.cnt, list(shape), dt))

    def op(self, eng, fn, r=(), w=(), dma=False):
        o = Op()
        o.eng = eng; o.fn = fn; o.dma = dma; o.waited = False; o.ms = 0; o.sem = None; o.semval = 0
        deps = {}
        for k in r:
            st = self.state.get(k)
            if st is not None and st[0] is not None:
                deps[st[0]] = 'raw'
        for k in w:
            st = self.state.get(k)
            if st is not None:
                if st[0] is not None:
                    deps.setdefault(st[0], 'waw')
                for ro in st[1].values():
                    deps.setdefault(ro, 'war')
                for ro in st[2]:
                    deps.setdefault(ro, 'war')
        if dma:
            n = len(self.dsem[eng])
            slot = self.dnext[eng]
            self.dnext[eng] = (slot + 1) % n
            prev = self.dlast[eng][slot]
            if prev is not None:
                deps.setdefault(prev, 'slot')
            self.duse[eng][slot] += 1
            o.sem = self.dsem[eng][slot]
            o.semval = 16 * self.duse[eng][slot]
            self.dlast[eng][slot] = o
        fd = []
        for d, kind in deps.items():
            if d is o:
                continue
            if (not d.dma) and (not dma) and d.eng == eng:
                if eng == 'pe':
                    continue
            fd.append(d)
            d.waited = True
        o.deps = fd
        for k in r:
            st = self.state.setdefault(k, [None, {}, []])
            if dma:
                st[2].append(o)
            else:
                st[1][eng] = o
        for k in w:
            self.state[k] = [o, {}, []]
        self.ops[eng].append(o)
        if not dma:
            self.last[eng] = o
        return o

    def barrier(self):
        targets = [self.last[e] for e in COMPUTE if self.last[e] is not None]
        for q in self.dlast:
            targets += [d for d in self.dlast[q] if d is not None]
        for e in ENGS:
            o = Op()
            o.eng = e; o.fn = None; o.dma = False; o.waited = False; o.ms = 0; o.sem = None; o.semval = 0
            o.deps = [t for t in targets if t.dma or t.eng != e]
            for t in o.deps:
                t.waited = True
            self.ops[e].append(o)
        self.state = {}

    def emit(self):
        for e in COMPUTE:
            for o in self.ops[e]:
                if o.fn is not None and (not o.dma) and o.waited:
                    self.mscount[e] += 1
                    o.ms = self.mscount[e]
        nc = self.nc
        with nc.Block() as block:
            decos = {'sp': block.sync, 'pool': block.gpsimd, 'act': block.scalar, 'dve': block.vector,
                     'pe': block.tensor}
            for e in ENGS:
                ops = self.ops[e]
                if not ops:
                    continue

                def body(eo, ops=ops, e=e):
                    wt = self.waitedv[e]
                    for o in ops:
                        for d in o.deps:
                            if d.dma:
                                sem, val = d.sem, d.semval
                            else:
                                sem, val = self.esem[d.eng], d.ms
                            key = id(sem)
                            if wt.get(key, 0) < val:
                                eo.wait_ge(sem, val)
                                wt[key] = val
                        if o.fn is None:
                            continue
                        ins = o.fn(eo)
                        if o.dma:
                            ins.then_inc(o.sem, 16)
                        elif o.waited:
                            ins.then_inc(self.esem[e], 1)
                decos[e](body)
        self.ops = {e: [] for e in ENGS}

    def dma(self, q, out, in_, r=(), w=()):
        return self.op(q, lambda e: e.dma_start(out=out, in_=in_), r, w, dma=True)

    def tt(self, eng, out, in0, in1, op, r=(), w=()):
        return self.op(eng, lambda e: e.tensor_tensor(out=out, in0=in0, in1=in1, op=op), r, w)

    def ts(self, eng, out, in0, s1, op0, s2=None, op1=None, accum=None, r=(), w=()):
        if op1 is None:
            return self.op(eng, lambda e: e.tensor_scalar(out=out, in0=in0, scalar1=s1, scalar2=None, op0=op0), r, w)
        if accum is None:
            return self.op(eng, lambda e: e.tensor_scalar(out=out, in0=in0, scalar1=s1, scalar2=s2, op0=op0,
                                                           op1=op1), r, w)
        return self.op(eng, lambda e: e.tensor_scalar(out=out, in0=in0, scalar1=s1, scalar2=s2, op0=op0,
                                                       op1=op1, accum_out=accum), r, w)

    def stt(self, eng, out, in0, scalar, in1, op0, op1, r=(), w=()):
        return self.op(eng, lambda e: e.scalar_tensor_tensor(out=out, in0=in0, scalar=scalar, in1=in1,
                                                              op0=op0, op1=op1), r, w)

    def act(self, out, in_, func, bias=None, scale=None, accum=None, r=(), w=()):
        kw = {}
        if bias is not None:
            kw['bias'] = bias
        if scale is not None:
            kw['scale'] = scale
        if accum is not None:
            kw['accum_out'] = accum
        return self.op('act', lambda e: e.activation(out=out, in_=in_, func=func, **kw), r, w)

    def cp(self, eng, out, in_, r=(), w=()):
        if eng == 'act':
            return self.op('act', lambda e: e.copy(out=out, in_=in_), r, w)
        return self.op(eng, lambda e: e.tensor_copy(out=out, in_=in_), r, w)

    def mm(self, out, lhsT, rhs, start, stop, r=(), w=(), acc0=False):
        if acc0:
            return self.op('pe', lambda e: e.matmul(out, lhsT=lhsT, rhs=rhs, start=False, stop=False,
                                                    skip_group_check=True), r, w)
        return self.op('pe', lambda e: e.matmul(out, lhsT=lhsT, rhs=rhs, start=start, stop=stop), r, w)

    def tr(self, out, in_, ident, r=(), w=()):
        return self.op('pe', lambda e: e.transpose(out=out, in_=in_, identity=ident), r, w)

    def red(self, out, in_, op, r=(), w=()):
        return self.op('dve', lambda e: e.tensor_reduce(out=out, in_=in_, axis=AX.X, op=op), r, w)

    def recip(self, out, in_, r=(), w=()):
        return self.op('dve', lambda e: e.reciprocal(out=out, in_=in_), r, w)

    def rsqrt(self, out, in_, addc, r=(), w=()):
        self.op('act', lambda e: e.activation(out=out, in_=in_, func=AF.Sqrt, bias=addc, scale=1.0), r, w)
        return self.op('dve', lambda e: e.reciprocal(out=out, in_=out), w, w)

    def memset(self, eng, ap, val, r=(), w=()):
        return self.op(eng, lambda e: e.memset(ap, val), r, w)


class Rot(object):
    def __init__(self, P, ctx, n, shape, dt, name, psum=False):
        self.bufs = [(P.ps if psum else P.sb)(ctx, shape, dt) for _ in range(n)]
        self.name = name
        self.i = -1

    def next(self):
        self.i = (self.i + 1) % len(self.bufs)
        return self.bufs[self.i], (self.name, self.i)


def bc_mid(ap, n):
    return ap.unsqueeze(1).to_broadcast([ap.shape[0], n, ap.shape[1]])


def bc_last(ap, n):
    return ap.unsqueeze(2).to_broadcast([ap.shape[0], ap.shape[1], n])


def build(L, NSEQ, DEPTH, debug=False):
    NT = L // 128
    T = NSEQ * L
    NTT = T // 128
    KTOP = min(256, L // 4)
    KT = KTOP // 128
    NG = T // 512
    nc = bass.Bass("TRN2", target_bir_lowering=False)

    def din(name, shape, dt=F32):
        return nc.dram_tensor(name, list(shape), dt, kind="ExternalInput").ap()

    x = din("x", [T, D])
    p_in = din("p", [DEPTH, T, PLE])
    g_mix = din("g_mix_norm", [DEPTH, D])
    w_in = din("w_in", [DEPTH, D, NIN])
    g_q = {f: din(f, [DEPTH, 64]) for f in ("g_qa", "g_ka", "g_qb", "g_kb")}
    lamv = {f: din(f, [DEPTH, 64]) for f in ("lam_q1", "lam_k1", "lam_q2", "lam_k2")}
    g_sub = din("g_subln", [DEPTH, 128])
    w_bra = din("w_branch_a", [DEPTH, 512, D])
    w_brb = din("w_branch_b", [DEPTH, 512, D])
    w_out = din("w_out", [DEPTH, D, D])
    g_ffn = din("g_ffn_norm", [DEPTH, D])
    w_up = din("w_up", [DEPTH, D, 2 * DFF])
    conv_w = din("conv_w", [DEPTH, 3, 2 * DFF])
    conv_b = din("conv_b", [DEPTH, 2 * DFF])
    w_down = din("w_down", [DEPTH, DFF, D])
    g_ple = din("g_ple_norm", [DEPTH, D])
    w_pg = din("w_ple_gate", [DEPTH, D, D])
    w_pp = din("w_ple_proj", [DEPTH, PLE, D])
    cs_in = din("cs_tab", [128, NT, 64])
    y = nc.dram_tensor("y", [T, D], F32, kind="ExternalOutput").ap()

    skind = "ExternalOutput" if debug else "Internal"

    def dscr(name, shape, dt):
        return nc.dram_tensor(name, list(shape), dt, kind=skind).ap()

    h_s = dscr("h_s", [T, D], F32)
    fm_s = dscr("fm_s", [NSEQ, 128, 16, L], BF16)
    va_s = dscr("va_s", [NSEQ, NT, 128, 64], BF16)
    vb_s = dscr("vb_s", [NSEQ, NT, 128, 512], BF16)
    wi_s = dscr("wi_s", [NSEQ, NT, 128, 4], F32)
    sg_s = dscr("sg_s", [T, 2048], BF16)
    pr_s = dscr("pr_s", [128, NFC, T], BF16)

    with ExitStack() as es:
        P = Prog(nc, es)
        ident = P.sb(es, [128, 128], BF16)
        identf = P.sb(es, [128, 128], F32)
        caus01T = P.sb(es, [128, 128], BF16)
        negmask = P.sb(es, [128, 128], F32)
        cs_sb = P.sb(es, [128, NT, 64], F32)
        pow2 = P.sb(es, [128, KBIS + 1], F32)
        lam_sb = P.sb(es, [128, DEPTH], F32)
        nlam_sb = P.sb(es, [128, DEPTH], F32)
        ones_bf = P.sb(es, [128, 128], BF16)
        zer_f = P.sb(es, [128, 128], F32)

        with ExitStack() as ph:
            P.memset('pool', ident[:], 0.0, w=['ident'])
            P.op('pool', lambda e: e.affine_select(out=ident[:], in_=ident[:], pattern=[[-1, 128]],
                                                   compare_op=ALU.not_equal, fill=1.0, base=0,
                                                   channel_multiplier=1), r=['ident'], w=['ident'])
            P.memset('pool', identf[:], 0.0, w=['identf'])
            P.op('pool', lambda e: e.affine_select(out=identf[:], in_=identf[:], pattern=[[-1, 128]],
                                                   compare_op=ALU.not_equal, fill=1.0, base=0,
                                                   channel_multiplier=1), r=['identf'], w=['identf'])
            P.memset('pool', ones_bf[:], 1.0, w=['ones'])
            P.memset('pool', zer_f[:], 0.0, w=['zer'])
            P.op('pool', lambda e: e.affine_select(out=caus01T[:], in_=ones_bf[:], pattern=[[1, 128]],
                                                   compare_op=ALU.is_ge, fill=0.0, base=0,
                                                   channel_multiplier=-1), r=['ones'], w=['caus'])
            P.op('pool', lambda e: e.affine_select(out=negmask[:], in_=zer_f[:], pattern=[[-1, 128]],
                                                   compare_op=ALU.is_ge, fill=NEG, base=0,
                                                   channel_multiplier=1), r=['zer'], w=['negm'])
            P.dma('sp', cs_sb[:], cs_in[:, :, :], w=['cs'])
            for k in range(KBIS + 1):
                P.memset('dve', pow2[:, k:k + 1], float(2.0 ** (-k)), w=['pow2'])
            lt = {f: P.sb(ph, [128, 64], F32) for f in lamv}
            ltmp = P.sb(ph, [128, 64], F32)
            ld = P.sb(ph, [128, 4], F32)
            for l in range(DEPTH):
                lam_init = 0.8 - 0.6 * math.exp(-0.3 * l)
                for f in lamv:
                    P.dma('sp', lt[f][:], lamv[f][l:l + 1, :].to_broadcast([128, 64]), w=[('lt', f)])
                P.tt('dve', ltmp[:], lt['lam_q1'][:], lt['lam_k1'][:], ALU.mult, r=[('lt', 'lam_q1'), ('lt', 'lam_k1')], w=['ltmp'])
                P.red(ld[:, 0:1], ltmp[:], ALU.add, r=['ltmp'], w=['ld0'])
                P.tt('dve', ltmp[:], lt['lam_q2'][:], lt['lam_k2'][:], ALU.mult, r=[('lt', 'lam_q2'), ('lt', 'lam_k2')], w=['ltmp'])
                P.red(ld[:, 1:2], ltmp[:], ALU.add, r=['ltmp'], w=['ld1'])
                P.act(ld[:, 2:4], ld[:, 0:2], AF.Exp, r=['ld0', 'ld1'], w=['ld23'])
                P.stt('dve', lam_sb[:, l:l + 1], ld[:, 2:3], lam_init, ld[:, 3:4], ALU.add, ALU.subtract,
                      r=['ld23'], w=[('lam', l)])
                P.ts('dve', nlam_sb[:, l:l + 1], lam_sb[:, l:l + 1], -1.0, ALU.mult, r=[('lam', l)], w=[('nlam', l)])
            P.barrier()
            P.emit()

        for l in range(DEPTH):
            lam_init = 0.8 - 0.6 * math.exp(-0.3 * l)
            hsrc = x if l == 0 else h_s
            with ExitStack() as ph:
                w_sb = P.sb(ph, [128, 8, NIN], BF16)
                for k in range(8):
                    P.dma('pool', w_sb[:, k, :], w_in[l, k * 128:(k + 1) * 128, :], w=[('w', k)])
                gs_bc = P.sb(ph, [128, D], F32)
                P.dma('sp', gs_bc[:], g_mix[l:l + 1, :].to_broadcast([128, D]), w=['gs0'])
                P.ts('dve', gs_bc[:], gs_bc[:], float(math.sqrt(D)), ALU.mult, r=['gs0'], w=['gs'])
                tabs = {}
                for f in g_q:
                    g8 = P.sb(ph, [128, 64], F32)
                    P.dma('sp', g8[:], g_q[f][l:l + 1, :].to_broadcast([128, 64]), w=[('g8', f)])
                    P.ts('dve', g8[:], g8[:], 8.0, ALU.mult, r=[('g8', f)], w=[('g8s', f)])
                    tb = P.sb(ph, [128, NT, 4, 32], F32)
                    for j, (co, go) in enumerate(((0, 0), (32, 32), (0, 32), (32, 0))):
                        P.tt('dve', tb[:, :, j, :], cs_sb[:, :, co:co + 32], bc_mid(g8[:, go:go + 32], NT), ALU.mult,
                             r=['cs', ('g8s', f)], w=[('tab', f, j)])
                    tabs[f] = tb
                hR = Rot(P, ph, 2, [128, D], F32, 'h')
                uR = Rot(P, ph, 2, [128, D], BF16, 'u')
                uTR = Rot(P, ph, 2, [128, 8, 128], BF16, 'uT')
                ssR = Rot(P, ph, 2, [128, 2], F32, 'ss')
                junkA = P.sb(ph, [128, D], BF16)
                sqR = Rot(P, ph, 2, [128, 512], F32, 'sq')
                xnR = Rot(P, ph, 2, [128, 512], F32, 'xn')
                smR = Rot(P, ph, 2, [128, 16], F32, 'sm')
                tR = [Rot(P, ph, 2, [128, 8, 32], F32, 't%d' % i) for i in range(4)]
                rqR = Rot(P, ph, 2, [128, 16, 128], BF16, 'rq')
                stR = Rot(P, ph, 2, [128, 16, 128], BF16, 'stage')
                vaR = Rot(P, ph, 2, [128, 64], BF16, 'va')
                vbR = Rot(P, ph, 2, [128, 512], BF16, 'vb')
                wiR = Rot(P, ph, 2, [128, 4], F32, 'wi')
                sgR = Rot(P, ph, 2, [128, 2048], BF16, 'sg')
                ptrR = Rot(P, ph, 2, [128, 8, 128], BF16, 'ptr', psum=True)
                ppR = Rot(P, ph, 4, [128, 512], F32, 'pp', psum=True)

                def normrope(src, srckey, H, fam, pos, rq, rqkey, blk0, half0, normed):
                    sv = src.rearrange("p (h d) -> p h d", d=64)
                    if normed:
                        sq, sqk = sqR.next()
                        P.act(sq[:, 0:H * 64], src, AF.Square, r=[srckey], w=[sqk])
                        sm, smk = smR.next()
                        P.red(sm[:, 0:H], sq[:, 0:H * 64].rearrange("p (h d) -> p h d", d=64), ALU.add, r=[sqk], w=[smk])
                        P.rsqrt(sm[:, 8:8 + H], sm[:, 0:H], 64.0 * EPS, r=[smk], w=[smk])
                        xn, xnk = xnR.next()
                        xv = xn[:, 0:H * 64].rearrange("p (h d) -> p h d", d=64)
                        P.tt('dve', xv, sv, bc_last(sm[:, 8:8 + H], 64), ALU.mult, r=[srckey, smk], w=[xnk])
                        xk = xnk
                        tb = tabs[fam]
                        C1, S2, C2, S1 = (tb[:, pos, j, :] for j in range(4))
                        tr_ = [('tab', fam, j) for j in range(4)]
                    else:
                        xv, xk = sv, srckey
                        C1 = C2 = cs_sb[:, pos, 0:32]
                        S1 = S2 = cs_sb[:, pos, 32:64]
                        tr_ = ['cs'] * 4
                    x1 = xv[:, :, 0:32]
                    x2 = xv[:, :, 32:64]
                    ov = rq[:].rearrange("p b c -> p (b c)")[:, blk0 * 128 + half0: blk0 * 128 + half0 + H * 64]
                    ov = ov.rearrange("p (h d) -> p h d", d=64)
                    tb_ = [r_.next() for r_ in tR]
                    (t1, k1), (t2, k2), (t3, k3), (t4, k4) = tb_
                    P.tt('dve', t1[:, 0:H, :], x1, bc_mid(C1, H), ALU.mult, r=[xk, tr_[0]], w=[k1])
                    P.tt('dve', t2[:, 0:H, :], x2, bc_mid(S2, H), ALU.mult, r=[xk, tr_[1]], w=[k2])
                    P.tt('dve', ov[:, :, 0:32], t1[:, 0:H, :], t2[:, 0:H, :], ALU.subtract, r=[k1, k2], w=[rqkey])
                    P.tt('dve', t3[:, 0:H, :], x2, bc_mid(C2, H), ALU.mult, r=[xk, tr_[2]], w=[k3])
                    P.tt('dve', t4[:, 0:H, :], x1, bc_mid(S1, H), ALU.mult, r=[xk, tr_[3]], w=[k4])
                    P.tt('dve', ov[:, :, 32:64], t3[:, 0:H, :], t4[:, 0:H, :], ALU.add, r=[k3, k4], w=[rqkey])

                groups = [(0, 512), (512, 964), (964, 1476), (1476, 1988), (1988, 2500),
                          (2500, 3012), (3012, 3524), (3524, 4036), (4036, 4548)]
                for tt in range(NTT):
                    sq_i, pos = tt // NT, tt % NT
                    h_t, hk = hR.next()
                    P.dma('sp', h_t[:], hsrc[tt * 128:(tt + 1) * 128, :], w=[hk])
                    ss, ssk = ssR.next()
                    P.act(junkA[:], h_t[:], AF.Square, accum=ss[:, 0:1], r=[hk], w=[ssk, 'junkA'])
                    P.rsqrt(ss[:, 1:2], ss[:, 0:1], D * EPS, r=[ssk], w=[ssk])
                    u, uk = uR.next()
                    P.stt('dve', u[:], h_t[:], ss[:, 1:2], gs_bc[:], ALU.mult, ALU.mult, r=[hk, ssk, 'gs'], w=[uk])
                    ptr, pk = ptrR.next()
                    for k in range(8):
                        P.tr(ptr[:, k, :], u[:, k * 128:(k + 1) * 128], ident[:], r=[uk, 'ident'], w=[pk])
                    uT, uTk = uTR.next()
                    P.cp('act', uT[:], ptr[:], r=[pk], w=[uTk])
                    rq, rqk = rqR.next()
                    sg, sgk = sgR.next()
                    for gi, (c0, c1) in enumerate(groups):
                        pp, ppk = ppR.next()
                        for k in range(8):
                            P.mm(pp[:, 0:c1 - c0], uT[:, k, :], w_sb[:, k, c0:c1], k == 0, k == 7,
                                 r=[uTk, ('w', k)], w=[ppk])
                        if gi == 0:
                            normrope(pp[:, 0:512], ppk, 8, 'g_qa', pos, rq, rqk, 0, 0, True)
                        elif gi == 1:
                            normrope(pp[:, 0:64], ppk, 1, 'g_ka', pos, rq, rqk, 7, 0, True)
                            P.cp('dve', rq[:, 7, 64:128], rq[:, 7, 0:64], r=[rqk], w=[rqk])
                            va, vak = vaR.next()
                            P.cp('act', va[:], pp[:, 64:128], r=[ppk], w=[vak])
                            P.dma('sp', va_s[sq_i, pos, :, :], va[:], r=[vak])
                            normrope(pp[:, 128:448], ppk, 5, None, pos, rq, rqk, 4, 0, False)
                            P.cp('dve', rq[:, 6, 64:128], rq[:, 6, 0:64], r=[rqk], w=[rqk])
                            wi, wik = wiR.next()
                            P.ts('dve', wi[:], pp[:, 448:452], 1.0 / 16.0, ALU.mult, r=[ppk], w=[wik])
                            P.dma('sp', wi_s[sq_i, pos, :, :], wi[:], r=[wik])
                        elif gi == 2:
                            normrope(pp[:, 0:512], ppk, 8, 'g_qb', pos, rq, rqk, 8, 0, True)
                        elif gi == 3:
                            normrope(pp[:, 0:512], ppk, 8, 'g_kb', pos, rq, rqk, 12, 0, True)
                        elif gi == 4:
                            vb, vbk = vbR.next()
                            P.cp('act', vb[:], pp[:, 0:512], r=[ppk], w=[vbk])
                            P.dma('sp', vb_s[sq_i, pos, :, :], vb[:], r=[vbk])
                        else:
                            q = gi - 5
                            P.act(sg[:, q * 512:(q + 1) * 512], pp[:, 0:512], AF.Sigmoid, r=[ppk], w=[sgk])
                    P.dma('sp', sg_s[tt * 128:(tt + 1) * 128, :], sg[:], r=[sgk])
                    stg, stk = stR.next()
                    for hb in range(2):
                        ptr, pk = ptrR.next()
                        for b in range(8):
                            P.tr(ptr[:, b, :], rq[:, hb * 8 + b, :], ident[:], r=[rqk, 'ident'], w=[pk])
                        P.cp('act', stg[:, hb * 8:(hb + 1) * 8, :], ptr[:], r=[pk], w=[stk])
                    P.dma('sp', fm_s[sq_i, :, :, pos * 128:(pos + 1) * 128], stg[:], r=[stk])
                P.barrier()
                P.emit()
            if debug == 1:
                break
            with ExitStack() as ph:
                wA = P.sb(ph, [128, 4, D], BF16)
                wB = P.sb(ph, [128, 4, D], BF16)
                wO = P.sb(ph, [128, 8, D], BF16)
                for k in range(4):
                    P.dma('pool', wA[:, k, :], w_bra[l, k * 128:(k + 1) * 128, :], w=[('wA', k)])
                    P.dma('pool', wB[:, k, :], w_brb[l, k * 128:(k + 1) * 128, :], w=[('wB', k)])
                for k in range(8):
                    P.dma('pool', wO[:, k, :], w_out[l, k * 128:(k + 1) * 128, :], w=[('wO', k)])
                gsub = P.sb(ph, [128, 128], F32)
                P.dma('sp', gsub[:], g_sub[l:l + 1, :].to_broadcast([128, 128]), w=['gsub0'])
                P.ts('dve', gsub[:], gsub[:], float((1.0 - lam_init) * math.sqrt(128.0)), ALU.mult, r=['gsub0'], w=['gsub'])
                FM = P.sb(ph, [128, 16, L], BF16)
                vaA = P.sb(ph, [128, NT, 65], BF16)
                vbA = P.sb(ph, [128, NT, 4, 129], BF16)
                wiS = P.sb(ph, [128, NT, 4], F32)
                isc = P.sb(ph, [128, L], F32)
                junkD = P.sb(ph, [128, L], BF16)
                Mk = P.sb(ph, [128, L], BF16)
                MTs = [P.sb(ph, [128, NT, 128], BF16) for _ in range(2)]
                rlR = Rot(P, ph, 3, [128, 512], F32, 'rl')
                bis = P.sb(ph, [128, 8], F32)
                stepsX = P.sb(ph, [128, KBIS + 1], F32)
                ER = Rot(P, ph, 4, [128, 4, 128], BF16, 'E')
                PR = Rot(P, ph, 3, [128, 4, 128], BF16, 'Pm')
                rsA = P.sb(ph, [128, 8], F32)
                rsB = P.sb(ph, [128, 16], F32)
                sB2 = P.sb(ph, [128, 8], F32)
                oa_bf = P.sb(ph, [128, 512], BF16)
                ob_f = P.sb(ph, [128, 4, 128], F32)
                ob_t = P.sb(ph, [128, 4, 128], F32)
                ob_n = P.sb(ph, [128, 4, 128], F32)
                ob_sq = P.sb(ph, [128, 512], F32)
                ob_bf = P.sb(ph, [128, 512], BF16)
                oT = P.sb(ph, [128, 8, 128], BF16)
                m1 = P.sb(ph, [128, D], F32)
                m2 = P.sb(ph, [128, D], F32)
                mix_bf = P.sb(ph, [128, D], BF16)
                mT = P.sb(ph, [128, 8, 128], BF16)
                hR = Rot(P, ph, 2, [128, D], F32, 'h')
                h2R = Rot(P, ph, 2, [128, D], F32, 'h2')
                sgR = Rot(P, ph, 2, [128, 2048], BF16, 'sg')
                ps0 = P.ps(ph, [128, 512], F32)
                ps0b = ps0[:].bitcast(BF16).rearrange("p (b c) -> p b c", c=128)
                SR = Rot(P, ph, 3, [128, 512], F32, 'S', psum=True)
                ps_OA = P.ps(ph, [128, 512], F32)
                ps_OB = P.ps(ph, [128, 3, 512], F32)

                def oa_ap(h):
                    if h < 7:
                        return ps_OA[:, h * 65:h * 65 + 65]
                    return ps_OB[:, 2, 258:323]

                def ob_ap(q):
                    return ps_OB[:, q // 3, (q % 3) * 129:(q % 3) * 129 + 129]

                def nX(j):
                    return 4 * ((128 * (j + 1) + 511) // 512) + 3 + (KBIS if j >= KT else 0)

                def nY(j):
                    return 2 + 2 * (j + 1) + 5

                def X(s_i, j):
                    S = 128 * (j + 1)
                    tsl = slice(j * 128, (j + 1) * 128)
                    MT = MTs[j % 2]
                    MTk = ('MT', j % 2)
                    for c0 in range(0, S, 512):
                        cw = min(512, S - c0)
                        for hh in range(4):
                            r0 = 64 * (hh % 2)
                            P.mm(ps0[:, 0:cw], FM[r0:r0 + 64, 4 + hh // 2, tsl], FM[r0:r0 + 64, 6, c0:c0 + cw],
                                 True, True, r=[('FM', 4 + hh // 2), ('FM', 6)], w=['b0'])
                            rl, rlk = rlR.next()
                            P.act(rl[:, 0:cw], ps0[:, 0:cw], AF.Relu, r=['b0'], w=[rlk])
                            if hh == 0:
                                P.ts('pool', isc[:, c0:c0 + cw], rl[:, 0:cw], wiS[:, j, 0:1], ALU.mult,
                                     r=[rlk, 'wiS'], w=['isc'])
                            else:
                                P.ts('pool', rl[:, 0:cw], rl[:, 0:cw], wiS[:, j, hh:hh + 1], ALU.mult,
                                     r=[rlk, 'wiS'], w=[rlk])
                                P.tt('pool', isc[:, c0:c0 + cw], isc[:, c0:c0 + cw], rl[:, 0:cw], ALU.add,
                                     r=[rlk, 'isc'], w=['isc'])
                            yield
                    if j >= KT:
                        P.red(bis[:, 0:1], isc[:, 0:S], ALU.min, r=['isc'], w=['mn'])
                    P.tt('dve', isc[:, S - 128:S], isc[:, S - 128:S], negmask[:], ALU.add, r=['isc', 'negm', 'mn'], w=['isc'])
                    if j >= KT:
                        P.red(bis[:, 1:2], isc[:, 0:S], ALU.max, r=['isc'], w=['mx'])
                        P.tt('dve', bis[:, 2:3], bis[:, 1:2], bis[:, 0:1], ALU.subtract, r=['mn', 'mx'], w=['w0'])
                        P.ts('dve', stepsX[:], pow2[:], bis[:, 2:3], ALU.mult, r=['w0', 'pow2'], w=['steps'])
                        P.tt('dve', bis[:, 3:4], bis[:, 0:1], stepsX[:, 1:2], ALU.add, r=['mn', 'steps'], w=['mid'])
                        yield
                        for k in range(KBIS):
                            P.ts('dve', junkD[:, 0:S], isc[:, 0:S], bis[:, 3:4], ALU.is_ge, 0.0, ALU.add,
                                 accum=bis[:, 4:5], r=['isc', 'mid'], w=['cnt', 'junkD'])
                            P.ts('dve', bis[:, 5:6], bis[:, 4:5], float(KTOP) - 0.5, ALU.is_ge, -0.5, ALU.add,
                                 r=['cnt'], w=['sgn'])
                            P.stt('dve', bis[:, 3:4], bis[:, 5:6], stepsX[:, k + 1:k + 2], bis[:, 3:4], ALU.mult, ALU.add,
                                  r=['sgn', 'steps', 'mid'], w=['mid'])
                            yield
                        P.stt('dve', bis[:, 6:7], stepsX[:, KBIS:KBIS + 1], -0.5, bis[:, 3:4], ALU.mult, ALU.add,
                              r=['steps', 'mid'], w=['thr'])
                        P.ts('dve', Mk[:, 0:S], isc[:, 0:S], bis[:, 6:7], ALU.is_ge, r=['isc', 'thr'], w=['Mk'])
                    else:
                        yield
                        P.ts('dve', Mk[:, 0:S], isc[:, 0:S], -1.0e29, ALU.is_ge, r=['isc'], w=['Mk'])
                    yield
                    for i0 in range(0, j + 1, 8):
                        n8 = min(8, j + 1 - i0)
                        for ii in range(n8):
                            i = i0 + ii
                            P.tr(ps0b[:, ii, :], Mk[:, i * 128:(i + 1) * 128], ident[:], r=['Mk', 'ident'], w=['b0'])
                        P.cp('act', MT[:, i0:i0 + n8, :], ps0b[:, 0:n8, :], r=['b0'], w=[MTk])
                    yield

                def Y(s_i, j):
                    tt = s_i * NT + j
                    tsl = slice(j * 128, (j + 1) * 128)
                    MT = MTs[j % 2]
                    MTk = ('MT', j % 2)
                    sg, sgk = sgR.next()
                    P.dma('sp', sg[:], sg_s[tt * 128:(tt + 1) * 128, :], w=[sgk])
                    h_t, hk = hR.next()
                    P.dma('sp', h_t[:], hsrc[tt * 128:(tt + 1) * 128, :], w=[hk])
                    P.memset('dve', ps_OA[:], 0.0, w=['OA'])
                    P.memset('dve', ps_OB[:], 0.0, w=['OB', 'OB2'])
                    units = [(i, kind) for i in range(j + 1) for kind in range(4)]
                    N = len(units)
                    live = {}

                    def qk_exp(n):
                        i, kind = units[n]
                        ssl = slice(i * 128, (i + 1) * 128)
                        S_, Sk = SR.next()
                        if kind < 2:
                            r0 = 64 * kind
                            P.mm(S_[:].rearrange("p (a t) -> p a t", t=128), FM[r0:r0 + 64, 7, ssl], FM[r0:r0 + 64, 0:4, tsl],
                                 True, True, r=[('FM', 7), ('FM', 0), ('FM', 1), ('FM', 2), ('FM', 3)], w=[Sk])
                        else:
                            r0 = 64 * (kind - 2)
                            for hh in range(4):
                                P.mm(S_[:, hh * 128:(hh + 1) * 128], FM[r0:r0 + 64, 12 + hh, ssl], FM[r0:r0 + 64, 8 + hh, tsl],
                                     True, True, r=[('FM', 12 + hh), ('FM', 8 + hh)], w=[Sk])
                        E, Ek = ER.next()
                        P.act(E[:].rearrange("p a t -> p (a t)"), S_[:], AF.Exp, scale=0.125, r=[Sk], w=[Ek])
                        live[n] = (E, Ek)

                    def av(n):
                        i, kind = units[n]
                        E, Ek = live.pop(n)
                        if kind < 2:
                            Pm, Pk = PR.next()
                            P.tt('dve', Pm[:], E[:], bc_mid(MT[:, i, :], 4), ALU.mult, r=[Ek, MTk], w=[Pk])
                            for pr in range(4):
                                hh = 2 * pr + kind
                                P.mm(oa_ap(hh), Pm[:, pr, :], vaA[:, i, :], False, False, r=[Pk, 'vaA'],
                                     w=['OA', 'OB2'] if hh == 7 else ['OA'], acc0=True)
                        else:
                            m_ = kind - 2
                            if i == j:
                                P.tt('dve', E[:], E[:], bc_mid(caus01T[:], 4), ALU.mult, r=[Ek, 'caus'], w=[Ek])
                            for hh in range(4):
                                P.mm(ob_ap(2 * hh + m_), E[:, hh, :], vbA[:, i, hh, :], False, False, r=[Ek, 'vbA'],
                                     w=['OB', 'OB2'] if hh == 3 else ['OB'], acc0=True)

                    qk_exp(0)
                    qk_exp(1)
                    yield
                    for n in range(N):
                        if n + 2 < N:
                            qk_exp(n + 2)
                        av(n)
                        if n % 2 == 1:
                            yield
                    v7 = ps_OA[:, 0:455].rearrange("p (h c) -> p h c", c=65)
                    P.recip(rsA[:, 0:7], v7[:, :, 64], r=['OA'], w=['rsA'])
                    P.recip(rsA[:, 7:8], ps_OB[:, 2, 322:323], r=['OA', 'OB2'], w=['rsA'])
                    P.tt('dve', oa_bf[:, 0:448].rearrange("p (h d) -> p h d", d=64), v7[:, :, 0:64], bc_last(rsA[:, 0:7], 64),
                         ALU.mult, r=['OA', 'rsA'], w=['oa'])
                    P.ts('dve', oa_bf[:, 448:512], ps_OB[:, 2, 258:322], rsA[:, 7:8], ALU.mult, r=['OA', 'OB2', 'rsA'], w=['oa'])
                    yield
                    for b in range(3):
                        nq = 3 if b < 2 else 2
                        vq = ps_OB[:, b, 0:nq * 129].rearrange("p (q c) -> p q c", c=129)
                        P.recip(rsB[:, 3 * b:3 * b + nq], vq[:, :, 128], r=['OB', 'OB2'], w=['rsB'])
                    P.ts('dve', rsB[:, 8:16], rsB[:, 0:8], nlam_sb[:, l:l + 1], ALU.mult, r=['rsB', ('nlam', l)], w=['rsB2'])
                    for hh in range(4):
                        P.ts('dve', ob_t[:, hh, :], ob_ap(2 * hh)[:, 0:128], rsB[:, 2 * hh:2 * hh + 1], ALU.mult,
                             r=['OB', 'OB2', 'rsB'], w=['ob_t'])
                        P.stt('dve', ob_f[:, hh, :], ob_ap(2 * hh + 1)[:, 0:128], rsB[:, 8 + 2 * hh + 1:8 + 2 * hh + 2],
                              ob_t[:, hh, :], ALU.mult, ALU.add, r=['OB', 'OB2', 'rsB2', 'ob_t'], w=['ob_f'])
                    P.act(ob_sq[:], ob_f[:].rearrange("p h d -> p (h d)"), AF.Square, r=['ob_f'], w=['ob_sq'])
                    P.red(sB2[:, 0:4], ob_sq[:].rearrange("p (h d) -> p h d", d=128), ALU.add, r=['ob_sq'], w=['ssb'])
                    P.rsqrt(sB2[:, 4:8], sB2[:, 0:4], 128.0 * EPS, r=['ssb'], w=['rsb'])
                    P.tt('dve', ob_n[:], ob_f[:], bc_last(sB2[:, 4:8], 128), ALU.mult, r=['ob_f', 'rsb'], w=['ob_n'])
                    P.tt('pool', ob_bf[:].rearrange("p (h d) -> p h d", d=128), ob_n[:], bc_mid(gsub[:], 4), ALU.mult,
                         r=['ob_n', 'gsub'], w=['ob'])
                    yield
                    for b in range(4):
                        P.tr(ps0b[:, b, :], oa_bf[:, b * 128:(b + 1) * 128], ident[:], r=['oa', 'ident'], w=['b0'])
                        P.tr(ps0b[:, 4 + b, :], ob_bf[:, b * 128:(b + 1) * 128], ident[:], r=['ob', 'ident'], w=['b0'])
                    P.cp('act', oT[:], ps0b, r=['b0'], w=['oT'])
                    yield
                    for n2 in range(2):
                        nsl = slice(n2 * 512, (n2 + 1) * 512)
                        S_, Sk = SR.next()
                        for k in range(4):
                            P.mm(S_[:], oT[:, k, :], wA[:, k, nsl], k == 0, k == 3, r=['oT', ('wA', k)], w=[Sk])
                        P.tt('dve', m1[:, nsl], S_[:], sg[:, nsl], ALU.mult, r=[Sk, sgk], w=['m1'])
                    for n2 in range(2):
                        nsl = slice(n2 * 512, (n2 + 1) * 512)
                        S_, Sk = SR.next()
                        for k in range(4):
                            P.mm(S_[:], oT[:, 4 + k, :], wB[:, k, nsl], k == 0, k == 3, r=['oT', ('wB', k)], w=[Sk])
                        P.tt('dve', m2[:, nsl], S_[:], sg[:, 1024 + n2 * 512:1024 + (n2 + 1) * 512], ALU.mult,
                             r=[Sk, sgk], w=['m2'])
                    P.tt('pool', mix_bf[:], m1[:], m2[:], ALU.add, r=['m1', 'm2'], w=['mix'])
                    yield
                    for k in range(8):
                        P.tr(ps0b[:, k, :], mix_bf[:, k * 128:(k + 1) * 128], ident[:], r=['mix', 'ident'], w=['b0'])
                    P.cp('act', mT[:], ps0b, r=['b0'], w=['mT'])
                    h2, h2k = h2R.next()
                    for n2 in range(2):
                        nsl = slice(n2 * 512, (n2 + 1) * 512)
                        S_, Sk = SR.next()
                        for k in range(8):
                            P.mm(S_[:], mT[:, k, :], wO[:, k, nsl], k == 0, k == 7, r=['mT', ('wO', k)], w=[Sk])
                        P.tt('dve', h2[:, nsl], S_[:], h_t[:, nsl], ALU.add, r=[Sk, hk], w=[h2k])
                    P.dma('sp', h_s[tt * 128:(tt + 1) * 128, :], h2[:], r=[h2k])
                    yield

                def interleave(ga, na, gb, nb):
                    ia = ib = 0
                    while ga is not None or gb is not None:
                        pick_a = gb is None or (ga is not None and ia * nb <= ib * na)
                        if pick_a:
                            try:
                                next(ga)
                                ia += 1
                            except StopIteration:
                                ga = None
                        else:
                            try:
                                next(gb)
                                ib += 1
                            except StopIteration:
                                gb = None

                for s_i in range(NSEQ):
                    for b in range(16):
                        P.dma('sp', FM[:, b, :], fm_s[s_i, :, b, :], w=[('FM', b)])
                    P.memset('pool', vaA[:], 1.0, w=['vaA'])
                    P.memset('pool', vbA[:], 1.0, w=['vbA'])
                    P.dma('sp', vaA[:, :, 0:64], va_s[s_i].rearrange("n p d -> p n d"), w=['vaA'])
                    for hh in range(4):
                        P.dma('sp', vbA[:, :, hh, 0:128], vb_s[s_i, :, :, hh * 128:(hh + 1) * 128].rearrange("n p d -> p n d"),
                              w=['vbA'])
                    P.dma('sp', wiS[:], wi_s[s_i].rearrange("n p d -> p n d"), w=['wiS'])
                    for _ in X(s_i, 0):
                        pass
                    for j in range(NT):
                        gx = X(s_i, j + 1) if j + 1 < NT else None
                        interleave(Y(s_i, j), nY(j), gx, nX(j + 1) if j + 1 < NT else 1)
                P.barrier()
                P.emit()
            if debug == 2:
                break
            with ExitStack() as ph:
                wU = P.sb(ph, [128, 8, 2 * DFF], BF16)
                for k in range(8):
                    P.dma('pool', wU[:, k, :], w_up[l, k * 128:(k + 1) * 128, :], w=[('wU', k)])
                gs_bc = P.sb(ph, [128, D], F32)
                P.dma('sp', gs_bc[:], g_ffn[l:l + 1, :].to_broadcast([128, D]), w=['gs0'])
                P.ts('dve', gs_bc[:], gs_bc[:], float(math.sqrt(D)), ALU.mult, r=['gs0'], w=['gs'])
                cwr = P.sb(ph, [44, 4, 128], F32)
                for jj in range(3):
                    P.dma('sp', cwr[:, jj, :], conv_w[l, jj, :].rearrange("(c p) -> c p", p=128), w=['cwr'])
                P.dma('sp', cwr[:, 3, :], conv_b[l, :].rearrange("(c p) -> c p", p=128), w=['cwr'])
                cw = P.sb(ph, [128, 4, 44], F32)
                ps_c = P.ps(ph, [128, 4, 44], F32)
                for jj in range(4):
                    P.tr(ps_c[:, jj, :], cwr[:, jj, :], identf[0:44, 0:44], r=['cwr', 'identf'], w=['ps_c'])
                P.cp('dve', cw[:], ps_c[:], r=['ps_c'], w=['cw'])
                halo = P.sb(ph, [128, 44, 2], F32)
                hG = Rot(P, ph, 1, [128, 4, D], F32, 'hG')
                uG = Rot(P, ph, 1, [128, 4, D], BF16, 'uG')
                uTG = Rot(P, ph, 2, [128, 8, 512], BF16, 'uTG')
                ssR = Rot(P, ph, 2, [128, 8], F32, 'ss')
                junkA = P.sb(ph, [128, D], BF16)
                xsR = Rot(P, ph, 3, [128, 514], F32, 'xs')
                acR = Rot(P, ph, 3, [128, 512], F32, 'acc')
                glR = Rot(P, ph, 2, [128, 512], F32, 'gl')
                prR = Rot(P, ph, 2, [128, NFC, 512], BF16, 'prod')
                ptrR = Rot(P, ph, 2, [128, 8, 128], BF16, 'ptr', psum=True)
                ppR = Rot(P, ph, 4, [128, 512], F32, 'pp', psum=True)
                for g in range(NG):
                    if (g * 512) % L == 0:
                        P.memset('dve', halo[:], 0.0, w=['halo'])
                    hg, hgk = hG.next()
                    P.dma('sp', hg[:], h_s[g * 512:(g + 1) * 512, :].rearrange("(a p) d -> p a d", p=128), w=[hgk])
                    ss, ssk = ssR.next()
                    for a in range(4):
                        P.act(junkA[:], hg[:, a, :], AF.Square, accum=ss[:, a:a + 1], r=[hgk], w=[ssk, 'junkA'])
                    P.rsqrt(ss[:, 4:8], ss[:, 0:4], D * EPS, r=[ssk], w=[ssk])
                    ug, ugk = uG.next()
                    uT, uTk = uTG.next()
                    for a in range(4):
                        P.stt('dve', ug[:, a, :], hg[:, a, :], ss[:, 4 + a:5 + a], gs_bc[:], ALU.mult, ALU.mult,
                              r=[hgk, ssk, 'gs'], w=[ugk])
                        ptr, pk = ptrR.next()
                        for k in range(8):
                            P.tr(ptr[:, k, :], ug[:, a, k * 128:(k + 1) * 128], ident[:], r=[ugk, 'ident'], w=[pk])
                        P.cp('act', uT[:, :, a * 128:(a + 1) * 128], ptr[:], r=[pk], w=[uTk])
                    prod, prk = prR.next()
                    for q in range(NFC):
                        res = []
                        for fc in (q, q + NFC):
                            pp, ppk = ppR.next()
                            for k in range(8):
                                P.mm(pp[:], wU[:, k, fc * 128:(fc + 1) * 128], uT[:, k, :], k == 0, k == 7,
                                     r=[uTk, ('wU', k)], w=[ppk])
                            xs, xsk = xsR.next()
                            P.cp('act', xs[:, 2:514], pp[:], r=[ppk], w=[xsk])
                            P.cp('pool', xs[:, 0:2], halo[:, fc, :], r=['halo'], w=[xsk])
                            acc, ack = acR.next()
                            P.act(acc[:], pp[:], AF.Identity, bias=cw[:, 3, fc:fc + 1], scale=cw[:, 2, fc:fc + 1],
                                  r=[ppk, 'cw'], w=[ack])
                            P.stt('dve', acc[:], xs[:, 1:513], cw[:, 1, fc:fc + 1], acc[:], ALU.mult, ALU.add,
                                  r=[xsk, 'cw', ack], w=[ack])
                            P.stt('dve', acc[:], xs[:, 0:512], cw[:, 0, fc:fc + 1], acc[:], ALU.mult, ALU.add,
                                  r=[xsk, 'cw', ack], w=[ack])
                            P.cp('pool', halo[:, fc, :], xs[:, 512:514], r=[xsk], w=['halo'])
                            res.append((acc, ack))
                        gl, glk = glR.next()
                        P.act(gl[:], res[0][0][:], AF.Gelu_apprx_tanh, r=[res[0][1]], w=[glk])
                        P.tt('dve', prod[:, q, :], gl[:], res[1][0][:], ALU.mult, r=[glk, res[1][1]], w=[prk])
                    P.dma('sp', pr_s[:, :, g * 512:(g + 1) * 512], prod[:], r=[prk])
                P.barrier()
                P.emit()
            if debug == 3:
                break
            with ExitStack() as ph:
                wD = P.sb(ph, [128, NFC, D], BF16)
                for fc in range(NFC):
                    P.dma('pool', wD[:, fc, :], w_down[l, fc * 128:(fc + 1) * 128, :], w=[('wD', fc)])
                wG = P.sb(ph, [128, 8, D], BF16)
                for k in range(8):
                    P.dma('pool', wG[:, k, :], w_pg[l, k * 128:(k + 1) * 128, :], w=[('wG', k)])
                wPp = P.sb(ph, [128, 2, D], BF16)
                for k in range(2):
                    P.dma('pool', wPp[:, k, :], w_pp[l, k * 128:(k + 1) * 128, :], w=[('wPp', k)])
                gs_bc = P.sb(ph, [128, D], F32)
                P.dma('sp', gs_bc[:], g_ple[l:l + 1, :].to_broadcast([128, D]), w=['gs0'])
                P.ts('dve', gs_bc[:], gs_bc[:], float(math.sqrt(D)), ALU.mult, r=['gs0'], w=['gs'])
                hR = Rot(P, ph, 2, [128, D], F32, 'h')
                h3R = Rot(P, ph, 2, [128, D], F32, 'h3')
                h4R = Rot(P, ph, 2, [128, D], F32, 'h4')
                prR = Rot(P, ph, 2, [128, NFC, 128], BF16, 'pr')
                pbR = Rot(P, ph, 2, [128, PLE], BF16, 'pb')
                pTR = Rot(P, ph, 2, [128, 2, 128], BF16, 'pT')
                ssR = Rot(P, ph, 2, [128, 2], F32, 'ss')
                uR = Rot(P, ph, 2, [128, D], BF16, 'u')
                uTR = Rot(P, ph, 2, [128, 8, 128], BF16, 'uT')
                sgR = Rot(P, ph, 2, [128, D], F32, 'sgt')
                tmR = Rot(P, ph, 2, [128, D], F32, 'tm')
                junkA = P.sb(ph, [128, D], BF16)
                psD = P.ps(ph, [128, 2, 512], F32)
                psG = P.ps(ph, [128, 2, 512], F32)
                psP = P.ps(ph, [128, 2, 512], F32)
                ptrR = Rot(P, ph, 2, [128, 8, 128], BF16, 'ptr', psum=True)
                hdst = y if l == DEPTH - 1 else h_s
                for tt in range(NTT):
                    rows = slice(tt * 128, (tt + 1) * 128)
                    h_t, hk = hR.next()
                    P.dma('sp', h_t[:], h_s[rows, :], w=[hk])
                    pr, prk = prR.next()
                    P.dma('sp', pr[:], pr_s[:, :, rows], w=[prk])
                    pb, pbk = pbR.next()
                    P.dma('pool', pb[:], p_in[l, rows, :], w=[pbk])
                    h3, h3k = h3R.next()
                    for n2 in range(2):
                        nsl = slice(n2 * 512, (n2 + 1) * 512)
                        for fc in range(NFC):
                            P.mm(psD[:, n2, :], pr[:, fc, :], wD[:, fc, nsl], fc == 0, fc == NFC - 1,
                                 r=[prk, ('wD', fc)], w=[('psD', n2)])
                        P.tt('dve', h3[:, nsl], psD[:, n2, :], h_t[:, nsl], ALU.add, r=[('psD', n2), hk], w=[h3k])
                    ss, ssk = ssR.next()
                    P.act(junkA[:], h3[:], AF.Square, accum=ss[:, 0:1], r=[h3k], w=[ssk, 'junkA'])
                    P.rsqrt(ss[:, 1:2], ss[:, 0:1], D * EPS, r=[ssk], w=[ssk])
                    u, uk = uR.next()
                    P.stt('dve', u[:], h3[:], ss[:, 1:2], gs_bc[:], ALU.mult, ALU.mult, r=[h3k, ssk, 'gs'], w=[uk])
                    ptr, pk = ptrR.next()
                    for k in range(8):
                        P.tr(ptr[:, k, :], u[:, k * 128:(k + 1) * 128], ident[:], r=[uk, 'ident'], w=[pk])
                    uT, uTk = uTR.next()
                    P.cp('act', uT[:], ptr[:], r=[pk], w=[uTk])
                    ptr, pk = ptrR.next()
                    for k in range(2):
                        P.tr(ptr[:, k, :], pb[:, k * 128:(k + 1) * 128], ident[:], r=[pbk, 'ident'], w=[pk])
                    pT, pTk = pTR.next()
                    P.cp('act', pT[:], ptr[:, 0:2, :], r=[pk], w=[pTk])
                    sgt, sgk = sgR.next()
                    tm, tmk = tmR.next()
                    h4, h4k = h4R.next()
                    for n2 in range(2):
                        nsl = slice(n2 * 512, (n2 + 1) * 512)
                        for k in range(8):
                            P.mm(psG[:, n2, :], uT[:, k, :], wG[:, k, nsl], k == 0, k == 7, r=[uTk, ('wG', k)], w=[('psG', n2)])
                        P.act(sgt[:, nsl], psG[:, n2, :], AF.Sigmoid, r=[('psG', n2)], w=[sgk])
                        for k in range(2):
                            P.mm(psP[:, n2, :], pT[:, k, :], wPp[:, k, nsl], k == 0, k == 1, r=[pTk, ('wPp', k)], w=[('psP', n2)])
                        P.tt('dve', tm[:, nsl], psP[:, n2, :], sgt[:, nsl], ALU.mult, r=[('psP', n2), sgk], w=[tmk])
                        P.tt('dve', h4[:, nsl], tm[:, nsl], h3[:, nsl], ALU.add, r=[tmk, h3k], w=[h4k])
                    P.dma('sp', hdst[rows, :], h4[:], r=[h4k])
                P.barrier()
                P.emit()
    return nc


def rope_table(L):
    NT = L // 128
    inv = 1.0 / (10000.0 ** (np.arange(0, 64, 2, dtype=np.float32) / np.float32(64.0)))
    ang = np.arange(L, dtype=np.float32)[:, None] * inv[None, :].astype(np.float32)
    cs = np.concatenate([np.cos(ang), np.sin(ang)], axis=1).astype(np.float32)
    return np.ascontiguousarray(cs.reshape(NT, 128, 64).transpose(1, 0, 2))


_CACHE = {}


def run(inputs, L, NSEQ, DEPTH, ncores, debug=False):
    key = (L, NSEQ, DEPTH, debug)
    if key not in _CACHE:
        _CACHE[key] = build(L, NSEQ, DEPTH, debug)
    nc = _CACHE[key]
    T = NSEQ * L
    xs = np.ascontiguousarray(inputs['x'], dtype=np.float32).reshape(ncores, T, D)
    ps = np.ascontiguousarray(inputs['p'], dtype=np.float32).reshape(DEPTH, ncores, T, PLE)
    cs = rope_table(L)
    in_maps = []
    for c in range(ncores):
        m = {k: np.ascontiguousarray(v, dtype=np.float32) for k, v in inputs.items() if k not in ('x', 'p')}
        m['x'] = xs[c]
        m['p'] = np.ascontiguousarray(ps[:, c])
        m['cs_tab'] = cs
        in_maps.append(m)
    res = run_bass_kernel_spmd(nc, in_maps, core_ids=list(range(ncores)))
    return res


def kernel(**inputs):
    B, L, _ = inputs['x'].shape
    DEPTH = inputs['p'].shape[0]
    ncores = 8
    NSEQ = B // ncores
    res = run(inputs, L, NSEQ, DEPTH, ncores)
    out = np.stack([np.asarray(r['y'], dtype=np.float32) for r in res.results], axis=0)
    return out.reshape(B, L, D)
```

```python
import math
import numpy as np
from contextlib import ExitStack
import concourse.bass as bass
import concourse.mybir as mybir
from concourse.bass_utils import run_bass_kernel_spmd

F32 = mybir.dt.float32
BF16 = mybir.dt.bfloat16
AF = mybir.ActivationFunctionType
ALU = mybir.AluOpType
AX = mybir.AxisListType

D = 1024
NIN = 4548
DFF = 2816
NFC = 22
PLE = 256
EPS = 1e-6
KBIS = 13
NEG = -1.0e30
COMPUTE = ('pe', 'act', 'dve', 'pool')
ENGS = ('sp', 'pool', 'act', 'dve', 'pe')


class Op(object):
    __slots__ = ('eng', 'fn', 'deps', 'dma', 'sem', 'semval', 'ms', 'waited')


class Prog(object):
    def __init__(self, nc, es, n_sp=16, n_pq=8):
        self.nc = nc
        self.esem = {e: es.enter_context(nc.semaphore('sem_' + e)) for e in COMPUTE}
        self.dsem = {'sp': [es.enter_context(nc.semaphore('dsp%d' % i)) for i in range(n_sp)],
                     'pool': [es.enter_context(nc.semaphore('dpq%d' % i)) for i in range(n_pq)]}
        self.duse = {q: [0] * len(v) for q, v in self.dsem.items()}
        self.dlast = {q: [None] * len(v) for q, v in self.dsem.items()}
        self.dnext = {q: 0 for q in self.dsem}
        self.mscount = {e: 0 for e in COMPUTE}
        self.waitedv = {e: {} for e in ENGS}
        self.state = {}
        self.ops = {e: [] for e in ENGS}
        self.last = {e: None for e in ENGS}
        self.cnt = 0

    def sb(self, ctx, shape, dt):
        self.cnt += 1
        return ctx.enter_context(self.nc.sbuf_tensor('sb%d' % self.cnt, list(shape), dt))

    def ps(self, ctx, shape, dt):
        self.cnt += 1
        return ctx.enter_context(self.nc.psum_tensor('ps%d' % self.cnt, list(shape), dt))

    def op(self, eng, fn, r=(), w=(), dma=False):
        o = Op()
        o.eng = eng; o.fn = fn; o.dma = dma; o.waited = False; o.ms = 0; o.sem = None; o.semval = 0
        deps = {}
        for k in r:
            st = self.state.get(k)
            if st is not None and st[0] is not None:
                deps[st[0]] = 'raw'
        for k in w:
            st = self.state.get(k)
            if st is not None:
                if st[0] is not None:
                    deps.setdefault(st[0], 'waw')
                for ro in st[1].values():
                    deps.setdefault(ro, 'war')
                for ro in st[2]:
                    deps.setdefault(ro, 'war')
        if dma:
            n = len(self.dsem[eng])
            slot = self.dnext[eng]
            self.dnext[eng] = (slot + 1) % n
            prev = self.dlast[eng][slot]
            if prev is not None:
                deps.setdefault(prev, 'slot')
            self.duse[eng][slot] += 1
            o.sem = self.dsem[eng][slot]
            o.semval = 16 * self.duse[eng][slot]
            self.dlast[eng][slot] = o
        fd = []
        for d, kind in deps.items():
            if d is o:
                continue
            if (not d.dma) and (not dma) and d.eng == eng:
                if eng == 'pe':
                    continue
            fd.append(d)
            d.waited = True
        o.deps = fd
        for k in r:
            st = self.state.setdefault(k, [None, {}, []])
            if dma:
                st[2].append(o)
            else:
                st[1][eng] = o
        for k in w:
            self.state[k] = [o, {}, []]
        self.ops[eng].append(o)
        if not dma:
            self.last[eng] = o
        return o

    def barrier(self):
        targets = [self.last[e] for e in COMPUTE if self.last[e] is not None]
        for q in self.dlast:
            targets += [d for d in self.dlast[q] if d is not None]
        for e in ENGS:
            o = Op()
            o.eng = e; o.fn = None; o.dma = False; o.waited = False; o.ms = 0; o.sem = None; o.semval = 0
            o.deps = [t for t in targets if t.dma or t.eng != e]
            for t in o.deps:
                t.waited = True
            self.ops[e].append(o)
        self.state = {}

    def emit(self):
        for e in COMPUTE:
            for o in self.ops[e]:
                if o.fn is not None and (not o.dma) and o.waited:
                    self.mscount[e] += 1
                    o.ms = self.mscount[e]
        nc = self.nc
        with nc.Block() as block:
            decos = {'sp': block.sync, 'pool': block.gpsimd, 'act': block.scalar, 'dve': block.vector,
                     'pe': block.tensor}
            for e in ENGS:
                ops = self.ops[e]
                if not ops:
                    continue

                def body(eo, ops=ops, e=e):
                    wt = self.waitedv[e]
                    for o in ops:
                        for d in o.deps:
                            if d.dma:
                                sem, val = d.sem, d.semval
                            else:
                                sem, val = self.esem[d.eng], d.ms
                            key = id(sem)
                            if wt.get(key, 0) < val:
                                eo.wait_ge(sem, val)
                                wt[key] = val
                        if o.fn is None:
                            continue
                        ins = o.fn(eo)
                        if o.dma:
                            ins.then_inc(o.sem, 16)
                        elif o.waited:
                            ins.then_inc(self.esem[e], 1)
                decos[e](body)
        self.ops = {e: [] for e in ENGS}

    def dma(self, q, out, in_, r=(), w=()):
        return self.op(q, lambda e: e.dma_start(out=out, in_=in_), r, w, dma=True)

    def tt(self, eng, out, in0, in1, op, r=(), w=()):
        return self.op(eng, lambda e: e.tensor_tensor(out=out, in0=in0, in1=in1, op=op), r, w)

    def ts(self, eng, out, in0, s1, op0, s2=None, op1=None, accum=None, r=(), w=()):
        if op1 is None:
            return self.op(eng, lambda e: e.tensor_scalar(out=out, in0=in0, scalar1=s1, scalar2=None, op0=op0), r, w)
        if accum is None:
            return self.op(eng, lambda e: e.tensor_scalar(out=out, in0=in0, scalar1=s1, scalar2=s2, op0=op0,
                                                           op1=op1), r, w)
        return self.op(eng, lambda e: e.tensor_scalar(out=out, in0=in0, scalar1=s1, scalar2=s2, op0=op0,
                                                       op1=op1, accum_out=accum), r, w)

    def stt(self, eng, out, in0, scalar, in1, op0, op1, r=(), w=()):
        return self.op(eng, lambda e: e.scalar_tensor_tensor(out=out, in0=in0, scalar=scalar, in1=in1,
                                                              op0=op0, op1=op1), r, w)

    def act(self, out, in_, func, bias=None, scale=None, accum=None, r=(), w=()):
        kw = {}
        if bias is not None:
            kw['bias'] = bias
        if scale is not None:
            kw['scale'] = scale
        if accum is not None:
            kw['accum_out'] = accum
        return self.op('act', lambda e: e.activation(out=out, in_=in_, func=func, **kw), r, w)

    def cp(self, eng, out, in_, r=(), w=()):
        if eng == 'act':
            return self.op('act', lambda e: e.copy(out=out, in_=in_), r, w)
        return self.op(eng, lambda e: e.tensor_copy(out=out, in_=in_), r, w)

    def mm(self, out, lhsT, rhs, start, stop, r=(), w=(), acc0=False, first=False):
        if acc0:
            return self.op('pe', lambda e: e.matmul(out, lhsT=lhsT, rhs=rhs, start=first, stop=False,
                                                    skip_group_check=True), r, w)
        return self.op('pe', lambda e: e.matmul(out, lhsT=lhsT, rhs=rhs, start=start, stop=stop), r, w)

    def tr(self, out, in_, ident, r=(), w=()):
        return self.op('pe', lambda e: e.transpose(out=out, in_=in_, identity=ident), r, w)

    def red(self, out, in_, op, r=(), w=()):
        return self.op('dve', lambda e: e.tensor_reduce(out=out, in_=in_, axis=AX.X, op=op), r, w)

    def recip(self, out, in_, r=(), w=()):
        return self.op('dve', lambda e: e.reciprocal(out=out, in_=in_), r, w)

    def rsqrt(self, out, in_, addc, r=(), w=()):
        self.op('act', lambda e: e.activation(out=out, in_=in_, func=AF.Sqrt, bias=addc, scale=1.0), r, w)
        return self.op('dve', lambda e: e.reciprocal(out=out, in_=out), w, w)

    def memset(self, eng, ap, val, r=(), w=()):
        return self.op(eng, lambda e: e.memset(ap, val), r, w)


class Rot(object):
    def __init__(self, P, ctx, n, shape, dt, name, psum=False):
        self.bufs = [(P.ps if psum else P.sb)(ctx, shape, dt) for _ in range(n)]
        self.name = name
        self.i = -1

    def next(self):
        self.i = (self.i + 1) % len(self.bufs)
        return self.bufs[self.i], (self.name, self.i)


def run_pipelined(makers, depth):
    active = []
    idx = 0
    while idx < len(makers) or active:
        while len(active) < depth and idx < len(makers):
            active.append(makers[idx]())
            idx += 1
        for g in list(active):
            try:
                next(g)
            except StopIteration:
                active.remove(g)


def bc_mid(ap, n):
    return ap.unsqueeze(1).to_broadcast([ap.shape[0], n, ap.shape[1]])


def bc_last(ap, n):
    return ap.unsqueeze(2).to_broadcast([ap.shape[0], ap.shape[1], n])


def build(L, NSEQ, DEPTH, debug=False):
    NT = L // 128
    T = NSEQ * L
    NTT = T // 128
    KTOP = min(256, L // 4)
    KT = KTOP // 128
    NG = T // 512
    nc = bass.Bass("TRN2", target_bir_lowering=False)

    def din(name, shape, dt=F32):
        return nc.dram_tensor(name, list(shape), dt, kind="ExternalInput").ap()

    x = din("x", [T, D])
    p_in = din("p", [DEPTH, T, PLE])
    g_mix = din("g_mix_norm", [DEPTH, D])
    w_in = din("w_in", [DEPTH, D, NIN])
    g_q = {f: din(f, [DEPTH, 64]) for f in ("g_qa", "g_ka", "g_qb", "g_kb")}
    lamv = {f: din(f, [DEPTH, 64]) for f in ("lam_q1", "lam_k1", "lam_q2", "lam_k2")}
    g_sub = din("g_subln", [DEPTH, 128])
    w_bra = din("w_branch_a", [DEPTH, 512, D])
    w_brb = din("w_branch_b", [DEPTH, 512, D])
    w_out = din("w_out", [DEPTH, D, D])
    g_ffn = din("g_ffn_norm", [DEPTH, D])
    w_up = din("w_up", [DEPTH, D, 2 * DFF])
    conv_w = din("conv_w", [DEPTH, 3, 2 * DFF])
    conv_b = din("conv_b", [DEPTH, 2 * DFF])
    w_down = din("w_down", [DEPTH, DFF, D])
    g_ple = din("g_ple_norm", [DEPTH, D])
    w_pg = din("w_ple_gate", [DEPTH, D, D])
    w_pp = din("w_ple_proj", [DEPTH, PLE, D])
    cs_in = din("cs_tab", [128, NT, 64])
    y = nc.dram_tensor("y", [T, D], F32, kind="ExternalOutput").ap()

    skind = "ExternalOutput" if debug else "Internal"

    def dscr(name, shape, dt):
        return nc.dram_tensor(name, list(shape), dt, kind=skind).ap()

    h_s = dscr("h_s", [T, D], F32)
    fm_s = dscr("fm_s", [NSEQ, 128, 16, L], BF16)
    va_s = dscr("va_s", [NSEQ, NT, 128, 64], BF16)
    vb_s = dscr("vb_s", [NSEQ, NT, 128, 512], BF16)
    wi_s = dscr("wi_s", [NSEQ, NT, 128, 4], F32)
    sg_s = dscr("sg_s", [T, 2048], BF16)
    pr_s = dscr("pr_s", [128, NFC, T], BF16)

    with ExitStack() as es:
        P = Prog(nc, es)
        ident = P.sb(es, [128, 128], BF16)
        identf = P.sb(es, [128, 128], F32)
        caus01T = P.sb(es, [128, 128], BF16)
        negmask = P.sb(es, [128, 128], F32)
        cs_sb = P.sb(es, [128, NT, 64], F32)
        pow2 = P.sb(es, [128, KBIS + 1], F32)
        lam_sb = P.sb(es, [128, DEPTH], F32)
        nlam_sb = P.sb(es, [128, DEPTH], F32)
        ones_bf = P.sb(es, [128, 128], BF16)
        zer_f = P.sb(es, [128, 128], F32)

        with ExitStack() as ph:
            P.memset('pool', ident[:], 0.0, w=['ident'])
            P.op('pool', lambda e: e.affine_select(out=ident[:], in_=ident[:], pattern=[[-1, 128]],
                                                   compare_op=ALU.not_equal, fill=1.0, base=0,
                                                   channel_multiplier=1), r=['ident'], w=['ident'])
            P.memset('pool', identf[:], 0.0, w=['identf'])
            P.op('pool', lambda e: e.affine_select(out=identf[:], in_=identf[:], pattern=[[-1, 128]],
                                                   compare_op=ALU.not_equal, fill=1.0, base=0,
                                                   channel_multiplier=1), r=['identf'], w=['identf'])
            P.memset('pool', ones_bf[:], 1.0, w=['ones'])
            P.memset('pool', zer_f[:], 0.0, w=['zer'])
            P.op('pool', lambda e: e.affine_select(out=caus01T[:], in_=ones_bf[:], pattern=[[1, 128]],
                                                   compare_op=ALU.is_ge, fill=0.0, base=0,
                                                   channel_multiplier=-1), r=['ones'], w=['caus'])
            P.op('pool', lambda e: e.affine_select(out=negmask[:], in_=zer_f[:], pattern=[[-1, 128]],
                                                   compare_op=ALU.is_ge, fill=NEG, base=0,
                                                   channel_multiplier=1), r=['zer'], w=['negm'])
            P.dma('sp', cs_sb[:], cs_in[:, :, :], w=['cs'])
            for k in range(KBIS + 1):
                P.memset('dve', pow2[:, k:k + 1], float(2.0 ** (-k)), w=['pow2'])
            lt = {f: P.sb(ph, [128, 64], F32) for f in lamv}
            ltmp = P.sb(ph, [128, 64], F32)
            ld = P.sb(ph, [128, 4], F32)
            for l in range(DEPTH):
                lam_init = 0.8 - 0.6 * math.exp(-0.3 * l)
                for f in lamv:
                    P.dma('sp', lt[f][:], lamv[f][l:l + 1, :].to_broadcast([128, 64]), w=[('lt', f)])
                P.tt('dve', ltmp[:], lt['lam_q1'][:], lt['lam_k1'][:], ALU.mult, r=[('lt', 'lam_q1'), ('lt', 'lam_k1')], w=['ltmp'])
                P.red(ld[:, 0:1], ltmp[:], ALU.add, r=['ltmp'], w=['ld0'])
                P.tt('dve', ltmp[:], lt['lam_q2'][:], lt['lam_k2'][:], ALU.mult, r=[('lt', 'lam_q2'), ('lt', 'lam_k2')], w=['ltmp'])
                P.red(ld[:, 1:2], ltmp[:], ALU.add, r=['ltmp'], w=['ld1'])
                P.act(ld[:, 2:4], ld[:, 0:2], AF.Exp, r=['ld0', 'ld1'], w=['ld23'])
                P.stt('dve', lam_sb[:, l:l + 1], ld[:, 2:3], lam_init, ld[:, 3:4], ALU.add, ALU.subtract,
                      r=['ld23'], w=[('lam', l)])
                P.ts('dve', nlam_sb[:, l:l + 1], lam_sb[:, l:l + 1], -1.0, ALU.mult, r=[('lam', l)], w=[('nlam', l)])
            P.barrier()
            P.emit()

        for l in range(DEPTH):
            lam_init = 0.8 - 0.6 * math.exp(-0.3 * l)
            hsrc = x if l == 0 else h_s
            with ExitStack() as ph:
                w_sb = P.sb(ph, [128, 8, NIN], BF16)
                for k in range(8):
                    P.dma('pool', w_sb[:, k, :], w_in[l, k * 128:(k + 1) * 128, :], w=[('w', k)])
                gs_bc = P.sb(ph, [128, D], F32)
                P.dma('sp', gs_bc[:], g_mix[l:l + 1, :].to_broadcast([128, D]), w=['gs0'])
                P.ts('dve', gs_bc[:], gs_bc[:], float(math.sqrt(D)), ALU.mult, r=['gs0'], w=['gs'])
                tabs = {}
                for f in g_q:
                    g8 = P.sb(ph, [128, 64], F32)
                    P.dma('sp', g8[:], g_q[f][l:l + 1, :].to_broadcast([128, 64]), w=[('g8', f)])
                    P.ts('dve', g8[:], g8[:], 8.0, ALU.mult, r=[('g8', f)], w=[('g8s', f)])
                    tb = P.sb(ph, [128, NT, 4, 32], F32)
                    for j, (co, go) in enumerate(((0, 0), (32, 32), (0, 32), (32, 0))):
                        P.tt('dve', tb[:, :, j, :], cs_sb[:, :, co:co + 32], bc_mid(g8[:, go:go + 32], NT), ALU.mult,
                             r=['cs', ('g8s', f)], w=[('tab', f, j)])
                    tabs[f] = tb
                hR = Rot(P, ph, 3, [128, D], F32, 'h')
                uR = Rot(P, ph, 3, [128, D], BF16, 'u')
                uTR = Rot(P, ph, 3, [128, 8, 128], BF16, 'uT')
                ssR = Rot(P, ph, 3, [128, 2], F32, 'ss')
                junkA = P.sb(ph, [128, D], BF16)
                sqR = Rot(P, ph, 3, [128, 512], F32, 'sq')
                xnR = Rot(P, ph, 3, [128, 512], F32, 'xn')
                smR = Rot(P, ph, 4, [128, 16], F32, 'sm')
                tR = [Rot(P, ph, 3, [128, 8, 32], F32, 't%d' % i) for i in range(4)]
                rqR = Rot(P, ph, 3, [128, 16, 128], BF16, 'rq')
                stR = Rot(P, ph, 2, [128, 16, 128], BF16, 'stage')
                vaR = Rot(P, ph, 3, [128, 64], BF16, 'va')
                vbR = Rot(P, ph, 3, [128, 512], BF16, 'vb')
                wiR = Rot(P, ph, 3, [128, 4], F32, 'wi')
                sgR = Rot(P, ph, 3, [128, 2048], BF16, 'sg')
                ptrR = Rot(P, ph, 2, [128, 8, 128], BF16, 'ptr', psum=True)
                ppR = Rot(P, ph, 6, [128, 512], F32, 'pp', psum=True)

                def normrope(src, srckey, H, fam, pos, rq, rqkey, blk0, half0, normed):
                    sv = src.rearrange("p (h d) -> p h d", d=64)
                    if normed:
                        sq, sqk = sqR.next()
                        P.act(sq[:, 0:H * 64], src, AF.Square, r=[srckey], w=[sqk])
                        sm, smk = smR.next()
                        P.red(sm[:, 0:H], sq[:, 0:H * 64].rearrange("p (h d) -> p h d", d=64), ALU.add, r=[sqk], w=[smk])
                        P.rsqrt(sm[:, 8:8 + H], sm[:, 0:H], 64.0 * EPS, r=[smk], w=[smk])
                        xn, xnk = xnR.next()
                        xv = xn[:, 0:H * 64].rearrange("p (h d) -> p h d", d=64)
                        P.tt('dve', xv, sv, bc_last(sm[:, 8:8 + H], 64), ALU.mult, r=[srckey, smk], w=[xnk])
                        xk = xnk
                        tb = tabs[fam]
                        C1, S2, C2, S1 = (tb[:, pos, j, :] for j in range(4))
                        tr_ = [('tab', fam, j) for j in range(4)]
                    else:
                        xv, xk = sv, srckey
                        C1 = C2 = cs_sb[:, pos, 0:32]
                        S1 = S2 = cs_sb[:, pos, 32:64]
                        tr_ = ['cs'] * 4
                    x1 = xv[:, :, 0:32]
                    x2 = xv[:, :, 32:64]
                    ov = rq[:].rearrange("p b c -> p (b c)")[:, blk0 * 128 + half0: blk0 * 128 + half0 + H * 64]
                    ov = ov.rearrange("p (h d) -> p h d", d=64)
                    tb_ = [r_.next() for r_ in tR]
                    (t1, k1), (t2, k2), (t3, k3), (t4, k4) = tb_
                    P.tt('dve', t1[:, 0:H, :], x1, bc_mid(C1, H), ALU.mult, r=[xk, tr_[0]], w=[k1])
                    P.tt('dve', t2[:, 0:H, :], x2, bc_mid(S2, H), ALU.mult, r=[xk, tr_[1]], w=[k2])
                    P.tt('dve', ov[:, :, 0:32], t1[:, 0:H, :], t2[:, 0:H, :], ALU.subtract, r=[k1, k2], w=[rqkey])
                    P.tt('dve', t3[:, 0:H, :], x2, bc_mid(C2, H), ALU.mult, r=[xk, tr_[2]], w=[k3])
                    P.tt('dve', t4[:, 0:H, :], x1, bc_mid(S1, H), ALU.mult, r=[xk, tr_[3]], w=[k4])
                    P.tt('dve', ov[:, :, 32:64], t3[:, 0:H, :], t4[:, 0:H, :], ALU.add, r=[k3, k4], w=[rqkey])

                groups = [(0, 512), (512, 964), (964, 1476), (1476, 1988), (1988, 2500),
                          (2500, 3012), (3012, 3524), (3524, 4036), (4036, 4548)]
                def p1_tile(tt):
                    sq_i, pos = tt // NT, tt % NT
                    h_t, hk = hR.next()
                    P.dma('sp', h_t[:], hsrc[tt * 128:(tt + 1) * 128, :], w=[hk])
                    yield
                    ss, ssk = ssR.next()
                    P.act(junkA[:], h_t[:], AF.Square, accum=ss[:, 0:1], r=[hk], w=[ssk, 'junkA'])
                    P.rsqrt(ss[:, 1:2], ss[:, 0:1], D * EPS, r=[ssk], w=[ssk])
                    u, uk = uR.next()
                    P.stt('dve', u[:], h_t[:], ss[:, 1:2], gs_bc[:], ALU.mult, ALU.mult, r=[hk, ssk, 'gs'], w=[uk])
                    ptr, pk = ptrR.next()
                    for k in range(8):
                        P.tr(ptr[:, k, :], u[:, k * 128:(k + 1) * 128], ident[:], r=[uk, 'ident'], w=[pk])
                    uT, uTk = uTR.next()
                    P.cp('act', uT[:], ptr[:], r=[pk], w=[uTk])
                    rq, rqk = rqR.next()
                    sg, sgk = sgR.next()
                    yield
                    for gi, (c0, c1) in enumerate(groups):
                        pp, ppk = ppR.next()
                        for k in range(8):
                            P.mm(pp[:, 0:c1 - c0], uT[:, k, :], w_sb[:, k, c0:c1], k == 0, k == 7,
                                 r=[uTk, ('w', k)], w=[ppk])
                        if gi == 0:
                            normrope(pp[:, 0:512], ppk, 8, 'g_qa', pos, rq, rqk, 0, 0, True)
                        elif gi == 1:
                            normrope(pp[:, 0:64], ppk, 1, 'g_ka', pos, rq, rqk, 7, 0, True)
                            P.cp('dve', rq[:, 7, 64:128], rq[:, 7, 0:64], r=[rqk], w=[rqk])
                            va, vak = vaR.next()
                            P.cp('act', va[:], pp[:, 64:128], r=[ppk], w=[vak])
                            P.dma('sp', va_s[sq_i, pos, :, :], va[:], r=[vak])
                            normrope(pp[:, 128:448], ppk, 5, None, pos, rq, rqk, 4, 0, False)
                            P.cp('dve', rq[:, 6, 64:128], rq[:, 6, 0:64], r=[rqk], w=[rqk])
                            wi, wik = wiR.next()
                            P.ts('dve', wi[:], pp[:, 448:452], 1.0 / 16.0, ALU.mult, r=[ppk], w=[wik])
                            P.dma('sp', wi_s[sq_i, pos, :, :], wi[:], r=[wik])
                        elif gi == 2:
                            normrope(pp[:, 0:512], ppk, 8, 'g_qb', pos, rq, rqk, 8, 0, True)
                        elif gi == 3:
                            normrope(pp[:, 0:512], ppk, 8, 'g_kb', pos, rq, rqk, 12, 0, True)
                        elif gi == 4:
                            vb, vbk = vbR.next()
                            P.cp('act', vb[:], pp[:, 0:512], r=[ppk], w=[vbk])
                            P.dma('sp', vb_s[sq_i, pos, :, :], vb[:], r=[vbk])
                        else:
                            q = gi - 5
                            P.act(sg[:, q * 512:(q + 1) * 512], pp[:, 0:512], AF.Sigmoid, r=[ppk], w=[sgk])
                        if gi in (0, 1, 2, 3, 4, 6):
                            yield
                    P.dma('sp', sg_s[tt * 128:(tt + 1) * 128, :], sg[:], r=[sgk])
                    yield
                    stg, stk = stR.next()
                    for hb in range(2):
                        ptr, pk = ptrR.next()
                        for b in range(8):
                            P.tr(ptr[:, b, :], rq[:, hb * 8 + b, :], ident[:], r=[rqk, 'ident'], w=[pk])
                        P.cp('act', stg[:, hb * 8:(hb + 1) * 8, :], ptr[:], r=[pk], w=[stk])
                    P.dma('sp', fm_s[sq_i, :, :, pos * 128:(pos + 1) * 128], stg[:], r=[stk])
                    yield

                run_pipelined([(lambda tt=tt: p1_tile(tt)) for tt in range(NTT)], 3)
                P.barrier()
                P.emit()
            if debug == 1:
                break
            with ExitStack() as ph:
                wA = P.sb(ph, [128, 4, D], BF16)
                wB = P.sb(ph, [128, 4, D], BF16)
                wO = P.sb(ph, [128, 8, D], BF16)
                for k in range(4):
                    P.dma('pool', wA[:, k, :], w_bra[l, k * 128:(k + 1) * 128, :], w=[('wA', k)])
                    P.dma('pool', wB[:, k, :], w_brb[l, k * 128:(k + 1) * 128, :], w=[('wB', k)])
                for k in range(8):
                    P.dma('pool', wO[:, k, :], w_out[l, k * 128:(k + 1) * 128, :], w=[('wO', k)])
                gsub = P.sb(ph, [128, 128], F32)
                P.dma('sp', gsub[:], g_sub[l:l + 1, :].to_broadcast([128, 128]), w=['gsub0'])
                P.ts('dve', gsub[:], gsub[:], float((1.0 - lam_init) * math.sqrt(128.0)), ALU.mult, r=['gsub0'], w=['gsub'])
                FM = P.sb(ph, [128, 16, L], BF16)
                vaA = P.sb(ph, [128, NT, 65], BF16)
                vbA = P.sb(ph, [128, NT, 4, 129], BF16)
                wiS = P.sb(ph, [128, NT, 4], F32)
                isc = P.sb(ph, [128, L], F32)
                Mk = P.sb(ph, [128, L], BF16)
                MTs = [P.sb(ph, [128, NT, 128], BF16) for _ in range(2)]
                rlR = Rot(P, ph, 3, [128, 512], F32, 'rl')
                bis = P.sb(ph, [128, 8], F32)
                stepsX = P.sb(ph, [128, KBIS + 1], F32)
                ER = Rot(P, ph, 5, [128, 4, 128], BF16, 'E')
                PR = Rot(P, ph, 4, [128, 4, 128], BF16, 'Pm')
                rsA = P.sb(ph, [128, 8], F32)
                rsB = P.sb(ph, [128, 16], F32)
                sB2 = P.sb(ph, [128, 8], F32)
                oa_bf = P.sb(ph, [128, 512], BF16)
                ob_f = P.sb(ph, [128, 4, 128], F32)
                ob_t = P.sb(ph, [128, 4, 128], F32)
                ob_n = P.sb(ph, [128, 4, 128], F32)
                ob_sq = P.sb(ph, [128, 512], F32)
                ob_bf = P.sb(ph, [128, 512], BF16)
                oT = P.sb(ph, [128, 8, 128], BF16)
                m1 = P.sb(ph, [128, D], F32)
                m2 = P.sb(ph, [128, D], F32)
                mix_bf = P.sb(ph, [128, D], BF16)
                mT = P.sb(ph, [128, 8, 128], BF16)
                hR = Rot(P, ph, 2, [128, D], F32, 'h')
                h2R = Rot(P, ph, 2, [128, D], F32, 'h2')
                sgR = Rot(P, ph, 2, [128, 2048], BF16, 'sg')
                ps0 = P.ps(ph, [128, 512], F32)
                ps0b = ps0[:].bitcast(BF16).rearrange("p (b c) -> p b c", c=128)
                SR = Rot(P, ph, 3, [128, 512], F32, 'S', psum=True)
                ps_OA = P.ps(ph, [128, 512], F32)
                ps_OB = P.ps(ph, [128, 3, 512], F32)

                def oa_ap(h):
                    if h < 7:
                        return ps_OA[:, h * 65:h * 65 + 65]
                    return ps_OB[:, 2, 258:323]

                def ob_ap(q):
                    return ps_OB[:, q // 3, (q % 3) * 129:(q % 3) * 129 + 129]

                def nX(j):
                    return 4 * ((128 * (j + 1) + 511) // 512) + 4 + (KBIS if j >= KT else 0)

                def nY(j):
                    return 4 + 2 * (j + 1) + 5

                def X(s_i, j):
                    S = 128 * (j + 1)
                    tsl = slice(j * 128, (j + 1) * 128)
                    MT = MTs[j % 2]
                    MTk = ('MT', j % 2)
                    steps_ = [(c0, min(512, S - c0), hh) for c0 in range(0, S, 512) for hh in range(4)]
                    pend = {}

                    def x_mm(k):
                        c0, cw, hh = steps_[k]
                        r0 = 64 * (hh % 2)
                        P.mm(ps0[:, 0:cw], FM[r0:r0 + 64, 4 + hh // 2, tsl], FM[r0:r0 + 64, 6, c0:c0 + cw],
                             True, True, r=[('FM', 4 + hh // 2), ('FM', 6)], w=['b0'])

                    def x_relu(k):
                        c0, cw, hh = steps_[k]
                        rl, rlk = rlR.next()
                        P.act(rl[:, 0:cw], ps0[:, 0:cw], AF.Relu, r=['b0'], w=[rlk])
                        pend[k] = (rl, rlk)

                    def x_acc(k):
                        c0, cw, hh = steps_[k]
                        rl, rlk = pend.pop(k)
                        if hh == 0:
                            P.ts('dve', isc[:, c0:c0 + cw], rl[:, 0:cw], wiS[:, j, 0:1], ALU.mult,
                                 r=[rlk, 'wiS'], w=['isc'])
                        else:
                            P.stt('dve', isc[:, c0:c0 + cw], rl[:, 0:cw], wiS[:, j, hh:hh + 1], isc[:, c0:c0 + cw],
                                  ALU.mult, ALU.add, r=[rlk, 'wiS', 'isc'], w=['isc'])

                    nst = len(steps_)
                    for k in range(nst + 2):
                        if 0 <= k - 1 < nst:
                            x_relu(k - 1)
                        if k < nst:
                            x_mm(k)
                        if 0 <= k - 2 < nst:
                            x_acc(k - 2)
                        if k < nst:
                            yield
                    yield
                    if j >= KT:
                        P.red(bis[:, 0:1], isc[:, 0:S], ALU.min, r=['isc'], w=['mn'])
                    P.tt('dve', isc[:, S - 128:S], isc[:, S - 128:S], negmask[:], ALU.add, r=['isc', 'negm', 'mn'], w=['isc'])
                    if j >= KT:
                        P.red(bis[:, 1:2], isc[:, 0:S], ALU.max, r=['isc'], w=['mx'])
                        P.tt('dve', bis[:, 2:3], bis[:, 1:2], bis[:, 0:1], ALU.subtract, r=['mn', 'mx'], w=['w0'])
                        P.ts('dve', stepsX[:], pow2[:], bis[:, 2:3], ALU.mult, r=['w0', 'pow2'], w=['steps'])
                        P.tt('dve', bis[:, 3:4], bis[:, 0:1], stepsX[:, 1:2], ALU.add, r=['mn', 'steps'], w=['mid'])
                        yield
                        for k in range(KBIS):
                            P.ts('dve', Mk[:, 0:S], isc[:, 0:S], bis[:, 3:4], ALU.is_ge, 0.0, ALU.add,
                                 accum=bis[:, 4:5], r=['isc', 'mid'], w=['cnt', 'Mk'])
                            P.ts('dve', bis[:, 5:6], bis[:, 4:5], float(KTOP) - 0.5, ALU.is_ge, -0.5, ALU.add,
                                 r=['cnt'], w=['sgn'])
                            P.stt('dve', bis[:, 3:4], bis[:, 5:6], stepsX[:, k + 1:k + 2], bis[:, 3:4], ALU.mult, ALU.add,
                                  r=['sgn', 'steps', 'mid'], w=['mid'])
                            yield
                        P.stt('dve', bis[:, 6:7], stepsX[:, KBIS:KBIS + 1], -0.5, bis[:, 3:4], ALU.mult, ALU.add,
                              r=['steps', 'mid'], w=['thr'])
                        P.ts('dve', Mk[:, 0:S], isc[:, 0:S], bis[:, 6:7], ALU.is_ge, r=['isc', 'thr'], w=['Mk'])
                    else:
                        yield
                        P.ts('dve', Mk[:, 0:S], isc[:, 0:S], -1.0e29, ALU.is_ge, r=['isc'], w=['Mk'])
                    yield
                    for i0 in range(0, j + 1, 8):
                        n8 = min(8, j + 1 - i0)
                        for ii in range(n8):
                            i = i0 + ii
                            P.tr(ps0b[:, ii, :], Mk[:, i * 128:(i + 1) * 128], ident[:], r=['Mk', 'ident'], w=['b0'])
                        P.cp('act', MT[:, i0:i0 + n8, :], ps0b[:, 0:n8, :], r=['b0'], w=[MTk])
                    yield

                def Y(s_i, j):
                    tt = s_i * NT + j
                    tsl = slice(j * 128, (j + 1) * 128)
                    MT = MTs[j % 2]
                    MTk = ('MT', j % 2)
                    sg, sgk = sgR.next()
                    P.dma('sp', sg[:], sg_s[tt * 128:(tt + 1) * 128, :], w=[sgk])
                    h_t, hk = hR.next()
                    P.dma('sp', h_t[:], hsrc[tt * 128:(tt + 1) * 128, :], w=[hk])
                    units = [(i, kind) for i in range(j + 1) for kind in range(4)]
                    N = len(units)
                    live = {}

                    Sof = {}

                    def st_qk(n):
                        i, kind = units[n]
                        ssl = slice(i * 128, (i + 1) * 128)
                        S_, Sk = SR.next()
                        if kind < 2:
                            r0 = 64 * kind
                            P.mm(S_[:].rearrange("p (a t) -> p a t", t=128), FM[r0:r0 + 64, 7, ssl], FM[r0:r0 + 64, 0:4, tsl],
                                 True, True, r=[('FM', 7), ('FM', 0), ('FM', 1), ('FM', 2), ('FM', 3)], w=[Sk])
                        else:
                            r0 = 64 * (kind - 2)
                            for hh in range(4):
                                P.mm(S_[:, hh * 128:(hh + 1) * 128], FM[r0:r0 + 64, 12 + hh, ssl], FM[r0:r0 + 64, 8 + hh, tsl],
                                     True, True, r=[('FM', 12 + hh), ('FM', 8 + hh)], w=[Sk])
                        Sof[n] = (S_, Sk)

                    def st_exp(n):
                        S_, Sk = Sof.pop(n)
                        E, Ek = ER.next()
                        P.act(E[:].rearrange("p a t -> p (a t)"), S_[:], AF.Exp, scale=0.125, r=[Sk], w=[Ek])
                        live[n] = [E, Ek, None, None]

                    def st_mask(n):
                        i, kind = units[n]
                        E, Ek = live[n][0], live[n][1]
                        if kind < 2:
                            Pm, Pk = PR.next()
                            P.tt('dve', Pm[:], E[:], bc_mid(MT[:, i, :], 4), ALU.mult, r=[Ek, MTk], w=[Pk])
                            live[n][2], live[n][3] = Pm, Pk
                        elif i == j:
                            P.tt('dve', E[:], E[:], bc_mid(caus01T[:], 4), ALU.mult, r=[Ek, 'caus'], w=[Ek])

                    def st_av(n):
                        i, kind = units[n]
                        E, Ek, Pm, Pk = live.pop(n)
                        if kind < 2:
                            for pr in range(4):
                                hh = 2 * pr + kind
                                P.mm(oa_ap(hh), Pm[:, pr, :], vaA[:, i, :], False, False, r=[Pk, 'vaA'],
                                     w=['OA', 'OB2'] if hh == 7 else ['OA'], acc0=True,
                                     first=(i == 0 and hh in (0, 7)))
                        else:
                            m_ = kind - 2
                            for hh in range(4):
                                P.mm(ob_ap(2 * hh + m_), E[:, hh, :], vbA[:, i, hh, :], False, False, r=[Ek, 'vbA'],
                                     w=['OB', 'OB2'] if hh == 3 else ['OB'], acc0=True,
                                     first=(i == 0 and m_ == 0 and hh in (0, 2)))

                    yield
                    for st in range(N + 3):
                        if st < N:
                            st_qk(st)
                        if 0 <= st - 1 < N:
                            st_exp(st - 1)
                        if 0 <= st - 2 < N:
                            st_mask(st - 2)
                        if 0 <= st - 3 < N:
                            st_av(st - 3)
                        if st % 2 == 1:
                            yield
                    yield
                    v7 = ps_OA[:, 0:455].rearrange("p (h c) -> p h c", c=65)
                    P.recip(rsA[:, 0:7], v7[:, :, 64], r=['OA'], w=['rsA'])
                    P.recip(rsA[:, 7:8], ps_OB[:, 2, 322:323], r=['OA', 'OB2'], w=['rsA'])
                    P.tt('dve', oa_bf[:, 0:448].rearrange("p (h d) -> p h d", d=64), v7[:, :, 0:64], bc_last(rsA[:, 0:7], 64),
                         ALU.mult, r=['OA', 'rsA'], w=['oa'])
                    P.ts('dve', oa_bf[:, 448:512], ps_OB[:, 2, 258:322], rsA[:, 7:8], ALU.mult, r=['OA', 'OB2', 'rsA'], w=['oa'])
                    yield
                    for b in range(3):
                        nq = 3 if b < 2 else 2
                        vq = ps_OB[:, b, 0:nq * 129].rearrange("p (q c) -> p q c", c=129)
                        P.recip(rsB[:, 3 * b:3 * b + nq], vq[:, :, 128], r=['OB', 'OB2'], w=['rsB'])
                    P.ts('dve', rsB[:, 8:16], rsB[:, 0:8], nlam_sb[:, l:l + 1], ALU.mult, r=['rsB', ('nlam', l)], w=['rsB2'])
                    for hh in range(4):
                        P.ts('dve', ob_t[:, hh, :], ob_ap(2 * hh)[:, 0:128], rsB[:, 2 * hh:2 * hh + 1], ALU.mult,
                             r=['OB', 'OB2', 'rsB'], w=['ob_t'])
                        P.stt('dve', ob_f[:, hh, :], ob_ap(2 * hh + 1)[:, 0:128], rsB[:, 8 + 2 * hh + 1:8 + 2 * hh + 2],
                              ob_t[:, hh, :], ALU.mult, ALU.add, r=['OB', 'OB2', 'rsB2', 'ob_t'], w=['ob_f'])
                    P.act(ob_sq[:], ob_f[:].rearrange("p h d -> p (h d)"), AF.Square, r=['ob_f'], w=['ob_sq'])
                    P.red(sB2[:, 0:4], ob_sq[:].rearrange("p (h d) -> p h d", d=128), ALU.add, r=['ob_sq'], w=['ssb'])
                    P.rsqrt(sB2[:, 4:8], sB2[:, 0:4], 128.0 * EPS, r=['ssb'], w=['rsb'])
                    P.tt('dve', ob_n[:], ob_f[:], bc_last(sB2[:, 4:8], 128), ALU.mult, r=['ob_f', 'rsb'], w=['ob_n'])
                    P.tt('dve', ob_bf[:].rearrange("p (h d) -> p h d", d=128), ob_n[:], bc_mid(gsub[:], 4), ALU.mult,
                         r=['ob_n', 'gsub'], w=['ob'])
                    yield
                    S_, Sk = SR.next()
                    Sb = S_[:].bitcast(BF16).rearrange("p (b c) -> p b c", c=128)
                    for b in range(4):
                        P.tr(Sb[:, b, :], oa_bf[:, b * 128:(b + 1) * 128], ident[:], r=['oa', 'ident'], w=[Sk])
                        P.tr(Sb[:, 4 + b, :], ob_bf[:, b * 128:(b + 1) * 128], ident[:], r=['ob', 'ident'], w=[Sk])
                    P.cp('act', oT[:], Sb, r=[Sk], w=['oT'])
                    yield
                    for n2 in range(2):
                        nsl = slice(n2 * 512, (n2 + 1) * 512)
                        S_, Sk = SR.next()
                        for k in range(4):
                            P.mm(S_[:], oT[:, k, :], wA[:, k, nsl], k == 0, k == 3, r=['oT', ('wA', k)], w=[Sk])
                        P.tt('dve', m1[:, nsl], S_[:], sg[:, nsl], ALU.mult, r=[Sk, sgk], w=['m1'])
                    for n2 in range(2):
                        nsl = slice(n2 * 512, (n2 + 1) * 512)
                        S_, Sk = SR.next()
                        for k in range(4):
                            P.mm(S_[:], oT[:, 4 + k, :], wB[:, k, nsl], k == 0, k == 3, r=['oT', ('wB', k)], w=[Sk])
                        P.tt('dve', m2[:, nsl], S_[:], sg[:, 1024 + n2 * 512:1024 + (n2 + 1) * 512], ALU.mult,
                             r=[Sk, sgk], w=['m2'])
                    P.tt('dve', mix_bf[:], m1[:], m2[:], ALU.add, r=['m1', 'm2'], w=['mix'])
                    yield
                    S_, Sk = SR.next()
                    Sb = S_[:].bitcast(BF16).rearrange("p (b c) -> p b c", c=128)
                    for k in range(8):
                        P.tr(Sb[:, k, :], mix_bf[:, k * 128:(k + 1) * 128], ident[:], r=['mix', 'ident'], w=[Sk])
                    P.cp('act', mT[:], Sb, r=[Sk], w=['mT'])
                    h2, h2k = h2R.next()
                    for n2 in range(2):
                        nsl = slice(n2 * 512, (n2 + 1) * 512)
                        S_, Sk = SR.next()
                        for k in range(8):
                            P.mm(S_[:], mT[:, k, :], wO[:, k, nsl], k == 0, k == 7, r=['mT', ('wO', k)], w=[Sk])
                        P.tt('dve', h2[:, nsl], S_[:], h_t[:, nsl], ALU.add, r=[Sk, hk], w=[h2k])
                    P.dma('sp', h_s[tt * 128:(tt + 1) * 128, :], h2[:], r=[h2k])
                    yield

                def interleave(ga, na, gb, nb):
                    ia = ib = 0
                    while ga is not None or gb is not None:
                        pick_a = gb is None or (ga is not None and ia * nb <= ib * na)
                        if pick_a:
                            try:
                                next(ga)
                                ia += 1
                            except StopIteration:
                                ga = None
                        else:
                            try:
                                next(gb)
                                ib += 1
                            except StopIteration:
                                gb = None

                for s_i in range(NSEQ):
                    for b in range(16):
                        P.dma('sp', FM[:, b, :], fm_s[s_i, :, b, :], w=[('FM', b)])
                    P.memset('pool', vaA[:], 1.0, w=['vaA'])
                    P.memset('pool', vbA[:], 1.0, w=['vbA'])
                    P.dma('sp', vaA[:, :, 0:64], va_s[s_i].rearrange("n p d -> p n d"), w=['vaA'])
                    for hh in range(4):
                        P.dma('sp', vbA[:, :, hh, 0:128], vb_s[s_i, :, :, hh * 128:(hh + 1) * 128].rearrange("n p d -> p n d"),
                              w=['vbA'])
                    P.dma('sp', wiS[:], wi_s[s_i].rearrange("n p d -> p n d"), w=['wiS'])
                    for _ in X(s_i, 0):
                        pass
                    for j in range(NT):
                        gx = X(s_i, j + 1) if j + 1 < NT else None
                        interleave(Y(s_i, j), nY(j), gx, nX(j + 1) if j + 1 < NT else 1)
                P.barrier()
                P.emit()
            if debug == 2:
                break
            with ExitStack() as ph:
                wU = P.sb(ph, [128, 8, 2 * DFF], BF16)
                for k in range(8):
                    P.dma('pool', wU[:, k, :], w_up[l, k * 128:(k + 1) * 128, :], w=[('wU', k)])
                gs_bc = P.sb(ph, [128, D], F32)
                P.dma('sp', gs_bc[:], g_ffn[l:l + 1, :].to_broadcast([128, D]), w=['gs0'])
                P.ts('dve', gs_bc[:], gs_bc[:], float(math.sqrt(D)), ALU.mult, r=['gs0'], w=['gs'])
                cwr = P.sb(ph, [44, 4, 128], F32)
                for jj in range(3):
                    P.dma('sp', cwr[:, jj, :], conv_w[l, jj, :].rearrange("(c p) -> c p", p=128), w=['cwr'])
                P.dma('sp', cwr[:, 3, :], conv_b[l, :].rearrange("(c p) -> c p", p=128), w=['cwr'])
                cw = P.sb(ph, [128, 4, 44], F32)
                ps_c = P.ps(ph, [128, 4, 44], F32)
                for jj in range(4):
                    P.tr(ps_c[:, jj, :], cwr[:, jj, :], identf[0:44, 0:44], r=['cwr', 'identf'], w=['ps_c'])
                P.cp('dve', cw[:], ps_c[:], r=['ps_c'], w=['cw'])
                halo = P.sb(ph, [128, 44, 2], F32)
                hG = Rot(P, ph, 3, [128, D], F32, 'hG')
                uG = Rot(P, ph, 2, [128, D], BF16, 'uG')
                uTG = Rot(P, ph, 2, [128, 8, 512], BF16, 'uTG')
                ssR = Rot(P, ph, 4, [128, 2], F32, 'ss')
                junkA = P.sb(ph, [128, D], BF16)
                xsR = Rot(P, ph, 4, [128, 514], F32, 'xs')
                acR = Rot(P, ph, 4, [128, 512], F32, 'acc')
                glR = Rot(P, ph, 2, [128, 512], F32, 'gl')
                prR = Rot(P, ph, 2, [128, NFC, 512], BF16, 'prod')
                ptrR = Rot(P, ph, 2, [128, 8, 128], BF16, 'ptr', psum=True)
                ppR = Rot(P, ph, 5, [128, 512], F32, 'pp', psum=True)

                def p3a_group(g):
                    first = (g * 512) % L == 0
                    uT, uTk = uTG.next()
                    for a in range(4):
                        hg, hgk = hG.next()
                        P.dma('sp', hg[:], h_s[g * 512 + a * 128:g * 512 + (a + 1) * 128, :], w=[hgk])
                        ss, ssk = ssR.next()
                        P.act(junkA[:], hg[:], AF.Square, accum=ss[:, 0:1], r=[hgk], w=[ssk, 'junkA'])
                        P.rsqrt(ss[:, 1:2], ss[:, 0:1], D * EPS, r=[ssk], w=[ssk])
                        ug, ugk = uG.next()
                        P.stt('dve', ug[:], hg[:], ss[:, 1:2], gs_bc[:], ALU.mult, ALU.mult, r=[hgk, ssk, 'gs'], w=[ugk])
                        ptr, pk = ptrR.next()
                        for k in range(8):
                            P.tr(ptr[:, k, :], ug[:, k * 128:(k + 1) * 128], ident[:], r=[ugk, 'ident'], w=[pk])
                        P.cp('act', uT[:, :, a * 128:(a + 1) * 128], ptr[:], r=[pk], w=[uTk])
                        yield
                    prod, prk = prR.next()
                    for q in range(NFC):
                        res = []
                        for fc in (q, q + NFC):
                            pp, ppk = ppR.next()
                            for k in range(8):
                                P.mm(pp[:], wU[:, k, fc * 128:(fc + 1) * 128], uT[:, k, :], k == 0, k == 7,
                                     r=[uTk, ('wU', k)], w=[ppk])
                            xs, xsk = xsR.next()
                            P.cp('act', xs[:, 2:514], pp[:], r=[ppk], w=[xsk])
                            if first:
                                P.memset('pool', xs[:, 0:2], 0.0, w=[xsk])
                            else:
                                P.cp('pool', xs[:, 0:2], halo[:, fc, :], r=[('halo', fc)], w=[xsk])
                            acc, ack = acR.next()
                            P.act(acc[:], pp[:], AF.Identity, bias=cw[:, 3, fc:fc + 1], scale=cw[:, 2, fc:fc + 1],
                                  r=[ppk, 'cw'], w=[ack])
                            P.stt('dve', acc[:], xs[:, 1:513], cw[:, 1, fc:fc + 1], acc[:], ALU.mult, ALU.add,
                                  r=[xsk, 'cw', ack], w=[ack])
                            P.stt('dve', acc[:], xs[:, 0:512], cw[:, 0, fc:fc + 1], acc[:], ALU.mult, ALU.add,
                                  r=[xsk, 'cw', ack], w=[ack])
                            P.cp('pool', halo[:, fc, :], xs[:, 512:514], r=[xsk], w=[('halo', fc)])
                            res.append((acc, ack))
                        gl, glk = glR.next()
                        P.act(gl[:], res[0][0][:], AF.Gelu_apprx_tanh, r=[res[0][1]], w=[glk])
                        P.tt('pool', prod[:, q, :], gl[:], res[1][0][:], ALU.mult, r=[glk, res[1][1]], w=[prk])
                        yield
                    P.dma('sp', pr_s[:, :, g * 512:(g + 1) * 512], prod[:], r=[prk])
                    yield

                run_pipelined([(lambda g=g: p3a_group(g)) for g in range(NG)], 2)
                P.barrier()
                P.emit()
            if debug == 3:
                break
            with ExitStack() as ph:
                wD = P.sb(ph, [128, NFC, D], BF16)
                for fc in range(NFC):
                    P.dma('pool', wD[:, fc, :], w_down[l, fc * 128:(fc + 1) * 128, :], w=[('wD', fc)])
                wG = P.sb(ph, [128, 8, D], BF16)
                for k in range(8):
                    P.dma('pool', wG[:, k, :], w_pg[l, k * 128:(k + 1) * 128, :], w=[('wG', k)])
                wPp = P.sb(ph, [128, 2, D], BF16)
                for k in range(2):
                    P.dma('pool', wPp[:, k, :], w_pp[l, k * 128:(k + 1) * 128, :], w=[('wPp', k)])
                gs_bc = P.sb(ph, [128, D], F32)
                P.dma('sp', gs_bc[:], g_ple[l:l + 1, :].to_broadcast([128, D]), w=['gs0'])
                P.ts('dve', gs_bc[:], gs_bc[:], float(math.sqrt(D)), ALU.mult, r=['gs0'], w=['gs'])
                hR = Rot(P, ph, 4, [128, D], F32, 'h')
                h3R = Rot(P, ph, 4, [128, D], F32, 'h3')
                h4R = Rot(P, ph, 3, [128, D], F32, 'h4')
                prR = Rot(P, ph, 3, [128, NFC, 128], BF16, 'pr')
                pbR = Rot(P, ph, 3, [128, PLE], BF16, 'pb')
                pTR = Rot(P, ph, 4, [128, 2, 128], BF16, 'pT')
                ssR = Rot(P, ph, 4, [128, 2], F32, 'ss')
                uR = Rot(P, ph, 3, [128, D], BF16, 'u')
                uTR = Rot(P, ph, 3, [128, 8, 128], BF16, 'uT')
                sgR = Rot(P, ph, 2, [128, D], F32, 'sgt')
                tmR = Rot(P, ph, 2, [128, D], F32, 'tm')
                junkA = P.sb(ph, [128, D], BF16)
                psD = P.ps(ph, [128, 2, 512], F32)
                psG = P.ps(ph, [128, 2, 512], F32)
                psP = P.ps(ph, [128, 2, 512], F32)
                ptrR = Rot(P, ph, 2, [128, 8, 128], BF16, 'ptr', psum=True)
                hdst = y if l == DEPTH - 1 else h_s
                def p3b_tile(tt):
                    rows = slice(tt * 128, (tt + 1) * 128)
                    h_t, hk = hR.next()
                    P.dma('sp', h_t[:], h_s[rows, :], w=[hk])
                    pr, prk = prR.next()
                    P.dma('sp', pr[:], pr_s[:, :, rows], w=[prk])
                    pb, pbk = pbR.next()
                    P.dma('pool', pb[:], p_in[l, rows, :], w=[pbk])
                    yield
                    h3, h3k = h3R.next()
                    for n2 in range(2):
                        nsl = slice(n2 * 512, (n2 + 1) * 512)
                        for fc in range(NFC):
                            P.mm(psD[:, n2, :], pr[:, fc, :], wD[:, fc, nsl], fc == 0, fc == NFC - 1,
                                 r=[prk, ('wD', fc)], w=[('psD', n2)])
                        P.tt('dve', h3[:, nsl], psD[:, n2, :], h_t[:, nsl], ALU.add, r=[('psD', n2), hk], w=[h3k])
                    ptr, pk = ptrR.next()
                    for k in range(2):
                        P.tr(ptr[:, k, :], pb[:, k * 128:(k + 1) * 128], ident[:], r=[pbk, 'ident'], w=[pk])
                    pT, pTk = pTR.next()
                    P.cp('act', pT[:], ptr[:, 0:2, :], r=[pk], w=[pTk])
                    ss, ssk = ssR.next()
                    P.act(junkA[:], h3[:], AF.Square, accum=ss[:, 0:1], r=[h3k], w=[ssk, 'junkA'])
                    yield
                    P.rsqrt(ss[:, 1:2], ss[:, 0:1], D * EPS, r=[ssk], w=[ssk])
                    u, uk = uR.next()
                    P.stt('dve', u[:], h3[:], ss[:, 1:2], gs_bc[:], ALU.mult, ALU.mult, r=[h3k, ssk, 'gs'], w=[uk])
                    yield
                    ptr, pk = ptrR.next()
                    for k in range(8):
                        P.tr(ptr[:, k, :], u[:, k * 128:(k + 1) * 128], ident[:], r=[uk, 'ident'], w=[pk])
                    uT, uTk = uTR.next()
                    P.cp('act', uT[:], ptr[:], r=[pk], w=[uTk])
                    yield
                    sgt, sgk = sgR.next()
                    tm, tmk = tmR.next()
                    h4, h4k = h4R.next()
                    for n2 in range(2):
                        nsl = slice(n2 * 512, (n2 + 1) * 512)
                        for k in range(8):
                            P.mm(psG[:, n2, :], uT[:, k, :], wG[:, k, nsl], k == 0, k == 7, r=[uTk, ('wG', k)], w=[('psG', n2)])
                        P.act(sgt[:, nsl], psG[:, n2, :], AF.Sigmoid, r=[('psG', n2)], w=[sgk])
                        for k in range(2):
                            P.mm(psP[:, n2, :], pT[:, k, :], wPp[:, k, nsl], k == 0, k == 1, r=[pTk, ('wPp', k)], w=[('psP', n2)])
                        P.tt('dve', tm[:, nsl], psP[:, n2, :], sgt[:, nsl], ALU.mult, r=[('psP', n2), sgk], w=[tmk])
                        P.tt('dve', h4[:, nsl], tm[:, nsl], h3[:, nsl], ALU.add, r=[tmk, h3k], w=[h4k])
                    P.dma('sp', hdst[rows, :], h4[:], r=[h4k])
                    yield

                run_pipelined([(lambda tt=tt: p3b_tile(tt)) for tt in range(NTT)], 3)
                P.barrier()
                P.emit()
    return nc


def rope_table(L):
    NT = L // 128
    inv = 1.0 / (10000.0 ** (np.arange(0, 64, 2, dtype=np.float32) / np.float32(64.0)))
    ang = np.arange(L, dtype=np.float32)[:, None] * inv[None, :].astype(np.float32)
    cs = np.concatenate([np.cos(ang), np.sin(ang)], axis=1).astype(np.float32)
    return np.ascontiguousarray(cs.reshape(NT, 128, 64).transpose(1, 0, 2))


_CACHE = {}


def run(inputs, L, NSEQ, DEPTH, ncores, debug=False):
    key = (L, NSEQ, DEPTH, debug)
    if key not in _CACHE:
        _CACHE[key] = build(L, NSEQ, DEPTH, debug)
    nc = _CACHE[key]
    T = NSEQ * L
    xs = np.ascontiguousarray(inputs['x'], dtype=np.float32).reshape(ncores, T, D)
    ps = np.ascontiguousarray(inputs['p'], dtype=np.float32).reshape(DEPTH, ncores, T, PLE)
    cs = rope_table(L)
    in_maps = []
    for c in range(ncores):
        m = {k: np.ascontiguousarray(v, dtype=np.float32) for k, v in inputs.items() if k not in ('x', 'p')}
        m['x'] = xs[c]
        m['p'] = np.ascontiguousarray(ps[:, c])
        m['cs_tab'] = cs
        in_maps.append(m)
    res = run_bass_kernel_spmd(nc, in_maps, core_ids=list(range(ncores)))
    return res


def kernel(**inputs):
    B, L, _ = inputs['x'].shape
    DEPTH = inputs['p'].shape[0]
    ncores = 8
    NSEQ = B // ncores
    res = run(inputs, L, NSEQ, DEPTH, ncores)
    out = np.stack([np.asarray(r['y'], dtype=np.float32) for r in res.results], axis=0)
    return out.reshape(B, L, D)
```

```python
import math
import numpy as np
from contextlib import ExitStack
import concourse.bass as bass
import concourse.mybir as mybir
from concourse.bass_utils import run_bass_kernel_spmd

F32 = mybir.dt.float32
BF16 = mybir.dt.bfloat16
AF = mybir.ActivationFunctionType
ALU = mybir.AluOpType
AX = mybir.AxisListType

D = 1024
NIN = 4548
DFF = 2816
NFC = 22
PLE = 256
EPS = 1e-6
KBIS = 13
NEG = -1.0e30
COMPUTE = ('pe', 'act', 'dve', 'pool')
ENGS = ('sp', 'pool', 'act', 'dve', 'pe')


class Op(object):
    __slots__ = ('eng', 'fn', 'deps', 'dma', 'sem', 'semval', 'ms', 'waited')


class Prog(object):
    def __init__(self, nc, es, n_sp=16, n_pq=8):
        self.nc = nc
        self.esem = {e: es.enter_context(nc.semaphore('sem_' + e)) for e in COMPUTE}
        self.dsem = {'sp': [es.enter_context(nc.semaphore('dsp%d' % i)) for i in range(n_sp)],
                     'pool': [es.enter_context(nc.semaphore('dpq%d' % i)) for i in range(n_pq)]}
        self.duse = {q: [0] * len(v) for q, v in self.dsem.items()}
        self.dlast = {q: [None] * len(v) for q, v in self.dsem.items()}
        self.dnext = {q: 0 for q in self.dsem}
        self.mscount = {e: 0 for e in COMPUTE}
        self.waitedv = {e: {} for e in ENGS}
        self.state = {}
        self.ops = {e: [] for e in ENGS}
        self.last = {e: None for e in ENGS}
        self.cnt = 0

    def sb(self, ctx, shape, dt):
        self.cnt += 1
        return ctx.enter_context(self.nc.sbuf_tensor('sb%d' % self.cnt, list(shape), dt))

    def ps(self, ctx, shape, dt):
        self.cnt += 1
        return ctx.enter_context(self.nc.psum_tensor('ps%d' % self.cnt, list(shape), dt))

    def op(self, eng, fn, r=(), w=(), dma=False):
        o = Op()
        o.eng = eng; o.fn = fn; o.dma = dma; o.waited = False; o.ms = 0; o.sem = None; o.semval = 0
        deps = {}
        for k in r:
            st = self.state.get(k)
            if st is not None and st[0] is not None:
                deps[st[0]] = 'raw'
        for k in w:
            st = self.state.get(k)
            if st is not None:
                if st[0] is not None:
                    deps.setdefault(st[0], 'waw')
                for ro in st[1].values():
                    deps.setdefault(ro, 'war')
                for ro in st[2]:
                    deps.setdefault(ro, 'war')
        if dma:
            n = len(self.dsem[eng])
            slot = self.dnext[eng]
            self.dnext[eng] = (slot + 1) % n
            prev = self.dlast[eng][slot]
            if prev is not None:
                deps.setdefault(prev, 'slot')
            self.duse[eng][slot] += 1
            o.sem = self.dsem[eng][slot]
            o.semval = 16 * self.duse[eng][slot]
            self.dlast[eng][slot] = o
        fd = []
        for d, kind in deps.items():
            if d is o:
                continue
            if (not d.dma) and (not dma) and d.eng == eng:
                if eng == 'pe':
                    continue
            fd.append(d)
            d.waited = True
        o.deps = fd
        for k in r:
            st = self.state.setdefault(k, [None, {}, []])
            if dma:
                st[2].append(o)
            else:
                st[1][eng] = o
        for k in w:
            self.state[k] = [o, {}, []]
        self.ops[eng].append(o)
        if not dma:
            self.last[eng] = o
        return o

    def barrier(self):
        targets = [self.last[e] for e in COMPUTE if self.last[e] is not None]
        for q in self.dlast:
            targets += [d for d in self.dlast[q] if d is not None]
        for e in ENGS:
            o = Op()
            o.eng = e; o.fn = None; o.dma = False; o.waited = False; o.ms = 0; o.sem = None; o.semval = 0
            o.deps = [t for t in targets if t.dma or t.eng != e]
            for t in o.deps:
                t.waited = True
            self.ops[e].append(o)
        self.state = {}

    def emit(self):
        for e in COMPUTE:
            for o in self.ops[e]:
                if o.fn is not None and (not o.dma) and o.waited:
                    self.mscount[e] += 1
                    o.ms = self.mscount[e]
        nc = self.nc
        with nc.Block() as block:
            decos = {'sp': block.sync, 'pool': block.gpsimd, 'act': block.scalar, 'dve': block.vector,
                     'pe': block.tensor}
            for e in ENGS:
                ops = self.ops[e]
                if not ops:
                    continue

                def body(eo, ops=ops, e=e):
                    wt = self.waitedv[e]
                    for o in ops:
                        for d in o.deps:
                            if d.dma:
                                sem, val = d.sem, d.semval
                            else:
                                sem, val = self.esem[d.eng], d.ms
                            key = id(sem)
                            if wt.get(key, 0) < val:
                                eo.wait_ge(sem, val)
                                wt[key] = val
                        if o.fn is None:
                            continue
                        ins = o.fn(eo)
                        if o.dma:
                            ins.then_inc(o.sem, 16)
                        elif o.waited:
                            ins.then_inc(self.esem[e], 1)
                decos[e](body)
        self.ops = {e: [] for e in ENGS}

    def dma(self, q, out, in_, r=(), w=()):
        return self.op(q, lambda e: e.dma_start(out=out, in_=in_), r, w, dma=True)

    def tt(self, eng, out, in0, in1, op, r=(), w=()):
        return self.op(eng, lambda e: e.tensor_tensor(out=out, in0=in0, in1=in1, op=op), r, w)

    def ts(self, eng, out, in0, s1, op0, s2=None, op1=None, accum=None, r=(), w=()):
        if op1 is None:
            return self.op(eng, lambda e: e.tensor_scalar(out=out, in0=in0, scalar1=s1, scalar2=None, op0=op0), r, w)
        if accum is None:
            return self.op(eng, lambda e: e.tensor_scalar(out=out, in0=in0, scalar1=s1, scalar2=s2, op0=op0,
                                                           op1=op1), r, w)
        return self.op(eng, lambda e: e.tensor_scalar(out=out, in0=in0, scalar1=s1, scalar2=s2, op0=op0,
                                                       op1=op1, accum_out=accum), r, w)

    def stt(self, eng, out, in0, scalar, in1, op0, op1, r=(), w=()):
        return self.op(eng, lambda e: e.scalar_tensor_tensor(out=out, in0=in0, scalar=scalar, in1=in1,
                                                              op0=op0, op1=op1), r, w)

    def act(self, out, in_, func, bias=None, scale=None, accum=None, r=(), w=()):
        kw = {}
        if bias is not None:
            kw['bias'] = bias
        if scale is not None:
            kw['scale'] = scale
        if accum is not None:
            kw['accum_out'] = accum
        return self.op('act', lambda e: e.activation(out=out, in_=in_, func=func, **kw), r, w)

    def cp(self, eng, out, in_, r=(), w=()):
        if eng == 'act':
            return self.op('act', lambda e: e.copy(out=out, in_=in_), r, w)
        return self.op(eng, lambda e: e.tensor_copy(out=out, in_=in_), r, w)

    def mm(self, out, lhsT, rhs, start, stop, r=(), w=(), acc0=False, first=False):
        if acc0:
            return self.op('pe', lambda e: e.matmul(out, lhsT=lhsT, rhs=rhs, start=first, stop=False,
                                                    skip_group_check=True), r, w)
        return self.op('pe', lambda e: e.matmul(out, lhsT=lhsT, rhs=rhs, start=start, stop=stop), r, w)

    def tr(self, out, in_, ident, r=(), w=()):
        return self.op('pe', lambda e: e.transpose(out=out, in_=in_, identity=ident), r, w)

    def red(self, out, in_, op, r=(), w=()):
        return self.op('dve', lambda e: e.tensor_reduce(out=out, in_=in_, axis=AX.X, op=op), r, w)

    def recip(self, out, in_, r=(), w=()):
        return self.op('dve', lambda e: e.reciprocal(out=out, in_=in_), r, w)

    def rsqrt(self, out, in_, addc, r=(), w=()):
        self.op('act', lambda e: e.activation(out=out, in_=in_, func=AF.Sqrt, bias=addc, scale=1.0), r, w)
        return self.op('dve', lambda e: e.reciprocal(out=out, in_=out), w, w)

    def memset(self, eng, ap, val, r=(), w=()):
        return self.op(eng, lambda e: e.memset(ap, val), r, w)


class Rot(object):
    def __init__(self, P, ctx, n, shape, dt, name, psum=False):
        self.bufs = [(P.ps if psum else P.sb)(ctx, shape, dt) for _ in range(n)]
        self.name = name
        self.i = -1

    def next(self):
        self.i = (self.i + 1) % len(self.bufs)
        return self.bufs[self.i], (self.name, self.i)


def run_pipelined(makers, depth, period=1):
    active = []
    idx = 0
    rnd = 0
    while idx < len(makers) or active:
        if idx < len(makers) and len(active) < depth and (rnd % period == 0 or not active):
            active.append(makers[idx]())
            idx += 1
        for g in list(active):
            try:
                next(g)
            except StopIteration:
                active.remove(g)
        rnd += 1


def bc_mid(ap, n):
    return ap.unsqueeze(1).to_broadcast([ap.shape[0], n, ap.shape[1]])


def bc_last(ap, n):
    return ap.unsqueeze(2).to_broadcast([ap.shape[0], ap.shape[1], n])


def build(L, NSEQ, DEPTH, debug=False):
    NT = L // 128
    T = NSEQ * L
    NTT = T // 128
    KTOP = min(256, L // 4)
    KT = KTOP // 128
    NG = T // 512
    nc = bass.Bass("TRN2", target_bir_lowering=False)

    def din(name, shape, dt=F32):
        return nc.dram_tensor(name, list(shape), dt, kind="ExternalInput").ap()

    x = din("x", [T, D])
    p_in = din("p", [DEPTH, T, PLE])
    g_mix = din("g_mix_norm", [DEPTH, D])
    w_in = din("w_in", [DEPTH, D, NIN])
    g_q = {f: din(f, [DEPTH, 64]) for f in ("g_qa", "g_ka", "g_qb", "g_kb")}
    lamv = {f: din(f, [DEPTH, 64]) for f in ("lam_q1", "lam_k1", "lam_q2", "lam_k2")}
    g_sub = din("g_subln", [DEPTH, 128])
    w_bra = din("w_branch_a", [DEPTH, 512, D])
    w_brb = din("w_branch_b", [DEPTH, 512, D])
    w_out = din("w_out", [DEPTH, D, D])
    g_ffn = din("g_ffn_norm", [DEPTH, D])
    w_up = din("w_up", [DEPTH, D, 2 * DFF])
    conv_w = din("conv_w", [DEPTH, 3, 2 * DFF])
    conv_b = din("conv_b", [DEPTH, 2 * DFF])
    w_down = din("w_down", [DEPTH, DFF, D])
    g_ple = din("g_ple_norm", [DEPTH, D])
    w_pg = din("w_ple_gate", [DEPTH, D, D])
    w_pp = din("w_ple_proj", [DEPTH, PLE, D])
    cs_in = din("cs_tab", [128, NT, 64])
    y = nc.dram_tensor("y", [T, D], F32, kind="ExternalOutput").ap()

    skind = "ExternalOutput" if debug else "Internal"

    def dscr(name, shape, dt):
        return nc.dram_tensor(name, list(shape), dt, kind=skind).ap()

    h_s = dscr("h_s", [T, D], F32)
    fm_s = dscr("fm_s", [NSEQ, 128, 16, L], BF16)
    va_s = dscr("va_s", [NSEQ, NT, 128, 64], BF16)
    vb_s = dscr("vb_s", [NSEQ, NT, 128, 512], BF16)
    wi_s = dscr("wi_s", [NSEQ, NT, 128, 4], F32)
    sg_s = dscr("sg_s", [T, 2048], BF16)
    pr_s = dscr("pr_s", [T // 128, 128, NFC * 128], BF16)

    with ExitStack() as es:
        P = Prog(nc, es)
        ident = P.sb(es, [128, 128], BF16)
        identf = P.sb(es, [128, 128], F32)
        caus01T = P.sb(es, [128, 128], BF16)
        negmask = P.sb(es, [128, 128], F32)
        cs_sb = P.sb(es, [128, NT, 64], F32)
        pow2 = P.sb(es, [128, KBIS + 1], F32)
        lam_sb = P.sb(es, [128, DEPTH], F32)
        nlam_sb = P.sb(es, [128, DEPTH], F32)
        ones_bf = P.sb(es, [128, 128], BF16)
        zer_f = P.sb(es, [128, 128], F32)

        with ExitStack() as ph:
            P.memset('pool', ident[:], 0.0, w=['ident'])
            P.op('pool', lambda e: e.affine_select(out=ident[:], in_=ident[:], pattern=[[-1, 128]],
                                                   compare_op=ALU.not_equal, fill=1.0, base=0,
                                                   channel_multiplier=1), r=['ident'], w=['ident'])
            P.memset('pool', identf[:], 0.0, w=['identf'])
            P.op('pool', lambda e: e.affine_select(out=identf[:], in_=identf[:], pattern=[[-1, 128]],
                                                   compare_op=ALU.not_equal, fill=1.0, base=0,
                                                   channel_multiplier=1), r=['identf'], w=['identf'])
            P.memset('pool', ones_bf[:], 1.0, w=['ones'])
            P.memset('pool', zer_f[:], 0.0, w=['zer'])
            P.op('pool', lambda e: e.affine_select(out=caus01T[:], in_=ones_bf[:], pattern=[[1, 128]],
                                                   compare_op=ALU.is_ge, fill=0.0, base=0,
                                                   channel_multiplier=-1), r=['ones'], w=['caus'])
            P.op('pool', lambda e: e.affine_select(out=negmask[:], in_=zer_f[:], pattern=[[-1, 128]],
                                                   compare_op=ALU.is_ge, fill=NEG, base=0,
                                                   channel_multiplier=1), r=['zer'], w=['negm'])
            P.dma('sp', cs_sb[:], cs_in[:, :, :], w=['cs'])
            for k in range(KBIS + 1):
                P.memset('dve', pow2[:, k:k + 1], float(2.0 ** (-k)), w=['pow2'])
            lt = {f: P.sb(ph, [128, 64], F32) for f in lamv}
            ltmp = P.sb(ph, [128, 64], F32)
            ld = P.sb(ph, [128, 4], F32)
            for l in range(DEPTH):
                lam_init = 0.8 - 0.6 * math.exp(-0.3 * l)
                for f in lamv:
                    P.dma('sp', lt[f][:], lamv[f][l:l + 1, :].to_broadcast([128, 64]), w=[('lt', f)])
                P.tt('dve', ltmp[:], lt['lam_q1'][:], lt['lam_k1'][:], ALU.mult, r=[('lt', 'lam_q1'), ('lt', 'lam_k1')], w=['ltmp'])
                P.red(ld[:, 0:1], ltmp[:], ALU.add, r=['ltmp'], w=['ld0'])
                P.tt('dve', ltmp[:], lt['lam_q2'][:], lt['lam_k2'][:], ALU.mult, r=[('lt', 'lam_q2'), ('lt', 'lam_k2')], w=['ltmp'])
                P.red(ld[:, 1:2], ltmp[:], ALU.add, r=['ltmp'], w=['ld1'])
                P.act(ld[:, 2:4], ld[:, 0:2], AF.Exp, r=['ld0', 'ld1'], w=['ld23'])
                P.stt('dve', lam_sb[:, l:l + 1], ld[:, 2:3], lam_init, ld[:, 3:4], ALU.add, ALU.subtract,
                      r=['ld23'], w=[('lam', l)])
                P.ts('dve', nlam_sb[:, l:l + 1], lam_sb[:, l:l + 1], -1.0, ALU.mult, r=[('lam', l)], w=[('nlam', l)])
            P.barrier()
            P.emit()

        for l in range(DEPTH):
            lam_init = 0.8 - 0.6 * math.exp(-0.3 * l)
            hsrc = x if l == 0 else h_s
            with ExitStack() as ph:
                w_sb = P.sb(ph, [128, 8, NIN], BF16)
                for k in range(8):
                    P.dma('pool', w_sb[:, k, :], w_in[l, k * 128:(k + 1) * 128, :], w=[('w', k)])
                gs_bc = P.sb(ph, [128, D], F32)
                P.dma('sp', gs_bc[:], g_mix[l:l + 1, :].to_broadcast([128, D]), w=['gs0'])
                P.ts('dve', gs_bc[:], gs_bc[:], float(math.sqrt(D)), ALU.mult, r=['gs0'], w=['gs'])
                tabs = {}
                for f in g_q:
                    g8 = P.sb(ph, [128, 64], F32)
                    P.dma('sp', g8[:], g_q[f][l:l + 1, :].to_broadcast([128, 64]), w=[('g8', f)])
                    P.ts('dve', g8[:], g8[:], 8.0, ALU.mult, r=[('g8', f)], w=[('g8s', f)])
                    tb = P.sb(ph, [128, NT, 4, 32], F32)
                    for j, (co, go) in enumerate(((0, 0), (32, 32), (0, 32), (32, 0))):
                        P.tt('dve', tb[:, :, j, :], cs_sb[:, :, co:co + 32], bc_mid(g8[:, go:go + 32], NT), ALU.mult,
                             r=['cs', ('g8s', f)], w=[('tab', f, j)])
                    tabs[f] = tb
                hR = Rot(P, ph, 3, [128, D], F32, 'h')
                uR = Rot(P, ph, 3, [128, D], BF16, 'u')
                uTR = Rot(P, ph, 3, [128, 8, 128], BF16, 'uT')
                ssR = Rot(P, ph, 3, [128, 2], F32, 'ss')
                junkA = P.sb(ph, [128, D], BF16)
                sqR = Rot(P, ph, 3, [128, 512], F32, 'sq')
                xnR = Rot(P, ph, 3, [128, 512], F32, 'xn')
                smR = Rot(P, ph, 4, [128, 16], F32, 'sm')
                tR = [Rot(P, ph, 3, [128, 8, 32], F32, 't%d' % i) for i in range(4)]
                rqR = Rot(P, ph, 3, [128, 16, 128], BF16, 'rq')
                stR = Rot(P, ph, 2, [128, 16, 128], BF16, 'stage')
                vaR = Rot(P, ph, 3, [128, 64], BF16, 'va')
                vbR = Rot(P, ph, 3, [128, 512], BF16, 'vb')
                wiR = Rot(P, ph, 3, [128, 4], F32, 'wi')
                sgR = Rot(P, ph, 3, [128, 2048], BF16, 'sg')
                ptrR = Rot(P, ph, 2, [128, 8, 128], BF16, 'ptr', psum=True)
                ppR = Rot(P, ph, 6, [128, 512], F32, 'pp', psum=True)

                def normrope(src, srckey, H, fam, pos, rq, rqkey, blk0, half0, normed):
                    sv = src.rearrange("p (h d) -> p h d", d=64)
                    if normed:
                        sq, sqk = sqR.next()
                        P.act(sq[:, 0:H * 64], src, AF.Square, r=[srckey], w=[sqk])
                        sm, smk = smR.next()
                        P.red(sm[:, 0:H], sq[:, 0:H * 64].rearrange("p (h d) -> p h d", d=64), ALU.add, r=[sqk], w=[smk])
                        P.rsqrt(sm[:, 8:8 + H], sm[:, 0:H], 64.0 * EPS, r=[smk], w=[smk])
                        xn, xnk = xnR.next()
                        xv = xn[:, 0:H * 64].rearrange("p (h d) -> p h d", d=64)
                        P.tt('dve', xv, sv, bc_last(sm[:, 8:8 + H], 64), ALU.mult, r=[srckey, smk], w=[xnk])
                        xk = xnk
                        tb = tabs[fam]
                        C1, S2, C2, S1 = (tb[:, pos, j, :] for j in range(4))
                        tr_ = [('tab', fam, j) for j in range(4)]
                    else:
                        xv, xk = sv, srckey
                        C1 = C2 = cs_sb[:, pos, 0:32]
                        S1 = S2 = cs_sb[:, pos, 32:64]
                        tr_ = ['cs'] * 4
                    x1 = xv[:, :, 0:32]
                    x2 = xv[:, :, 32:64]
                    ov = rq[:].rearrange("p b c -> p (b c)")[:, blk0 * 128 + half0: blk0 * 128 + half0 + H * 64]
                    ov = ov.rearrange("p (h d) -> p h d", d=64)
                    tb_ = [r_.next() for r_ in tR]
                    (t1, k1), (t2, k2), (t3, k3), (t4, k4) = tb_
                    P.tt('dve', t1[:, 0:H, :], x1, bc_mid(C1, H), ALU.mult, r=[xk, tr_[0]], w=[k1])
                    P.tt('dve', t2[:, 0:H, :], x2, bc_mid(S2, H), ALU.mult, r=[xk, tr_[1]], w=[k2])
                    P.tt('dve', ov[:, :, 0:32], t1[:, 0:H, :], t2[:, 0:H, :], ALU.subtract, r=[k1, k2], w=[rqkey])
                    P.tt('dve', t3[:, 0:H, :], x2, bc_mid(C2, H), ALU.mult, r=[xk, tr_[2]], w=[k3])
                    P.tt('dve', t4[:, 0:H, :], x1, bc_mid(S1, H), ALU.mult, r=[xk, tr_[3]], w=[k4])
                    P.tt('dve', ov[:, :, 32:64], t3[:, 0:H, :], t4[:, 0:H, :], ALU.add, r=[k3, k4], w=[rqkey])

                groups = [(0, 512), (512, 964), (964, 1476), (1476, 1988), (1988, 2500),
                          (2500, 3012), (3012, 3524), (3524, 4036), (4036, 4548)]
                def p1_tile(tt):
                    sq_i, pos = tt // NT, tt % NT
                    h_t, hk = hR.next()
                    P.dma('sp', h_t[:], hsrc[tt * 128:(tt + 1) * 128, :], w=[hk])
                    yield
                    ss, ssk = ssR.next()
                    P.act(junkA[:], h_t[:], AF.Square, accum=ss[:, 0:1], r=[hk], w=[ssk, 'junkA'])
                    P.rsqrt(ss[:, 1:2], ss[:, 0:1], D * EPS, r=[ssk], w=[ssk])
                    u, uk = uR.next()
                    P.stt('dve', u[:], h_t[:], ss[:, 1:2], gs_bc[:], ALU.mult, ALU.mult, r=[hk, ssk, 'gs'], w=[uk])
                    ptr, pk = ptrR.next()
                    for k in range(8):
                        P.tr(ptr[:, k, :], u[:, k * 128:(k + 1) * 128], ident[:], r=[uk, 'ident'], w=[pk])
                    uT, uTk = uTR.next()
                    P.cp('act', uT[:], ptr[:], r=[pk], w=[uTk])
                    rq, rqk = rqR.next()
                    sg, sgk = sgR.next()
                    yield
                    for gi, (c0, c1) in enumerate(groups):
                        pp, ppk = ppR.next()
                        for k in range(8):
                            P.mm(pp[:, 0:c1 - c0], uT[:, k, :], w_sb[:, k, c0:c1], k == 0, k == 7,
                                 r=[uTk, ('w', k)], w=[ppk])
                        if gi == 0:
                            normrope(pp[:, 0:512], ppk, 8, 'g_qa', pos, rq, rqk, 0, 0, True)
                        elif gi == 1:
                            normrope(pp[:, 0:64], ppk, 1, 'g_ka', pos, rq, rqk, 7, 0, True)
                            P.cp('dve', rq[:, 7, 64:128], rq[:, 7, 0:64], r=[rqk], w=[rqk])
                            va, vak = vaR.next()
                            P.cp('act', va[:], pp[:, 64:128], r=[ppk], w=[vak])
                            P.dma('sp', va_s[sq_i, pos, :, :], va[:], r=[vak])
                            normrope(pp[:, 128:448], ppk, 5, None, pos, rq, rqk, 4, 0, False)
                            P.cp('dve', rq[:, 6, 64:128], rq[:, 6, 0:64], r=[rqk], w=[rqk])
                            wi, wik = wiR.next()
                            P.ts('dve', wi[:], pp[:, 448:452], 1.0 / 16.0, ALU.mult, r=[ppk], w=[wik])
                            P.dma('sp', wi_s[sq_i, pos, :, :], wi[:], r=[wik])
                        elif gi == 2:
                            normrope(pp[:, 0:512], ppk, 8, 'g_qb', pos, rq, rqk, 8, 0, True)
                        elif gi == 3:
                            normrope(pp[:, 0:512], ppk, 8, 'g_kb', pos, rq, rqk, 12, 0, True)
                        elif gi == 4:
                            vb, vbk = vbR.next()
                            P.cp('act', vb[:], pp[:, 0:512], r=[ppk], w=[vbk])
                            P.dma('sp', vb_s[sq_i, pos, :, :], vb[:], r=[vbk])
                        else:
                            q = gi - 5
                            P.act(sg[:, q * 512:(q + 1) * 512], pp[:, 0:512], AF.Sigmoid, r=[ppk], w=[sgk])
                        if gi in (0, 1, 2, 3, 4, 6):
                            yield
                    P.dma('sp', sg_s[tt * 128:(tt + 1) * 128, :], sg[:], r=[sgk])
                    yield
                    stg, stk = stR.next()
                    for hb in range(2):
                        ptr, pk = ptrR.next()
                        for b in range(8):
                            P.tr(ptr[:, b, :], rq[:, hb * 8 + b, :], ident[:], r=[rqk, 'ident'], w=[pk])
                        P.cp('act', stg[:, hb * 8:(hb + 1) * 8, :], ptr[:], r=[pk], w=[stk])
                    P.dma('sp', fm_s[sq_i, :, :, pos * 128:(pos + 1) * 128], stg[:], r=[stk])
                    yield

                run_pipelined([(lambda tt=tt: p1_tile(tt)) for tt in range(NTT)], 3, 4)
                P.barrier()
                P.emit()
            if debug == 1:
                break
            with ExitStack() as ph:
                wA = P.sb(ph, [128, 4, D], BF16)
                wB = P.sb(ph, [128, 4, D], BF16)
                wO = P.sb(ph, [128, 8, D], BF16)
                for k in range(4):
                    P.dma('pool', wA[:, k, :], w_bra[l, k * 128:(k + 1) * 128, :], w=[('wA', k)])
                    P.dma('pool', wB[:, k, :], w_brb[l, k * 128:(k + 1) * 128, :], w=[('wB', k)])
                for k in range(8):
                    P.dma('pool', wO[:, k, :], w_out[l, k * 128:(k + 1) * 128, :], w=[('wO', k)])
                gsub = P.sb(ph, [128, 128], F32)
                P.dma('sp', gsub[:], g_sub[l:l + 1, :].to_broadcast([128, 128]), w=['gsub0'])
                P.ts('dve', gsub[:], gsub[:], float((1.0 - lam_init) * math.sqrt(128.0)), ALU.mult, r=['gsub0'], w=['gsub'])
                FM = P.sb(ph, [128, 16, L], BF16)
                vaA = P.sb(ph, [128, NT, 65], BF16)
                vbA = P.sb(ph, [128, NT, 4, 129], BF16)
                wiS = P.sb(ph, [128, NT, 4], F32)
                isc = P.sb(ph, [128, L], F32)
                Mk = P.sb(ph, [128, L], BF16)
                MTs = [P.sb(ph, [128, NT, 128], BF16) for _ in range(2)]
                rlR = Rot(P, ph, 3, [128, 512], F32, 'rl')
                bis = P.sb(ph, [128, 8], F32)
                stepsX = P.sb(ph, [128, KBIS + 1], F32)
                ER = Rot(P, ph, 5, [128, 4, 128], BF16, 'E')
                PR = Rot(P, ph, 4, [128, 4, 128], BF16, 'Pm')
                rsA = P.sb(ph, [128, 8], F32)
                rsB = P.sb(ph, [128, 16], F32)
                sB2 = P.sb(ph, [128, 8], F32)
                oa_bf = P.sb(ph, [128, 512], BF16)
                ob_f = P.sb(ph, [128, 4, 128], F32)
                ob_t = P.sb(ph, [128, 4, 128], F32)
                ob_bf = P.sb(ph, [128, 512], BF16)
                oT = P.sb(ph, [128, 8, 128], BF16)
                m1 = P.sb(ph, [128, D], F32)
                ob_sq = m1[:, 0:512]
                ob_n = m1[:, 512:1024].rearrange("p (h d) -> p h d", d=128)
                m2 = P.sb(ph, [128, D], F32)
                mix_bf = P.sb(ph, [128, D], BF16)
                mT = P.sb(ph, [128, 8, 128], BF16)
                hR = Rot(P, ph, 2, [128, D], F32, 'h')
                h2R = Rot(P, ph, 2, [128, D], F32, 'h2')
                sgR = Rot(P, ph, 2, [128, 2048], BF16, 'sg')
                ps0 = P.ps(ph, [128, 512], F32)
                ps0b = ps0[:].bitcast(BF16).rearrange("p (b c) -> p b c", c=128)
                SR = Rot(P, ph, 3, [128, 512], F32, 'S', psum=True)
                ps_OA = P.ps(ph, [128, 512], F32)
                ps_OB = P.ps(ph, [128, 3, 512], F32)

                def oa_ap(h):
                    if h < 7:
                        return ps_OA[:, h * 65:h * 65 + 65]
                    return ps_OB[:, 2, 258:323]

                def ob_ap(q):
                    return ps_OB[:, q // 3, (q % 3) * 129:(q % 3) * 129 + 129]

                def nX(j):
                    return 4 * ((128 * (j + 1) + 511) // 512) + 4 + (KBIS if j >= KT else 0)

                def nY(j):
                    return 4 + 2 * (j + 1) + 5

                def X(s_i, j):
                    S = 128 * (j + 1)
                    tsl = slice(j * 128, (j + 1) * 128)
                    MT = MTs[j % 2]
                    MTk = ('MT', j % 2)
                    steps_ = [(c0, min(512, S - c0), hh) for c0 in range(0, S, 512) for hh in range(4)]
                    pend = {}

                    def x_mm(k):
                        c0, cw, hh = steps_[k]
                        r0 = 64 * (hh % 2)
                        P.mm(ps0[:, 0:cw], FM[r0:r0 + 64, 4 + hh // 2, tsl], FM[r0:r0 + 64, 6, c0:c0 + cw],
                             True, True, r=[('FM', 4 + hh // 2), ('FM', 6)], w=['b0'])

                    def x_relu(k):
                        c0, cw, hh = steps_[k]
                        rl, rlk = rlR.next()
                        P.act(rl[:, 0:cw], ps0[:, 0:cw], AF.Relu, r=['b0'], w=[rlk])
                        pend[k] = (rl, rlk)

                    def x_acc(k):
                        c0, cw, hh = steps_[k]
                        rl, rlk = pend.pop(k)
                        if hh == 0:
                            P.ts('dve', isc[:, c0:c0 + cw], rl[:, 0:cw], wiS[:, j, 0:1], ALU.mult,
                                 r=[rlk, 'wiS'], w=['isc'])
                        else:
                            P.stt('dve', isc[:, c0:c0 + cw], rl[:, 0:cw], wiS[:, j, hh:hh + 1], isc[:, c0:c0 + cw],
                                  ALU.mult, ALU.add, r=[rlk, 'wiS', 'isc'], w=['isc'])

                    nst = len(steps_)
                    for k in range(nst + 2):
                        if 0 <= k - 1 < nst:
                            x_relu(k - 1)
                        if k < nst:
                            x_mm(k)
                        if 0 <= k - 2 < nst:
                            x_acc(k - 2)
                        if k < nst:
                            yield
                    yield
                    if j >= KT:
                        P.red(bis[:, 0:1], isc[:, 0:S], ALU.min, r=['isc'], w=['mn'])
                    P.tt('dve', isc[:, S - 128:S], isc[:, S - 128:S], negmask[:], ALU.add, r=['isc', 'negm', 'mn'], w=['isc'])
                    if j >= KT:
                        P.red(bis[:, 1:2], isc[:, 0:S], ALU.max, r=['isc'], w=['mx'])
                        P.tt('dve', bis[:, 2:3], bis[:, 1:2], bis[:, 0:1], ALU.subtract, r=['mn', 'mx'], w=['w0'])
                        P.ts('dve', stepsX[:], pow2[:], bis[:, 2:3], ALU.mult, r=['w0', 'pow2'], w=['steps'])
                        P.tt('dve', bis[:, 3:4], bis[:, 0:1], stepsX[:, 1:2], ALU.add, r=['mn', 'steps'], w=['mid'])
                        yield
                        for k in range(KBIS):
                            P.ts('dve', Mk[:, 0:S], isc[:, 0:S], bis[:, 3:4], ALU.is_ge, 0.0, ALU.add,
                                 accum=bis[:, 4:5], r=['isc', 'mid'], w=['cnt', 'Mk'])
                            P.ts('dve', bis[:, 5:6], bis[:, 4:5], float(KTOP) - 0.5, ALU.is_ge, -0.5, ALU.add,
                                 r=['cnt'], w=['sgn'])
                            P.stt('dve', bis[:, 3:4], bis[:, 5:6], stepsX[:, k + 1:k + 2], bis[:, 3:4], ALU.mult, ALU.add,
                                  r=['sgn', 'steps', 'mid'], w=['mid'])
                            yield
                        P.stt('dve', bis[:, 6:7], stepsX[:, KBIS:KBIS + 1], -0.5, bis[:, 3:4], ALU.mult, ALU.add,
                              r=['steps', 'mid'], w=['thr'])
                        P.ts('dve', Mk[:, 0:S], isc[:, 0:S], bis[:, 6:7], ALU.is_ge, r=['isc', 'thr'], w=['Mk'])
                    else:
                        yield
                        P.ts('dve', Mk[:, 0:S], isc[:, 0:S], -1.0e29, ALU.is_ge, r=['isc'], w=['Mk'])
                    yield
                    for i0 in range(0, j + 1, 8):
                        n8 = min(8, j + 1 - i0)
                        for ii in range(n8):
                            i = i0 + ii
                            P.tr(ps0b[:, ii, :], Mk[:, i * 128:(i + 1) * 128], ident[:], r=['Mk', 'ident'], w=['b0'])
                        P.cp('act', MT[:, i0:i0 + n8, :], ps0b[:, 0:n8, :], r=['b0'], w=[MTk])
                    yield

                def Y(s_i, j):
                    tt = s_i * NT + j
                    tsl = slice(j * 128, (j + 1) * 128)
                    MT = MTs[j % 2]
                    MTk = ('MT', j % 2)
                    sg, sgk = sgR.next()
                    P.dma('sp', sg[:], sg_s[tt * 128:(tt + 1) * 128, :], w=[sgk])
                    h_t, hk = hR.next()
                    P.dma('sp', h_t[:], hsrc[tt * 128:(tt + 1) * 128, :], w=[hk])
                    units = [(i, kind) for i in range(j + 1) for kind in range(4)]
                    N = len(units)
                    live = {}

                    Sof = {}

                    def st_qk(n):
                        i, kind = units[n]
                        ssl = slice(i * 128, (i + 1) * 128)
                        S_, Sk = SR.next()
                        if kind < 2:
                            r0 = 64 * kind
                            P.mm(S_[:].rearrange("p (a t) -> p a t", t=128), FM[r0:r0 + 64, 7, ssl], FM[r0:r0 + 64, 0:4, tsl],
                                 True, True, r=[('FM', 7), ('FM', 0), ('FM', 1), ('FM', 2), ('FM', 3)], w=[Sk])
                        else:
                            r0 = 64 * (kind - 2)
                            for hh in range(4):
                                P.mm(S_[:, hh * 128:(hh + 1) * 128], FM[r0:r0 + 64, 12 + hh, ssl], FM[r0:r0 + 64, 8 + hh, tsl],
                                     True, True, r=[('FM', 12 + hh), ('FM', 8 + hh)], w=[Sk])
                        Sof[n] = (S_, Sk)

                    def st_exp(n):
                        S_, Sk = Sof.pop(n)
                        E, Ek = ER.next()
                        P.act(E[:].rearrange("p a t -> p (a t)"), S_[:], AF.Exp, scale=0.125, r=[Sk], w=[Ek])
                        live[n] = [E, Ek, None, None]

                    def st_mask(n):
                        i, kind = units[n]
                        E, Ek = live[n][0], live[n][1]
                        if kind < 2:
                            Pm, Pk = PR.next()
                            P.tt('dve', Pm[:], E[:], bc_mid(MT[:, i, :], 4), ALU.mult, r=[Ek, MTk], w=[Pk])
                            live[n][2], live[n][3] = Pm, Pk
                        elif i == j:
                            P.tt('dve', E[:], E[:], bc_mid(caus01T[:], 4), ALU.mult, r=[Ek, 'caus'], w=[Ek])

                    def st_av(n):
                        i, kind = units[n]
                        E, Ek, Pm, Pk = live.pop(n)
                        if kind < 2:
                            for pr in range(4):
                                hh = 2 * pr + kind
                                P.mm(oa_ap(hh), Pm[:, pr, :], vaA[:, i, :], False, False, r=[Pk, 'vaA'],
                                     w=['OA', 'OB2'] if hh == 7 else ['OA'], acc0=True,
                                     first=(i == 0 and hh in (0, 7)))
                        else:
                            m_ = kind - 2
                            for hh in range(4):
                                P.mm(ob_ap(2 * hh + m_), E[:, hh, :], vbA[:, i, hh, :], False, False, r=[Ek, 'vbA'],
                                     w=['OB', 'OB2'] if hh == 3 else ['OB'], acc0=True,
                                     first=(i == 0 and m_ == 0 and hh in (0, 2)))

                    yield
                    for st in range(N + 3):
                        if st < N:
                            st_qk(st)
                        if 0 <= st - 1 < N:
                            st_exp(st - 1)
                        if 0 <= st - 2 < N:
                            st_mask(st - 2)
                        if 0 <= st - 3 < N:
                            st_av(st - 3)
                        if st % 2 == 1:
                            yield
                    yield
                    v7 = ps_OA[:, 0:455].rearrange("p (h c) -> p h c", c=65)
                    P.recip(rsA[:, 0:7], v7[:, :, 64], r=['OA'], w=['rsA'])
                    P.recip(rsA[:, 7:8], ps_OB[:, 2, 322:323], r=['OA', 'OB2'], w=['rsA'])
                    P.tt('dve', oa_bf[:, 0:448].rearrange("p (h d) -> p h d", d=64), v7[:, :, 0:64], bc_last(rsA[:, 0:7], 64),
                         ALU.mult, r=['OA', 'rsA'], w=['oa'])
                    P.ts('dve', oa_bf[:, 448:512], ps_OB[:, 2, 258:322], rsA[:, 7:8], ALU.mult, r=['OA', 'OB2', 'rsA'], w=['oa'])
                    yield
                    for b in range(3):
                        nq = 3 if b < 2 else 2
                        vq = ps_OB[:, b, 0:nq * 129].rearrange("p (q c) -> p q c", c=129)
                        P.recip(rsB[:, 3 * b:3 * b + nq], vq[:, :, 128], r=['OB', 'OB2'], w=['rsB'])
                    P.ts('dve', rsB[:, 8:16], rsB[:, 0:8], nlam_sb[:, l:l + 1], ALU.mult, r=['rsB', ('nlam', l)], w=['rsB2'])
                    for hh in range(4):
                        P.ts('dve', ob_t[:, hh, :], ob_ap(2 * hh)[:, 0:128], rsB[:, 2 * hh:2 * hh + 1], ALU.mult,
                             r=['OB', 'OB2', 'rsB'], w=['ob_t'])
                        P.stt('dve', ob_f[:, hh, :], ob_ap(2 * hh + 1)[:, 0:128], rsB[:, 8 + 2 * hh + 1:8 + 2 * hh + 2],
                              ob_t[:, hh, :], ALU.mult, ALU.add, r=['OB', 'OB2', 'rsB2', 'ob_t'], w=['ob_f'])
                    P.act(ob_sq, ob_f[:].rearrange("p h d -> p (h d)"), AF.Square, r=['ob_f'], w=['m1'])
                    P.red(sB2[:, 0:4], ob_sq.rearrange("p (h d) -> p h d", d=128), ALU.add, r=['m1'], w=['ssb'])
                    P.rsqrt(sB2[:, 4:8], sB2[:, 0:4], 128.0 * EPS, r=['ssb'], w=['rsb'])
                    P.tt('dve', ob_n, ob_f[:], bc_last(sB2[:, 4:8], 128), ALU.mult, r=['ob_f', 'rsb', 'm1'], w=['m1'])
                    P.tt('dve', ob_bf[:].rearrange("p (h d) -> p h d", d=128), ob_n, bc_mid(gsub[:], 4), ALU.mult,
                         r=['m1', 'gsub'], w=['ob'])
                    yield
                    S_, Sk = SR.next()
                    Sb = S_[:].bitcast(BF16).rearrange("p (b c) -> p b c", c=128)
                    for b in range(4):
                        P.tr(Sb[:, b, :], oa_bf[:, b * 128:(b + 1) * 128], ident[:], r=['oa', 'ident'], w=[Sk])
                        P.tr(Sb[:, 4 + b, :], ob_bf[:, b * 128:(b + 1) * 128], ident[:], r=['ob', 'ident'], w=[Sk])
                    P.cp('act', oT[:], Sb, r=[Sk], w=['oT'])
                    yield
                    for n2 in range(2):
                        nsl = slice(n2 * 512, (n2 + 1) * 512)
                        S_, Sk = SR.next()
                        for k in range(4):
                            P.mm(S_[:], oT[:, k, :], wA[:, k, nsl], k == 0, k == 3, r=['oT', ('wA', k)], w=[Sk])
                        P.tt('dve', m1[:, nsl], S_[:], sg[:, nsl], ALU.mult, r=[Sk, sgk], w=['m1'])
                    for n2 in range(2):
                        nsl = slice(n2 * 512, (n2 + 1) * 512)
                        S_, Sk = SR.next()
                        for k in range(4):
                            P.mm(S_[:], oT[:, 4 + k, :], wB[:, k, nsl], k == 0, k == 3, r=['oT', ('wB', k)], w=[Sk])
                        P.tt('dve', m2[:, nsl], S_[:], sg[:, 1024 + n2 * 512:1024 + (n2 + 1) * 512], ALU.mult,
                             r=[Sk, sgk], w=['m2'])
                    P.tt('dve', mix_bf[:], m1[:], m2[:], ALU.add, r=['m1', 'm2'], w=['mix'])
                    yield
                    S_, Sk = SR.next()
                    Sb = S_[:].bitcast(BF16).rearrange("p (b c) -> p b c", c=128)
                    for k in range(8):
                        P.tr(Sb[:, k, :], mix_bf[:, k * 128:(k + 1) * 128], ident[:], r=['mix', 'ident'], w=[Sk])
                    P.cp('act', mT[:], Sb, r=[Sk], w=['mT'])
                    h2, h2k = h2R.next()
                    for n2 in range(2):
                        nsl = slice(n2 * 512, (n2 + 1) * 512)
                        S_, Sk = SR.next()
                        for k in range(8):
                            P.mm(S_[:], mT[:, k, :], wO[:, k, nsl], k == 0, k == 7, r=['mT', ('wO', k)], w=[Sk])
                        P.tt('dve', h2[:, nsl], S_[:], h_t[:, nsl], ALU.add, r=[Sk, hk], w=[h2k])
                    P.dma('sp', h_s[tt * 128:(tt + 1) * 128, :], h2[:], r=[h2k])
                    yield

                def interleave(ga, na, gb, nb):
                    ia = ib = 0
                    while ga is not None or gb is not None:
                        pick_a = gb is None or (ga is not None and ia * nb <= ib * na)
                        if pick_a:
                            try:
                                next(ga)
                                ia += 1
                            except StopIteration:
                                ga = None
                        else:
                            try:
                                next(gb)
                                ib += 1
                            except StopIteration:
                                gb = None

                for s_i in range(NSEQ):
                    P.dma('sp', wiS[:], wi_s[s_i].rearrange("n p d -> p n d"), w=['wiS'])
                    for b in (4, 5, 6, 0, 1, 2, 3, 7, 8, 9, 10, 11, 12, 13, 14, 15):
                        P.dma('sp', FM[:, b, :], fm_s[s_i, :, b, :], w=[('FM', b)])
                    P.memset('pool', vaA[:], 1.0, w=['vaA'])
                    P.memset('pool', vbA[:], 1.0, w=['vbA'])
                    P.dma('sp', vaA[:, :, 0:64], va_s[s_i].rearrange("n p d -> p n d"), w=['vaA'])
                    for hh in range(4):
                        P.dma('sp', vbA[:, :, hh, 0:128], vb_s[s_i, :, :, hh * 128:(hh + 1) * 128].rearrange("n p d -> p n d"),
                              w=['vbA'])
                    for _ in X(s_i, 0):
                        pass
                    for j in range(NT):
                        gx = X(s_i, j + 1) if j + 1 < NT else None
                        interleave(Y(s_i, j), nY(j), gx, nX(j + 1) if j + 1 < NT else 1)
                P.barrier()
                P.emit()
            if debug == 2:
                break
            with ExitStack() as ph:
                wU = P.sb(ph, [128, 8, 2 * DFF], BF16)
                for k in range(8):
                    P.dma('pool', wU[:, k, :], w_up[l, k * 128:(k + 1) * 128, :], w=[('wU', k)])
                gs_bc = P.sb(ph, [128, D], F32)
                P.dma('sp', gs_bc[:], g_ffn[l:l + 1, :].to_broadcast([128, D]), w=['gs0'])
                P.ts('dve', gs_bc[:], gs_bc[:], float(math.sqrt(D)), ALU.mult, r=['gs0'], w=['gs'])
                cwr = P.sb(ph, [44, 4, 128], F32)
                for jj in range(3):
                    P.dma('sp', cwr[:, jj, :], conv_w[l, jj, :].rearrange("(c p) -> c p", p=128), w=['cwr'])
                P.dma('sp', cwr[:, 3, :], conv_b[l, :].rearrange("(c p) -> c p", p=128), w=['cwr'])
                cw = P.sb(ph, [128, 4, 44], F32)
                ps_c = P.ps(ph, [128, 4, 44], F32)
                for jj in range(4):
                    P.tr(ps_c[:, jj, :], cwr[:, jj, :], identf[0:44, 0:44], r=['cwr', 'identf'], w=['ps_c'])
                P.cp('dve', cw[:], ps_c[:], r=['ps_c'], w=['cw'])
                halo = P.sb(ph, [128, 44, 2], F32)
                hG = Rot(P, ph, 3, [128, D], F32, 'hG')
                uG = Rot(P, ph, 2, [128, D], BF16, 'uG')
                uTG = Rot(P, ph, 2, [128, 8, 512], BF16, 'uTG')
                ssR = Rot(P, ph, 4, [128, 2], F32, 'ss')
                junkA = P.sb(ph, [128, D], BF16)
                xsR = Rot(P, ph, 4, [128, 514], F32, 'xs')
                acR = Rot(P, ph, 4, [128, 512], F32, 'acc')
                glR = Rot(P, ph, 2, [128, 512], F32, 'gl')
                prR = Rot(P, ph, 2, [128, 4, NFC, 128], BF16, 'prod')
                ptrR = Rot(P, ph, 2, [128, 8, 128], BF16, 'ptr', psum=True)
                ppR = Rot(P, ph, 5, [128, 512], F32, 'pp', psum=True)

                uts = {}

                def p3a_pro(g):
                    uT, uTk = uTG.next()
                    uts[g] = (uT, uTk)
                    for a in range(4):
                        hg, hgk = hG.next()
                        P.dma('sp', hg[:], h_s[g * 512 + a * 128:g * 512 + (a + 1) * 128, :], w=[hgk])
                        ss, ssk = ssR.next()
                        P.act(junkA[:], hg[:], AF.Square, accum=ss[:, 0:1], r=[hgk], w=[ssk, 'junkA'])
                        P.rsqrt(ss[:, 1:2], ss[:, 0:1], D * EPS, r=[ssk], w=[ssk])
                        ug, ugk = uG.next()
                        P.stt('dve', ug[:], hg[:], ss[:, 1:2], gs_bc[:], ALU.mult, ALU.mult, r=[hgk, ssk, 'gs'], w=[ugk])
                        ptr, pk = ptrR.next()
                        for k in range(8):
                            P.tr(ptr[:, k, :], ug[:, k * 128:(k + 1) * 128], ident[:], r=[ugk, 'ident'], w=[pk])
                        P.cp('act', uT[:, :, a * 128:(a + 1) * 128], ptr[:], r=[pk], w=[uTk])
                        yield

                def p3a_q(g):
                    first = (g * 512) % L == 0
                    uT, uTk = uts.pop(g)
                    prod, prk = prR.next()
                    for q in range(NFC):
                        res = []
                        for fc in (q, q + NFC):
                            pp, ppk = ppR.next()
                            for k in range(8):
                                P.mm(pp[:], wU[:, k, fc * 128:(fc + 1) * 128], uT[:, k, :], k == 0, k == 7,
                                     r=[uTk, ('wU', k)], w=[ppk])
                            xs, xsk = xsR.next()
                            P.cp('act', xs[:, 2:514], pp[:], r=[ppk], w=[xsk])
                            if first:
                                P.memset('pool', xs[:, 0:2], 0.0, w=[xsk])
                            else:
                                P.cp('pool', xs[:, 0:2], halo[:, fc, :], r=[('halo', fc)], w=[xsk])
                            acc, ack = acR.next()
                            P.act(acc[:], pp[:], AF.Identity, bias=cw[:, 3, fc:fc + 1], scale=cw[:, 2, fc:fc + 1],
                                  r=[ppk, 'cw'], w=[ack])
                            P.stt('dve', acc[:], xs[:, 1:513], cw[:, 1, fc:fc + 1], acc[:], ALU.mult, ALU.add,
                                  r=[xsk, 'cw', ack], w=[ack])
                            P.stt('dve', acc[:], xs[:, 0:512], cw[:, 0, fc:fc + 1], acc[:], ALU.mult, ALU.add,
                                  r=[xsk, 'cw', ack], w=[ack])
                            P.cp('pool', halo[:, fc, :], xs[:, 512:514], r=[xsk], w=[('halo', fc)])
                            res.append((acc, ack))
                        gl, glk = glR.next()
                        P.act(gl[:], res[0][0][:], AF.Gelu_apprx_tanh, r=[res[0][1]], w=[glk])
                        P.tt('pool', prod[:, :, q, :], gl[:].rearrange("p (a t) -> p a t", t=128),
                             res[1][0][:].rearrange("p (a t) -> p a t", t=128), ALU.mult, r=[glk, res[1][1]], w=[prk])
                        yield
                    for a in range(4):
                        P.dma('sp', pr_s[g * 4 + a, :, :], prod[:, a, :, :].rearrange("p q t -> p (q t)"), r=[prk])
                    yield

                def interleave2(ga, na, gb, nb):
                    ia = ib = 0
                    while ga is not None or gb is not None:
                        pick_a = gb is None or (ga is not None and ia * nb <= ib * na)
                        if pick_a:
                            try:
                                next(ga)
                                ia += 1
                            except StopIteration:
                                ga = None
                        else:
                            try:
                                next(gb)
                                ib += 1
                            except StopIteration:
                                gb = None

                for _ in p3a_pro(0):
                    pass
                for g in range(NG):
                    interleave2(p3a_q(g), NFC + 1, p3a_pro(g + 1) if g + 1 < NG else None, 4)
                P.barrier()
                P.emit()
            if debug == 3:
                break
            with ExitStack() as ph:
                wD = P.sb(ph, [128, NFC, D], BF16)
                for fc in range(NFC):
                    P.dma('pool', wD[:, fc, :], w_down[l, fc * 128:(fc + 1) * 128, :], w=[('wD', fc)])
                wG = P.sb(ph, [128, 8, D], BF16)
                for k in range(8):
                    P.dma('pool', wG[:, k, :], w_pg[l, k * 128:(k + 1) * 128, :], w=[('wG', k)])
                wPp = P.sb(ph, [128, 2, D], BF16)
                for k in range(2):
                    P.dma('pool', wPp[:, k, :], w_pp[l, k * 128:(k + 1) * 128, :], w=[('wPp', k)])
                gs_bc = P.sb(ph, [128, D], F32)
                P.dma('sp', gs_bc[:], g_ple[l:l + 1, :].to_broadcast([128, D]), w=['gs0'])
                P.ts('dve', gs_bc[:], gs_bc[:], float(math.sqrt(D)), ALU.mult, r=['gs0'], w=['gs'])
                hR = Rot(P, ph, 5, [128, D], F32, 'h')
                h3R = Rot(P, ph, 5, [128, D], F32, 'h3')
                h4R = Rot(P, ph, 5, [128, D], F32, 'h4')
                prR = Rot(P, ph, 5, [128, NFC, 128], BF16, 'pr')
                pbR = Rot(P, ph, 5, [128, PLE], BF16, 'pb')
                pTR = Rot(P, ph, 5, [128, 2, 128], BF16, 'pT')
                ssR = Rot(P, ph, 5, [128, 2], F32, 'ss')
                uR = Rot(P, ph, 5, [128, D], BF16, 'u')
                uTR = Rot(P, ph, 5, [128, 8, 128], BF16, 'uT')
                sgR = Rot(P, ph, 2, [128, D], F32, 'sgt')
                tmR = Rot(P, ph, 2, [128, D], F32, 'tm')
                junkA = P.sb(ph, [128, D], BF16)
                psD = P.ps(ph, [128, 2, 512], F32)
                psG = P.ps(ph, [128, 2, 512], F32)
                psP = P.ps(ph, [128, 2, 512], F32)
                ptrR = Rot(P, ph, 2, [128, 8, 128], BF16, 'ptr', psum=True)
                hdst = y if l == DEPTH - 1 else h_s
                def p3b_tile(tt):
                    rows = slice(tt * 128, (tt + 1) * 128)
                    h_t, hk = hR.next()
                    P.dma('sp', h_t[:], h_s[rows, :], w=[hk])
                    pr, prk = prR.next()
                    P.dma('sp', pr[:].rearrange("p q t -> p (q t)"), pr_s[tt, :, :], w=[prk])
                    pb, pbk = pbR.next()
                    P.dma('pool', pb[:], p_in[l, rows, :], w=[pbk])
                    yield
                    h3, h3k = h3R.next()
                    for n2 in range(2):
                        nsl = slice(n2 * 512, (n2 + 1) * 512)
                        for fc in range(NFC):
                            P.mm(psD[:, n2, :], pr[:, fc, :], wD[:, fc, nsl], fc == 0, fc == NFC - 1,
                                 r=[prk, ('wD', fc)], w=[('psD', n2)])
                        P.tt('dve', h3[:, nsl], psD[:, n2, :], h_t[:, nsl], ALU.add, r=[('psD', n2), hk], w=[h3k])
                    ptr, pk = ptrR.next()
                    for k in range(2):
                        P.tr(ptr[:, k, :], pb[:, k * 128:(k + 1) * 128], ident[:], r=[pbk, 'ident'], w=[pk])
                    pT, pTk = pTR.next()
                    P.cp('act', pT[:], ptr[:, 0:2, :], r=[pk], w=[pTk])
                    ss, ssk = ssR.next()
                    P.act(junkA[:], h3[:], AF.Square, accum=ss[:, 0:1], r=[h3k], w=[ssk, 'junkA'])
                    yield
                    P.rsqrt(ss[:, 1:2], ss[:, 0:1], D * EPS, r=[ssk], w=[ssk])
                    u, uk = uR.next()
                    P.stt('dve', u[:], h3[:], ss[:, 1:2], gs_bc[:], ALU.mult, ALU.mult, r=[h3k, ssk, 'gs'], w=[uk])
                    yield
                    ptr, pk = ptrR.next()
                    for k in range(8):
                        P.tr(ptr[:, k, :], u[:, k * 128:(k + 1) * 128], ident[:], r=[uk, 'ident'], w=[pk])
                    uT, uTk = uTR.next()
                    P.cp('act', uT[:], ptr[:], r=[pk], w=[uTk])
                    yield
                    sgt, sgk = sgR.next()
                    tm, tmk = tmR.next()
                    h4, h4k = h4R.next()
                    for n2 in range(2):
                        nsl = slice(n2 * 512, (n2 + 1) * 512)
                        for k in range(8):
                            P.mm(psG[:, n2, :], uT[:, k, :], wG[:, k, nsl], k == 0, k == 7, r=[uTk, ('wG', k)], w=[('psG', n2)])
                        P.act(sgt[:, nsl], psG[:, n2, :], AF.Sigmoid, r=[('psG', n2)], w=[sgk])
                        for k in range(2):
                            P.mm(psP[:, n2, :], pT[:, k, :], wPp[:, k, nsl], k == 0, k == 1, r=[pTk, ('wPp', k)], w=[('psP', n2)])
                        P.tt('dve', tm[:, nsl], psP[:, n2, :], sgt[:, nsl], ALU.mult, r=[('psP', n2), sgk], w=[tmk])
                        P.tt('dve', h4[:, nsl], tm[:, nsl], h3[:, nsl], ALU.add, r=[tmk, h3k], w=[h4k])
                    P.dma('sp', hdst[rows, :], h4[:], r=[h4k])
                    yield

                run_pipelined([(lambda tt=tt: p3b_tile(tt)) for tt in range(NTT)], 5, 1)
                P.barrier()
                P.emit()
    return nc


def rope_table(L):
    NT = L // 128
    inv = 1.0 / (10000.0 ** (np.arange(0, 64, 2, dtype=np.float32) / np.float32(64.0)))
    ang = np.arange(L, dtype=np.float32)[:, None] * inv[None, :].astype(np.float32)
    cs = np.concatenate([np.cos(ang), np.sin(ang)], axis=1).astype(np.float32)
    return np.ascontiguousarray(cs.reshape(NT, 128, 64).transpose(1, 0, 2))


_CACHE = {}


def run(inputs, L, NSEQ, DEPTH, ncores, debug=False):
    key = (L, NSEQ, DEPTH, debug)
    if key not in _CACHE:
        _CACHE[key] = build(L, NSEQ, DEPTH, debug)
    nc = _CACHE[key]
    T = NSEQ * L
    xs = np.ascontiguousarray(inputs['x'], dtype=np.float32).reshape(ncores, T, D)
    ps = np.ascontiguousarray(inputs['p'], dtype=np.float32).reshape(DEPTH, ncores, T, PLE)
    cs = rope_table(L)
    in_maps = []
    for c in range(ncores):
        m = {k: np.ascontiguousarray(v, dtype=np.float32) for k, v in inputs.items() if k not in ('x', 'p')}
        m['x'] = xs[c]
        m['p'] = np.ascontiguousarray(ps[:, c])
        m['cs_tab'] = cs
        in_maps.append(m)
    res = run_bass_kernel_spmd(nc, in_maps, core_ids=list(range(ncores)))
    return res


def kernel(**inputs):
    B, L, _ = inputs['x'].shape
    DEPTH = inputs['p'].shape[0]
    ncores = 8
    NSEQ = B // ncores
    res = run(inputs, L, NSEQ, DEPTH, ncores)
    out = np.stack([np.asarray(r['y'], dtype=np.float32) for r in res.results], axis=0)
    return out.reshape(B, L, D)
```

```python
import math
import numpy as np
from contextlib import ExitStack
import concourse.bass as bass
import concourse.mybir as mybir
from concourse.bass_utils import run_bass_kernel_spmd

F32 = mybir.dt.float32
BF16 = mybir.dt.bfloat16
AF = mybir.ActivationFunctionType
ALU = mybir.AluOpType
AX = mybir.AxisListType

D = 1024
NIN = 4548
DFF = 2816
NFC = 22
PLE = 256
EPS = 1e-6
KBIS = 13
NEG = -1.0e30
COMPUTE = ('pe', 'act', 'dve', 'pool')
ENGS = ('sp', 'pool', 'act', 'dve', 'pe')


class Op(object):
    __slots__ = ('eng', 'fn', 'deps', 'dma', 'sem', 'semval', 'ms', 'waited')


class Prog(object):
    def __init__(self, nc, es, n_sp=16, n_pq=8):
        self.nc = nc
        self.esem = {e: es.enter_context(nc.semaphore('sem_' + e)) for e in COMPUTE}
        self.dsem = {'sp': [es.enter_context(nc.semaphore('dsp%d' % i)) for i in range(n_sp)],
                     'pool': [es.enter_context(nc.semaphore('dpq%d' % i)) for i in range(n_pq)]}
        self.duse = {q: [0] * len(v) for q, v in self.dsem.items()}
        self.dlast = {q: [None] * len(v) for q, v in self.dsem.items()}
        self.dnext = {q: 0 for q in self.dsem}
        self.mscount = {e: 0 for e in COMPUTE}
        self.waitedv = {e: {} for e in ENGS}
        self.state = {}
        self.ops = {e: [] for e in ENGS}
        self.last = {e: None for e in ENGS}
        self.cnt = 0

    def sb(self, ctx, shape, dt):
        self.cnt += 1
        return ctx.enter_context(self.nc.sbuf_tensor('sb%d' % self.cnt, list(shape), dt))

    def ps(self, ctx, shape, dt):
        self.cnt += 1
        return ctx.enter_context(self.nc.psum_tensor('ps%d' % self.cnt, list(shape), dt))

    def op(self, eng, fn, r=(), w=(), dma=False):
        o = Op()
        o.eng = eng; o.fn = fn; o.dma = dma; o.waited = False; o.ms = 0; o.sem = None; o.semval = 0
        deps = {}
        for k in r:
            st = self.state.get(k)
            if st is not None and st[0] is not None:
                deps[st[0]] = 'raw'
        for k in w:
            st = self.state.get(k)
            if st is not None:
                if st[0] is not None:
                    deps.setdefault(st[0], 'waw')
                for ro in st[1].values():
                    deps.setdefault(ro, 'war')
                for ro in st[2]:
                    deps.setdefault(ro, 'war')
        if dma:
            n = len(self.dsem[eng])
            slot = self.dnext[eng]
            self.dnext[eng] = (slot + 1) % n
            prev = self.dlast[eng][slot]
            if prev is not None:
                deps.setdefault(prev, 'slot')
            self.duse[eng][slot] += 1
            o.sem = self.dsem[eng][slot]
            o.semval = 16 * self.duse[eng][slot]
            self.dlast[eng][slot] = o
        fd = []
        for d, kind in deps.items():
            if d is o:
                continue
            if (not d.dma) and (not dma) and d.eng == eng:
                if eng == 'pe':
                    continue
            fd.append(d)
            d.waited = True
        o.deps = fd
        for k in r:
            st = self.state.setdefault(k, [None, {}, []])
            if dma:
                st[2].append(o)
            else:
                st[1][eng] = o
        for k in w:
            self.state[k] = [o, {}, []]
        self.ops[eng].append(o)
        if not dma:
            self.last[eng] = o
        return o

    def barrier(self):
        targets = [self.last[e] for e in COMPUTE if self.last[e] is not None]
        for q in self.dlast:
            targets += [d for d in self.dlast[q] if d is not None]
        for e in ENGS:
            o = Op()
            o.eng = e; o.fn = None; o.dma = False; o.waited = False; o.ms = 0; o.sem = None; o.semval = 0
            o.deps = [t for t in targets if t.dma or t.eng != e]
            for t in o.deps:
                t.waited = True
            self.ops[e].append(o)
        self.state = {}

    def emit(self):
        for e in COMPUTE:
            for o in self.ops[e]:
                if o.fn is not None and (not o.dma) and o.waited:
                    self.mscount[e] += 1
                    o.ms = self.mscount[e]
        nc = self.nc
        with nc.Block() as block:
            decos = {'sp': block.sync, 'pool': block.gpsimd, 'act': block.scalar, 'dve': block.vector,
                     'pe': block.tensor}
            for e in ENGS:
                ops = self.ops[e]
                if not ops:
                    continue

                def body(eo, ops=ops, e=e):
                    wt = self.waitedv[e]
                    for o in ops:
                        for d in o.deps:
                            if d.dma:
                                sem, val = d.sem, d.semval
                            else:
                                sem, val = self.esem[d.eng], d.ms
                            key = id(sem)
                            if wt.get(key, 0) < val:
                                eo.wait_ge(sem, val)
                                wt[key] = val
                        if o.fn is None:
                            continue
                        ins = o.fn(eo)
                        if o.dma:
                            ins.then_inc(o.sem, 16)
                        elif o.waited:
                            ins.then_inc(self.esem[e], 1)
                decos[e](body)
        self.ops = {e: [] for e in ENGS}

    def dma(self, q, out, in_, r=(), w=()):
        return self.op(q, lambda e: e.dma_start(out=out, in_=in_), r, w, dma=True)

    def tt(self, eng, out, in0, in1, op, r=(), w=()):
        return self.op(eng, lambda e: e.tensor_tensor(out=out, in0=in0, in1=in1, op=op), r, w)

    def ts(self, eng, out, in0, s1, op0, s2=None, op1=None, accum=None, r=(), w=()):
        if op1 is None:
            return self.op(eng, lambda e: e.tensor_scalar(out=out, in0=in0, scalar1=s1, scalar2=None, op0=op0), r, w)
        if accum is None:
            return self.op(eng, lambda e: e.tensor_scalar(out=out, in0=in0, scalar1=s1, scalar2=s2, op0=op0,
                                                           op1=op1), r, w)
        return self.op(eng, lambda e: e.tensor_scalar(out=out, in0=in0, scalar1=s1, scalar2=s2, op0=op0,
                                                       op1=op1, accum_out=accum), r, w)

    def stt(self, eng, out, in0, scalar, in1, op0, op1, r=(), w=()):
        return self.op(eng, lambda e: e.scalar_tensor_tensor(out=out, in0=in0, scalar=scalar, in1=in1,
                                                              op0=op0, op1=op1), r, w)

    def act(self, out, in_, func, bias=None, scale=None, accum=None, r=(), w=()):
        kw = {}
        if bias is not None:
            kw['bias'] = bias
        if scale is not None:
            kw['scale'] = scale
        if accum is not None:
            kw['accum_out'] = accum
        return self.op('act', lambda e: e.activation(out=out, in_=in_, func=func, **kw), r, w)

    def cp(self, eng, out, in_, r=(), w=()):
        if eng == 'act':
            return self.op('act', lambda e: e.copy(out=out, in_=in_), r, w)
        return self.op(eng, lambda e: e.tensor_copy(out=out, in_=in_), r, w)

    def mm(self, out, lhsT, rhs, start, stop, r=(), w=(), acc0=False, first=False):
        if acc0:
            return self.op('pe', lambda e: e.matmul(out, lhsT=lhsT, rhs=rhs, start=first, stop=False,
                                                    skip_group_check=True), r, w)
        return self.op('pe', lambda e: e.matmul(out, lhsT=lhsT, rhs=rhs, start=start, stop=stop), r, w)

    def tr(self, out, in_, ident, r=(), w=()):
        return self.op('pe', lambda e: e.transpose(out=out, in_=in_, identity=ident), r, w)

    def red(self, out, in_, op, r=(), w=()):
        return self.op('dve', lambda e: e.tensor_reduce(out=out, in_=in_, axis=AX.X, op=op), r, w)

    def recip(self, out, in_, r=(), w=()):
        return self.op('dve', lambda e: e.reciprocal(out=out, in_=in_), r, w)

    def rsqrt(self, out, in_, addc, r=(), w=()):
        self.op('act', lambda e: e.activation(out=out, in_=in_, func=AF.Sqrt, bias=addc, scale=1.0), r, w)
        return self.op('dve', lambda e: e.reciprocal(out=out, in_=out), w, w)

    def memset(self, eng, ap, val, r=(), w=()):
        return self.op(eng, lambda e: e.memset(ap, val), r, w)


class Rot(object):
    def __init__(self, P, ctx, n, shape, dt, name, psum=False):
        self.bufs = [(P.ps if psum else P.sb)(ctx, shape, dt) for _ in range(n)]
        self.name = name
        self.i = -1

    def next(self):
        self.i = (self.i + 1) % len(self.bufs)
        return self.bufs[self.i], (self.name, self.i)


def run_pipelined(makers, depth, period=1):
    active = []
    idx = 0
    rnd = 0
    while idx < len(makers) or active:
        if idx < len(makers) and len(active) < depth and (rnd % period == 0 or not active):
            active.append(makers[idx]())
            idx += 1
        for g in list(active):
            try:
                next(g)
            except StopIteration:
                active.remove(g)
        rnd += 1


def bc_mid(ap, n):
    return ap.unsqueeze(1).to_broadcast([ap.shape[0], n, ap.shape[1]])


def bc_last(ap, n):
    return ap.unsqueeze(2).to_broadcast([ap.shape[0], ap.shape[1], n])


def build(L, NSEQ, DEPTH, debug=False):
    NT = L // 128
    T = NSEQ * L
    NTT = T // 128
    KTOP = min(256, L // 4)
    KT = KTOP // 128
    NG = T // 512
    nc = bass.Bass("TRN2", target_bir_lowering=False)

    def din(name, shape, dt=F32):
        return nc.dram_tensor(name, list(shape), dt, kind="ExternalInput").ap()

    x = din("x", [T, D])
    p_in = din("p", [DEPTH, T, PLE])
    g_mix = din("g_mix_norm", [DEPTH, D])
    w_in = din("w_in", [DEPTH, D, NIN])
    g_q = {f: din(f, [DEPTH, 64]) for f in ("g_qa", "g_ka", "g_qb", "g_kb")}
    lamv = {f: din(f, [DEPTH, 64]) for f in ("lam_q1", "lam_k1", "lam_q2", "lam_k2")}
    g_sub = din("g_subln", [DEPTH, 128])
    w_bra = din("w_branch_a", [DEPTH, 512, D])
    w_brb = din("w_branch_b", [DEPTH, 512, D])
    w_out = din("w_out", [DEPTH, D, D])
    g_ffn = din("g_ffn_norm", [DEPTH, D])
    w_up = din("w_up", [DEPTH, D, 2 * DFF])
    conv_w = din("conv_w", [DEPTH, 3, 2 * DFF])
    conv_b = din("conv_b", [DEPTH, 2 * DFF])
    w_down = din("w_down", [DEPTH, DFF, D])
    g_ple = din("g_ple_norm", [DEPTH, D])
    w_pg = din("w_ple_gate", [DEPTH, D, D])
    w_pp = din("w_ple_proj", [DEPTH, PLE, D])
    cs_in = din("cs_tab", [128, NT, 64])
    y = nc.dram_tensor("y", [T, D], F32, kind="ExternalOutput").ap()

    skind = "ExternalOutput" if debug else "Internal"

    def dscr(name, shape, dt):
        return nc.dram_tensor(name, list(shape), dt, kind=skind).ap()

    h_s = dscr("h_s", [T, D], F32)
    fm_s = dscr("fm_s", [NSEQ, 128, 16, L], BF16)
    va_s = dscr("va_s", [NSEQ, NT, 128, 64], BF16)
    vb_s = dscr("vb_s", [NSEQ, NT, 128, 512], BF16)
    wi_s = dscr("wi_s", [NSEQ, NT, 128, 4], F32)
    sg_s = dscr("sg_s", [T, 2048], BF16)
    pr_s = dscr("pr_s", [T // 128, 128, NFC * 128], BF16)

    with ExitStack() as es:
        P = Prog(nc, es)
        ident = P.sb(es, [128, 128], BF16)
        identf = P.sb(es, [128, 128], F32)
        caus01T = P.sb(es, [128, 128], BF16)
        negmask = P.sb(es, [128, 128], F32)
        cs_sb = P.sb(es, [128, NT, 64], F32)
        pow2 = P.sb(es, [128, KBIS + 1], F32)
        lam_sb = P.sb(es, [128, DEPTH], F32)
        nlam_sb = P.sb(es, [128, DEPTH], F32)
        ones_bf = P.sb(es, [128, 128], BF16)
        zer_f = P.sb(es, [128, 128], F32)

        with ExitStack() as ph:
            P.memset('pool', ident[:], 0.0, w=['ident'])
            P.op('pool', lambda e: e.affine_select(out=ident[:], in_=ident[:], pattern=[[-1, 128]],
                                                   compare_op=ALU.not_equal, fill=1.0, base=0,
                                                   channel_multiplier=1), r=['ident'], w=['ident'])
            P.memset('pool', identf[:], 0.0, w=['identf'])
            P.op('pool', lambda e: e.affine_select(out=identf[:], in_=identf[:], pattern=[[-1, 128]],
                                                   compare_op=ALU.not_equal, fill=1.0, base=0,
                                                   channel_multiplier=1), r=['identf'], w=['identf'])
            P.memset('pool', ones_bf[:], 1.0, w=['ones'])
            P.memset('pool', zer_f[:], 0.0, w=['zer'])
            P.op('pool', lambda e: e.affine_select(out=caus01T[:], in_=ones_bf[:], pattern=[[1, 128]],
                                                   compare_op=ALU.is_ge, fill=0.0, base=0,
                                                   channel_multiplier=-1), r=['ones'], w=['caus'])
            P.op('pool', lambda e: e.affine_select(out=negmask[:], in_=zer_f[:], pattern=[[-1, 128]],
                                                   compare_op=ALU.is_ge, fill=NEG, base=0,
                                                   channel_multiplier=1), r=['zer'], w=['negm'])
            P.dma('sp', cs_sb[:], cs_in[:, :, :], w=['cs'])
            for k in range(KBIS + 1):
                P.memset('dve', pow2[:, k:k + 1], float(2.0 ** (-k)), w=['pow2'])
            lt = {f: P.sb(ph, [128, 64], F32) for f in lamv}
            ltmp = P.sb(ph, [128, 64], F32)
            ld = P.sb(ph, [128, 4], F32)
            for l in range(DEPTH):
                lam_init = 0.8 - 0.6 * math.exp(-0.3 * l)
                for f in lamv:
                    P.dma('sp', lt[f][:], lamv[f][l:l + 1, :].to_broadcast([128, 64]), w=[('lt', f)])
                P.tt('dve', ltmp[:], lt['lam_q1'][:], lt['lam_k1'][:], ALU.mult, r=[('lt', 'lam_q1'), ('lt', 'lam_k1')], w=['ltmp'])
                P.red(ld[:, 0:1], ltmp[:], ALU.add, r=['ltmp'], w=['ld0'])
                P.tt('dve', ltmp[:], lt['lam_q2'][:], lt['lam_k2'][:], ALU.mult, r=[('lt', 'lam_q2'), ('lt', 'lam_k2')], w=['ltmp'])
                P.red(ld[:, 1:2], ltmp[:], ALU.add, r=['ltmp'], w=['ld1'])
                P.act(ld[:, 2:4], ld[:, 0:2], AF.Exp, r=['ld0', 'ld1'], w=['ld23'])
                P.stt('dve', lam_sb[:, l:l + 1], ld[:, 2:3], lam_init, ld[:, 3:4], ALU.add, ALU.subtract,
                      r=['ld23'], w=[('lam', l)])
                P.ts('dve', nlam_sb[:, l:l + 1], lam_sb[:, l:l + 1], -1.0, ALU.mult, r=[('lam', l)], w=[('nlam', l)])
            P.barrier()
            P.emit()

        for l in range(DEPTH):
            lam_init = 0.8 - 0.6 * math.exp(-0.3 * l)
            hsrc = x if l == 0 else h_s
            with ExitStack() as ph:
                w_sb = P.sb(ph, [128, 8, NIN], BF16)
                for k in range(8):
                    P.dma('pool', w_sb[:, k, :], w_in[l, k * 128:(k + 1) * 128, :], w=[('w', k)])
                gs_bc = P.sb(ph, [128, D], F32)
                P.dma('sp', gs_bc[:], g_mix[l:l + 1, :].to_broadcast([128, D]), w=['gs0'])
                P.ts('dve', gs_bc[:], gs_bc[:], float(math.sqrt(D)), ALU.mult, r=['gs0'], w=['gs'])
                tabs = {}
                for f in g_q:
                    g8 = P.sb(ph, [128, 64], F32)
                    P.dma('sp', g8[:], g_q[f][l:l + 1, :].to_broadcast([128, 64]), w=[('g8', f)])
                    P.ts('dve', g8[:], g8[:], 8.0, ALU.mult, r=[('g8', f)], w=[('g8s', f)])
                    tb = P.sb(ph, [128, NT, 4, 32], F32)
                    for j, (co, go) in enumerate(((0, 0), (32, 32), (0, 32), (32, 0))):
                        P.tt('dve', tb[:, :, j, :], cs_sb[:, :, co:co + 32], bc_mid(g8[:, go:go + 32], NT), ALU.mult,
                             r=['cs', ('g8s', f)], w=[('tab', f, j)])
                    tabs[f] = tb
                hR = Rot(P, ph, 3, [128, D], F32, 'h')
                uR = Rot(P, ph, 3, [128, D], BF16, 'u')
                uTR = Rot(P, ph, 3, [128, 8, 128], BF16, 'uT')
                ssR = Rot(P, ph, 3, [128, 2], F32, 'ss')
                junkA = P.sb(ph, [128, D], BF16)
                sqR = Rot(P, ph, 3, [128, 512], F32, 'sq')
                xnR = Rot(P, ph, 3, [128, 512], F32, 'xn')
                smR = Rot(P, ph, 4, [128, 16], F32, 'sm')
                tR = [Rot(P, ph, 3, [128, 8, 32], F32, 't%d' % i) for i in range(4)]
                rqR = Rot(P, ph, 3, [128, 16, 128], BF16, 'rq')
                stR = Rot(P, ph, 2, [128, 16, 128], BF16, 'stage')
                vaR = Rot(P, ph, 3, [128, 64], BF16, 'va')
                vbR = Rot(P, ph, 3, [128, 512], BF16, 'vb')
                wiR = Rot(P, ph, 3, [128, 4], F32, 'wi')
                sgR = Rot(P, ph, 3, [128, 2048], BF16, 'sg')
                ptrR = Rot(P, ph, 2, [128, 8, 128], BF16, 'ptr', psum=True)
                ppR = Rot(P, ph, 6, [128, 512], F32, 'pp', psum=True)

                def normrope(src, srckey, H, fam, pos, rq, rqkey, blk0, half0, normed):
                    sv = src.rearrange("p (h d) -> p h d", d=64)
                    if normed:
                        sq, sqk = sqR.next()
                        P.act(sq[:, 0:H * 64], src, AF.Square, r=[srckey], w=[sqk])
                        sm, smk = smR.next()
                        P.red(sm[:, 0:H], sq[:, 0:H * 64].rearrange("p (h d) -> p h d", d=64), ALU.add, r=[sqk], w=[smk])
                        P.rsqrt(sm[:, 8:8 + H], sm[:, 0:H], 64.0 * EPS, r=[smk], w=[smk])
                        xn, xnk = xnR.next()
                        xv = xn[:, 0:H * 64].rearrange("p (h d) -> p h d", d=64)
                        P.tt('dve', xv, sv, bc_last(sm[:, 8:8 + H], 64), ALU.mult, r=[srckey, smk], w=[xnk])
                        xk = xnk
                        tb = tabs[fam]
                        C1, S2, C2, S1 = (tb[:, pos, j, :] for j in range(4))
                        tr_ = [('tab', fam, j) for j in range(4)]
                    else:
                        xv, xk = sv, srckey
                        C1 = C2 = cs_sb[:, pos, 0:32]
                        S1 = S2 = cs_sb[:, pos, 32:64]
                        tr_ = ['cs'] * 4
                    x1 = xv[:, :, 0:32]
                    x2 = xv[:, :, 32:64]
                    ov = rq[:].rearrange("p b c -> p (b c)")[:, blk0 * 128 + half0: blk0 * 128 + half0 + H * 64]
                    ov = ov.rearrange("p (h d) -> p h d", d=64)
                    tb_ = [r_.next() for r_ in tR]
                    (t1, k1), (t2, k2), (t3, k3), (t4, k4) = tb_
                    P.tt('dve', t1[:, 0:H, :], x1, bc_mid(C1, H), ALU.mult, r=[xk, tr_[0]], w=[k1])
                    P.tt('dve', t2[:, 0:H, :], x2, bc_mid(S2, H), ALU.mult, r=[xk, tr_[1]], w=[k2])
                    P.tt('dve', ov[:, :, 0:32], t1[:, 0:H, :], t2[:, 0:H, :], ALU.subtract, r=[k1, k2], w=[rqkey])
                    P.tt('dve', t3[:, 0:H, :], x2, bc_mid(C2, H), ALU.mult, r=[xk, tr_[2]], w=[k3])
                    P.tt('dve', t4[:, 0:H, :], x1, bc_mid(S1, H), ALU.mult, r=[xk, tr_[3]], w=[k4])
                    P.tt('dve', ov[:, :, 32:64], t3[:, 0:H, :], t4[:, 0:H, :], ALU.add, r=[k3, k4], w=[rqkey])

                groups = [(0, 512), (512, 964), (964, 1476), (1476, 1988), (1988, 2500),
                          (2500, 3012), (3012, 3524), (3524, 4036), (4036, 4548)]
                def p1_tile(tt):
                    sq_i, pos = tt // NT, tt % NT
                    h_t, hk = hR.next()
                    P.dma('sp', h_t[:], hsrc[tt * 128:(tt + 1) * 128, :], w=[hk])
                    yield
                    ss, ssk = ssR.next()
                    P.act(junkA[:], h_t[:], AF.Square, accum=ss[:, 0:1], r=[hk], w=[ssk, 'junkA'])
                    P.rsqrt(ss[:, 1:2], ss[:, 0:1], D * EPS, r=[ssk], w=[ssk])
                    u, uk = uR.next()
                    P.stt('dve', u[:], h_t[:], ss[:, 1:2], gs_bc[:], ALU.mult, ALU.mult, r=[hk, ssk, 'gs'], w=[uk])
                    ptr, pk = ptrR.next()
                    for k in range(8):
                        P.tr(ptr[:, k, :], u[:, k * 128:(k + 1) * 128], ident[:], r=[uk, 'ident'], w=[pk])
                    uT, uTk = uTR.next()
                    P.cp('act', uT[:], ptr[:], r=[pk], w=[uTk])
                    rq, rqk = rqR.next()
                    sg, sgk = sgR.next()
                    yield
                    for gi, (c0, c1) in enumerate(groups):
                        pp, ppk = ppR.next()
                        for k in range(8):
                            P.mm(pp[:, 0:c1 - c0], uT[:, k, :], w_sb[:, k, c0:c1], k == 0, k == 7,
                                 r=[uTk, ('w', k)], w=[ppk])
                        if gi == 0:
                            normrope(pp[:, 0:512], ppk, 8, 'g_qa', pos, rq, rqk, 0, 0, True)
                        elif gi == 1:
                            normrope(pp[:, 0:64], ppk, 1, 'g_ka', pos, rq, rqk, 7, 0, True)
                            P.cp('dve', rq[:, 7, 64:128], rq[:, 7, 0:64], r=[rqk], w=[rqk])
                            va, vak = vaR.next()
                            P.cp('act', va[:], pp[:, 64:128], r=[ppk], w=[vak])
                            P.dma('sp', va_s[sq_i, pos, :, :], va[:], r=[vak])
                            normrope(pp[:, 128:448], ppk, 5, None, pos, rq, rqk, 4, 0, False)
                            P.cp('dve', rq[:, 6, 64:128], rq[:, 6, 0:64], r=[rqk], w=[rqk])
                            wi, wik = wiR.next()
                            P.ts('dve', wi[:], pp[:, 448:452], 1.0 / 16.0, ALU.mult, r=[ppk], w=[wik])
                            P.dma('sp', wi_s[sq_i, pos, :, :], wi[:], r=[wik])
                        elif gi == 2:
                            normrope(pp[:, 0:512], ppk, 8, 'g_qb', pos, rq, rqk, 8, 0, True)
                        elif gi == 3:
                            normrope(pp[:, 0:512], ppk, 8, 'g_kb', pos, rq, rqk, 12, 0, True)
                        elif gi == 4:
                            vb, vbk = vbR.next()
                            P.cp('act', vb[:], pp[:, 0:512], r=[ppk], w=[vbk])
                            P.dma('sp', vb_s[sq_i, pos, :, :], vb[:], r=[vbk])
                        else:
                            q = gi - 5
                            P.act(sg[:, q * 512:(q + 1) * 512], pp[:, 0:512], AF.Sigmoid, r=[ppk], w=[sgk])
                        if gi in (0, 1, 2, 3, 4, 6):
                            yield
                    P.dma('sp', sg_s[tt * 128:(tt + 1) * 128, :], sg[:], r=[sgk])
                    yield
                    stg, stk = stR.next()
                    for hb in range(2):
                        ptr, pk = ptrR.next()
                        for b in range(8):
                            P.tr(ptr[:, b, :], rq[:, hb * 8 + b, :], ident[:], r=[rqk, 'ident'], w=[pk])
                        P.cp('act', stg[:, hb * 8:(hb + 1) * 8, :], ptr[:], r=[pk], w=[stk])
                    P.dma('sp', fm_s[sq_i, :, :, pos * 128:(pos + 1) * 128], stg[:], r=[stk])
                    yield

                run_pipelined([(lambda tt=tt: p1_tile(tt)) for tt in range(NTT)], 3, 4)
                P.barrier()
                P.emit()
            if debug == 1:
                break
            with ExitStack() as ph:
                wA = P.sb(ph, [128, 4, D], BF16)
                wB = P.sb(ph, [128, 4, D], BF16)
                wO = P.sb(ph, [128, 8, D], BF16)
                for k in range(4):
                    P.dma('pool', wA[:, k, :], w_bra[l, k * 128:(k + 1) * 128, :], w=[('wA', k)])
                    P.dma('pool', wB[:, k, :], w_brb[l, k * 128:(k + 1) * 128, :], w=[('wB', k)])
                for k in range(8):
                    P.dma('pool', wO[:, k, :], w_out[l, k * 128:(k + 1) * 128, :], w=[('wO', k)])
                gsub = P.sb(ph, [128, 128], F32)
                P.dma('sp', gsub[:], g_sub[l:l + 1, :].to_broadcast([128, 128]), w=['gsub0'])
                P.ts('dve', gsub[:], gsub[:], float((1.0 - lam_init) * math.sqrt(128.0)), ALU.mult, r=['gsub0'], w=['gsub'])
                FM = P.sb(ph, [128, 16, L], BF16)
                vaA = P.sb(ph, [128, NT, 65], BF16)
                vbA = P.sb(ph, [128, NT, 4, 129], BF16)
                wiS = P.sb(ph, [128, NT, 4], F32)
                isc = P.sb(ph, [128, L], F32)
                Mk = P.sb(ph, [128, L], BF16)
                MTs = [P.sb(ph, [128, NT, 128], BF16) for _ in range(2)]
                rlR = Rot(P, ph, 3, [128, 512], F32, 'rl')
                bis = P.sb(ph, [128, 8], F32)
                stepsX = P.sb(ph, [128, KBIS + 1], F32)
                ER = Rot(P, ph, 5, [128, 4, 128], BF16, 'E')
                PR = Rot(P, ph, 4, [128, 4, 128], BF16, 'Pm')
                rsA = P.sb(ph, [128, 8], F32)
                rsB = P.sb(ph, [128, 16], F32)
                sB2 = P.sb(ph, [128, 8], F32)
                oa_bf = P.sb(ph, [128, 512], BF16)
                ob_f = P.sb(ph, [128, 4, 128], F32)
                ob_t = P.sb(ph, [128, 4, 128], F32)
                ob_bf = P.sb(ph, [128, 512], BF16)
                oT = P.sb(ph, [128, 8, 128], BF16)
                m1 = P.sb(ph, [128, D], F32)
                ob_sq = m1[:, 0:512]
                ob_n = m1[:, 512:1024].rearrange("p (h d) -> p h d", d=128)
                m2 = P.sb(ph, [128, D], F32)
                mix_bf = P.sb(ph, [128, D], BF16)
                mT = P.sb(ph, [128, 8, 128], BF16)
                hR = Rot(P, ph, 2, [128, D], F32, 'h')
                h2R = Rot(P, ph, 2, [128, D], F32, 'h2')
                sgR = Rot(P, ph, 2, [128, 2048], BF16, 'sg')
                ps0 = P.ps(ph, [128, 512], F32)
                ps0b = ps0[:].bitcast(BF16).rearrange("p (b c) -> p b c", c=128)
                SR = Rot(P, ph, 3, [128, 512], F32, 'S', psum=True)
                ps_OA = P.ps(ph, [128, 512], F32)
                ps_OB = P.ps(ph, [128, 3, 512], F32)

                def oa_ap(h):
                    if h < 7:
                        return ps_OA[:, h * 65:h * 65 + 65]
                    return ps_OB[:, 2, 258:323]

                def ob_ap(q):
                    return ps_OB[:, q // 3, (q % 3) * 129:(q % 3) * 129 + 129]

                def nX(j):
                    return 4 * ((128 * (j + 1) + 511) // 512) + 4 + (KBIS if j >= KT else 0)

                def nY(j):
                    return 4 + 2 * (j + 1) + 5

                def X(s_i, j):
                    S = 128 * (j + 1)
                    tsl = slice(j * 128, (j + 1) * 128)
                    MT = MTs[j % 2]
                    MTk = ('MT', j % 2)
                    steps_ = [(c0, min(512, S - c0), hh) for c0 in range(0, S, 512) for hh in range(4)]
                    pend = {}

                    def x_mm(k):
                        c0, cw, hh = steps_[k]
                        r0 = 64 * (hh % 2)
                        P.mm(ps0[:, 0:cw], FM[r0:r0 + 64, 4 + hh // 2, tsl], FM[r0:r0 + 64, 6, c0:c0 + cw],
                             True, True, r=[('FM', 4 + hh // 2), ('FM', 6)], w=['b0'])

                    def x_relu(k):
                        c0, cw, hh = steps_[k]
                        rl, rlk = rlR.next()
                        P.act(rl[:, 0:cw], ps0[:, 0:cw], AF.Relu, r=['b0'], w=[rlk])
                        pend[k] = (rl, rlk)

                    def x_acc(k):
                        c0, cw, hh = steps_[k]
                        rl, rlk = pend.pop(k)
                        if hh == 0:
                            P.ts('dve', isc[:, c0:c0 + cw], rl[:, 0:cw], wiS[:, j, 0:1], ALU.mult,
                                 r=[rlk, 'wiS'], w=['isc'])
                        else:
                            P.stt('dve', isc[:, c0:c0 + cw], rl[:, 0:cw], wiS[:, j, hh:hh + 1], isc[:, c0:c0 + cw],
                                  ALU.mult, ALU.add, r=[rlk, 'wiS', 'isc'], w=['isc'])

                    nst = len(steps_)
                    for k in range(nst + 2):
                        if 0 <= k - 1 < nst:
                            x_relu(k - 1)
                        if k < nst:
                            x_mm(k)
                        if 0 <= k - 2 < nst:
                            x_acc(k - 2)
                        if k < nst:
                            yield
                    yield
                    if j >= KT:
                        P.red(bis[:, 0:1], isc[:, 0:S], ALU.min, r=['isc'], w=['mn'])
                    P.tt('dve', isc[:, S - 128:S], isc[:, S - 128:S], negmask[:], ALU.add, r=['isc', 'negm', 'mn'], w=['isc'])
                    if j >= KT:
                        P.red(bis[:, 1:2], isc[:, 0:S], ALU.max, r=['isc'], w=['mx'])
                        P.tt('dve', bis[:, 2:3], bis[:, 1:2], bis[:, 0:1], ALU.subtract, r=['mn', 'mx'], w=['w0'])
                        P.ts('dve', stepsX[:], pow2[:], bis[:, 2:3], ALU.mult, r=['w0', 'pow2'], w=['steps'])
                        P.tt('dve', bis[:, 3:4], bis[:, 0:1], stepsX[:, 1:2], ALU.add, r=['mn', 'steps'], w=['mid'])
                        yield
                        for k in range(KBIS):
                            P.ts('dve', Mk[:, 0:S], isc[:, 0:S], bis[:, 3:4], ALU.is_ge, 0.0, ALU.add,
                                 accum=bis[:, 4:5], r=['isc', 'mid'], w=['cnt', 'Mk'])
                            P.ts('dve', bis[:, 5:6], bis[:, 4:5], float(KTOP) - 0.5, ALU.is_ge, -0.5, ALU.add,
                                 r=['cnt'], w=['sgn'])
                            P.stt('dve', bis[:, 3:4], bis[:, 5:6], stepsX[:, k + 1:k + 2], bis[:, 3:4], ALU.mult, ALU.add,
                                  r=['sgn', 'steps', 'mid'], w=['mid'])
                            yield
                        P.stt('dve', bis[:, 6:7], stepsX[:, KBIS:KBIS + 1], -0.5, bis[:, 3:4], ALU.mult, ALU.add,
                              r=['steps', 'mid'], w=['thr'])
                        P.ts('dve', Mk[:, 0:S], isc[:, 0:S], bis[:, 6:7], ALU.is_ge, r=['isc', 'thr'], w=['Mk'])
                    else:
                        yield
                        P.ts('dve', Mk[:, 0:S], isc[:, 0:S], -1.0e29, ALU.is_ge, r=['isc'], w=['Mk'])
                    yield
                    for i0 in range(0, j + 1, 8):
                        n8 = min(8, j + 1 - i0)
                        for ii in range(n8):
                            i = i0 + ii
                            P.tr(ps0b[:, ii, :], Mk[:, i * 128:(i + 1) * 128], ident[:], r=['Mk', 'ident'], w=['b0'])
                        P.cp('act', MT[:, i0:i0 + n8, :], ps0b[:, 0:n8, :], r=['b0'], w=[MTk])
                    yield

                def Y(s_i, j):
                    tt = s_i * NT + j
                    tsl = slice(j * 128, (j + 1) * 128)
                    MT = MTs[j % 2]
                    MTk = ('MT', j % 2)
                    sg, sgk = sgR.next()
                    P.dma('sp', sg[:], sg_s[tt * 128:(tt + 1) * 128, :], w=[sgk])
                    h_t, hk = hR.next()
                    P.dma('sp', h_t[:], hsrc[tt * 128:(tt + 1) * 128, :], w=[hk])
                    units = [(i, kind) for i in range(j + 1) for kind in range(4)]
                    N = len(units)
                    live = {}

                    Sof = {}

                    def st_qk(n):
                        i, kind = units[n]
                        ssl = slice(i * 128, (i + 1) * 128)
                        S_, Sk = SR.next()
                        if kind < 2:
                            r0 = 64 * kind
                            P.mm(S_[:].rearrange("p (a t) -> p a t", t=128), FM[r0:r0 + 64, 7, ssl], FM[r0:r0 + 64, 0:4, tsl],
                                 True, True, r=[('FM', 7), ('FM', 0), ('FM', 1), ('FM', 2), ('FM', 3)], w=[Sk])
                        else:
                            r0 = 64 * (kind - 2)
                            for hh in range(4):
                                P.mm(S_[:, hh * 128:(hh + 1) * 128], FM[r0:r0 + 64, 12 + hh, ssl], FM[r0:r0 + 64, 8 + hh, tsl],
                                     True, True, r=[('FM', 12 + hh), ('FM', 8 + hh)], w=[Sk])
                        Sof[n] = (S_, Sk)

                    def st_exp(n):
                        S_, Sk = Sof.pop(n)
                        E, Ek = ER.next()
                        P.act(E[:].rearrange("p a t -> p (a t)"), S_[:], AF.Exp, scale=0.125, r=[Sk], w=[Ek])
                        live[n] = [E, Ek, None, None]

                    def st_mask(n):
                        i, kind = units[n]
                        E, Ek = live[n][0], live[n][1]
                        if kind < 2:
                            Pm, Pk = PR.next()
                            P.tt('dve', Pm[:], E[:], bc_mid(MT[:, i, :], 4), ALU.mult, r=[Ek, MTk], w=[Pk])
                            live[n][2], live[n][3] = Pm, Pk
                        elif i == j:
                            P.tt('dve', E[:], E[:], bc_mid(caus01T[:], 4), ALU.mult, r=[Ek, 'caus'], w=[Ek])

                    def st_av(n):
                        i, kind = units[n]
                        E, Ek, Pm, Pk = live.pop(n)
                        if kind < 2:
                            for pr in range(4):
                                hh = 2 * pr + kind
                                P.mm(oa_ap(hh), Pm[:, pr, :], vaA[:, i, :], False, False, r=[Pk, 'vaA'],
                                     w=['OA', 'OB2'] if hh == 7 else ['OA'], acc0=True,
                                     first=(i == 0 and hh in (0, 7)))
                        else:
                            m_ = kind - 2
                            for hh in range(4):
                                P.mm(ob_ap(2 * hh + m_), E[:, hh, :], vbA[:, i, hh, :], False, False, r=[Ek, 'vbA'],
                                     w=['OB', 'OB2'] if hh == 3 else ['OB'], acc0=True,
                                     first=(i == 0 and m_ == 0 and hh in (0, 2)))

                    yield
                    for st in range(N + 3):
                        if st < N:
                            st_qk(st)
                        if 0 <= st - 1 < N:
                            st_exp(st - 1)
                        if 0 <= st - 2 < N:
                            st_mask(st - 2)
                        if 0 <= st - 3 < N:
                            st_av(st - 3)
                        if st % 2 == 1:
                            yield
                    yield
                    v7 = ps_OA[:, 0:455].rearrange("p (h c) -> p h c", c=65)
                    P.recip(rsA[:, 0:7], v7[:, :, 64], r=['OA'], w=['rsA'])
                    P.recip(rsA[:, 7:8], ps_OB[:, 2, 322:323], r=['OA', 'OB2'], w=['rsA'])
                    P.tt('dve', oa_bf[:, 0:448].rearrange("p (h d) -> p h d", d=64), v7[:, :, 0:64], bc_last(rsA[:, 0:7], 64),
                         ALU.mult, r=['OA', 'rsA'], w=['oa'])
                    P.ts('dve', oa_bf[:, 448:512], ps_OB[:, 2, 258:322], rsA[:, 7:8], ALU.mult, r=['OA', 'OB2', 'rsA'], w=['oa'])
                    yield
                    for b in range(3):
                        nq = 3 if b < 2 else 2
                        vq = ps_OB[:, b, 0:nq * 129].rearrange("p (q c) -> p q c", c=129)
                        P.recip(rsB[:, 3 * b:3 * b + nq], vq[:, :, 128], r=['OB', 'OB2'], w=['rsB'])
                    P.ts('dve', rsB[:, 8:16], rsB[:, 0:8], nlam_sb[:, l:l + 1], ALU.mult, r=['rsB', ('nlam', l)], w=['rsB2'])
                    for hh in range(4):
                        P.ts('dve', ob_t[:, hh, :], ob_ap(2 * hh)[:, 0:128], rsB[:, 2 * hh:2 * hh + 1], ALU.mult,
                             r=['OB', 'OB2', 'rsB'], w=['ob_t'])
                        P.stt('dve', ob_f[:, hh, :], ob_ap(2 * hh + 1)[:, 0:128], rsB[:, 8 + 2 * hh + 1:8 + 2 * hh + 2],
                              ob_t[:, hh, :], ALU.mult, ALU.add, r=['OB', 'OB2', 'rsB2', 'ob_t'], w=['ob_f'])
                    P.act(ob_sq, ob_f[:].rearrange("p h d -> p (h d)"), AF.Square, r=['ob_f'], w=['m1'])
                    P.red(sB2[:, 0:4], ob_sq.rearrange("p (h d) -> p h d", d=128), ALU.add, r=['m1'], w=['ssb'])
                    P.rsqrt(sB2[:, 4:8], sB2[:, 0:4], 128.0 * EPS, r=['ssb'], w=['rsb'])
                    P.tt('dve', ob_n, ob_f[:], bc_last(sB2[:, 4:8], 128), ALU.mult, r=['ob_f', 'rsb', 'm1'], w=['m1'])
                    P.tt('dve', ob_bf[:].rearrange("p (h d) -> p h d", d=128), ob_n, bc_mid(gsub[:], 4), ALU.mult,
                         r=['m1', 'gsub'], w=['ob'])
                    yield
                    S_, Sk = SR.next()
                    Sb = S_[:].bitcast(BF16).rearrange("p (b c) -> p b c", c=128)
                    for b in range(4):
                        P.tr(Sb[:, b, :], oa_bf[:, b * 128:(b + 1) * 128], ident[:], r=['oa', 'ident'], w=[Sk])
                        P.tr(Sb[:, 4 + b, :], ob_bf[:, b * 128:(b + 1) * 128], ident[:], r=['ob', 'ident'], w=[Sk])
                    P.cp('act', oT[:], Sb, r=[Sk], w=['oT'])
                    yield
                    for n2 in range(2):
                        nsl = slice(n2 * 512, (n2 + 1) * 512)
                        S_, Sk = SR.next()
                        for k in range(4):
                            P.mm(S_[:], oT[:, k, :], wA[:, k, nsl], k == 0, k == 3, r=['oT', ('wA', k)], w=[Sk])
                        P.tt('dve', m1[:, nsl], S_[:], sg[:, nsl], ALU.mult, r=[Sk, sgk], w=['m1'])
                    for n2 in range(2):
                        nsl = slice(n2 * 512, (n2 + 1) * 512)
                        S_, Sk = SR.next()
                        for k in range(4):
                            P.mm(S_[:], oT[:, 4 + k, :], wB[:, k, nsl], k == 0, k == 3, r=['oT', ('wB', k)], w=[Sk])
                        P.tt('dve', m2[:, nsl], S_[:], sg[:, 1024 + n2 * 512:1024 + (n2 + 1) * 512], ALU.mult,
                             r=[Sk, sgk], w=['m2'])
                    P.tt('dve', mix_bf[:], m1[:], m2[:], ALU.add, r=['m1', 'm2'], w=['mix'])
                    yield
                    S_, Sk = SR.next()
                    Sb = S_[:].bitcast(BF16).rearrange("p (b c) -> p b c", c=128)
                    for k in range(8):
                        P.tr(Sb[:, k, :], mix_bf[:, k * 128:(k + 1) * 128], ident[:], r=['mix', 'ident'], w=[Sk])
                    P.cp('act', mT[:], Sb, r=[Sk], w=['mT'])
                    h2, h2k = h2R.next()
                    for n2 in range(2):
                        nsl = slice(n2 * 512, (n2 + 1) * 512)
                        S_, Sk = SR.next()
                        for k in range(8):
                            P.mm(S_[:], mT[:, k, :], wO[:, k, nsl], k == 0, k == 7, r=['mT', ('wO', k)], w=[Sk])
                        P.tt('dve', h2[:, nsl], S_[:], h_t[:, nsl], ALU.add, r=[Sk, hk], w=[h2k])
                    P.dma('sp', h_s[tt * 128:(tt + 1) * 128, :], h2[:], r=[h2k])
                    yield

                def interleave(ga, na, gb, nb):
                    ia = ib = 0
                    while ga is not None or gb is not None:
                        pick_a = gb is None or (ga is not None and ia * nb <= ib * na)
                        if pick_a:
                            try:
                                next(ga)
                                ia += 1
                            except StopIteration:
                                ga = None
                        else:
                            try:
                                next(gb)
                                ib += 1
                            except StopIteration:
                                gb = None

                for s_i in range(NSEQ):
                    P.dma('sp', wiS[:], wi_s[s_i].rearrange("n p d -> p n d"), w=['wiS'])
                    for b in (4, 5, 6, 0, 1, 2, 3, 7, 8, 9, 10, 11, 12, 13, 14, 15):
                        P.dma('sp', FM[:, b, :], fm_s[s_i, :, b, :], w=[('FM', b)])
                    P.memset('pool', vaA[:], 1.0, w=['vaA'])
                    P.memset('pool', vbA[:], 1.0, w=['vbA'])
                    P.dma('sp', vaA[:, :, 0:64], va_s[s_i].rearrange("n p d -> p n d"), w=['vaA'])
                    for hh in range(4):
                        P.dma('sp', vbA[:, :, hh, 0:128], vb_s[s_i, :, :, hh * 128:(hh + 1) * 128].rearrange("n p d -> p n d"),
                              w=['vbA'])
                    for _ in X(s_i, 0):
                        pass
                    for j in range(NT):
                        gx = X(s_i, j + 1) if j + 1 < NT else None
                        interleave(Y(s_i, j), nY(j), gx, nX(j + 1) if j + 1 < NT else 1)
                P.barrier()
                P.emit()
            if debug == 2:
                break
            with ExitStack() as ph:
                wU = P.sb(ph, [128, 8, 2 * DFF], BF16)
                for k in range(8):
                    P.dma('pool', wU[:, k, :], w_up[l, k * 128:(k + 1) * 128, :], w=[('wU', k)])
                gs_bc = P.sb(ph, [128, D], F32)
                P.dma('sp', gs_bc[:], g_ffn[l:l + 1, :].to_broadcast([128, D]), w=['gs0'])
                P.ts('dve', gs_bc[:], gs_bc[:], float(math.sqrt(D)), ALU.mult, r=['gs0'], w=['gs'])
                cwr = P.sb(ph, [44, 4, 128], F32)
                for jj in range(3):
                    P.dma('sp', cwr[:, jj, :], conv_w[l, jj, :].rearrange("(c p) -> c p", p=128), w=['cwr'])
                P.dma('sp', cwr[:, 3, :], conv_b[l, :].rearrange("(c p) -> c p", p=128), w=['cwr'])
                cw = P.sb(ph, [128, 4, 44], F32)
                ps_c = P.ps(ph, [128, 4, 44], F32)
                for jj in range(4):
                    P.tr(ps_c[:, jj, :], cwr[:, jj, :], identf[0:44, 0:44], r=['cwr', 'identf'], w=['ps_c'])
                P.cp('dve', cw[:], ps_c[:], r=['ps_c'], w=['cw'])
                halo = P.sb(ph, [128, 44, 2], F32)
                hG = Rot(P, ph, 3, [128, D], F32, 'hG')
                uG = Rot(P, ph, 2, [128, D], BF16, 'uG')
                uTG = Rot(P, ph, 2, [128, 8, 512], BF16, 'uTG')
                ssR = Rot(P, ph, 4, [128, 2], F32, 'ss')
                junkA = P.sb(ph, [128, D], BF16)
                xsR = Rot(P, ph, 4, [128, 514], F32, 'xs')
                acR = Rot(P, ph, 6, [128, 512], F32, 'acc')
                glR = Rot(P, ph, 2, [128, 512], F32, 'gl')
                prR = Rot(P, ph, 2, [128, 4, NFC, 128], BF16, 'prod')
                ptrR = Rot(P, ph, 2, [128, 8, 128], BF16, 'ptr', psum=True)
                ppR = Rot(P, ph, 5, [128, 512], F32, 'pp', psum=True)

                uts = {}

                def p3a_pro(g):
                    uT, uTk = uTG.next()
                    uts[g] = (uT, uTk)
                    for a in range(4):
                        hg, hgk = hG.next()
                        P.dma('sp', hg[:], h_s[g * 512 + a * 128:g * 512 + (a + 1) * 128, :], w=[hgk])
                        ss, ssk = ssR.next()
                        P.act(junkA[:], hg[:], AF.Square, accum=ss[:, 0:1], r=[hgk], w=[ssk, 'junkA'])
                        P.rsqrt(ss[:, 1:2], ss[:, 0:1], D * EPS, r=[ssk], w=[ssk])
                        ug, ugk = uG.next()
                        P.stt('dve', ug[:], hg[:], ss[:, 1:2], gs_bc[:], ALU.mult, ALU.mult, r=[hgk, ssk, 'gs'], w=[ugk])
                        ptr, pk = ptrR.next()
                        for k in range(8):
                            P.tr(ptr[:, k, :], ug[:, k * 128:(k + 1) * 128], ident[:], r=[ugk, 'ident'], w=[pk])
                        P.cp('act', uT[:, :, a * 128:(a + 1) * 128], ptr[:], r=[pk], w=[uTk])
                        yield

                def p3a_q(g):
                    first = (g * 512) % L == 0
                    uT, uTk = uts.pop(g)
                    prod, prk = prR.next()
                    prevq = None

                    def fin(res, q):
                        gl, glk = glR.next()
                        P.act(gl[:], res[0][0][:], AF.Gelu_apprx_tanh, r=[res[0][1]], w=[glk])
                        P.tt('pool', prod[:, :, q, :], gl[:].rearrange("p (a t) -> p a t", t=128),
                             res[1][0][:].rearrange("p (a t) -> p a t", t=128), ALU.mult, r=[glk, res[1][1]], w=[prk])

                    for q in range(NFC):
                        res = []
                        for fc in (q, q + NFC):
                            pp, ppk = ppR.next()
                            for k in range(8):
                                P.mm(pp[:], wU[:, k, fc * 128:(fc + 1) * 128], uT[:, k, :], k == 0, k == 7,
                                     r=[uTk, ('wU', k)], w=[ppk])
                            xs, xsk = xsR.next()
                            P.cp('act', xs[:, 2:514], pp[:], r=[ppk], w=[xsk])
                            if first:
                                P.memset('pool', xs[:, 0:2], 0.0, w=[xsk])
                            else:
                                P.cp('pool', xs[:, 0:2], halo[:, fc, :], r=[('halo', fc)], w=[xsk])
                            acc, ack = acR.next()
                            P.act(acc[:], pp[:], AF.Identity, bias=cw[:, 3, fc:fc + 1], scale=cw[:, 2, fc:fc + 1],
                                  r=[ppk, 'cw'], w=[ack])
                            P.stt('dve', acc[:], xs[:, 1:513], cw[:, 1, fc:fc + 1], acc[:], ALU.mult, ALU.add,
                                  r=[xsk, 'cw', ack], w=[ack])
                            P.stt('dve', acc[:], xs[:, 0:512], cw[:, 0, fc:fc + 1], acc[:], ALU.mult, ALU.add,
                                  r=[xsk, 'cw', ack], w=[ack])
                            P.cp('pool', halo[:, fc, :], xs[:, 512:514], r=[xsk], w=[('halo', fc)])
                            res.append((acc, ack))
                        if prevq is not None:
                            fin(*prevq)
                        prevq = (res, q)
                        yield
                    fin(*prevq)
                    yield
                    for a in range(4):
                        P.dma('sp', pr_s[g * 4 + a, :, :], prod[:, a, :, :].rearrange("p q t -> p (q t)"), r=[prk])
                    yield

                def interleave2(ga, na, gb, nb):
                    ia = ib = 0
                    while ga is not None or gb is not None:
                        pick_a = gb is None or (ga is not None and ia * nb <= ib * na)
                        if pick_a:
                            try:
                                next(ga)
                                ia += 1
                            except StopIteration:
                                ga = None
                        else:
                            try:
                                next(gb)
                                ib += 1
                            except StopIteration:
                                gb = None

                for _ in p3a_pro(0):
                    pass
                for g in range(NG):
                    interleave2(p3a_q(g), NFC + 2, p3a_pro(g + 1) if g + 1 < NG else None, 4)
                P.barrier()
                P.emit()
            if debug == 3:
                break
            with ExitStack() as ph:
                wD = P.sb(ph, [128, NFC, D], BF16)
                for fc in range(NFC):
                    P.dma('pool', wD[:, fc, :], w_down[l, fc * 128:(fc + 1) * 128, :], w=[('wD', fc)])
                wG = P.sb(ph, [128, 8, D], BF16)
                for k in range(8):
                    P.dma('pool', wG[:, k, :], w_pg[l, k * 128:(k + 1) * 128, :], w=[('wG', k)])
                wPp = P.sb(ph, [128, 2, D], BF16)
                for k in range(2):
                    P.dma('pool', wPp[:, k, :], w_pp[l, k * 128:(k + 1) * 128, :], w=[('wPp', k)])
                gs_bc = P.sb(ph, [128, D], F32)
                P.dma('sp', gs_bc[:], g_ple[l:l + 1, :].to_broadcast([128, D]), w=['gs0'])
                P.ts('dve', gs_bc[:], gs_bc[:], float(math.sqrt(D)), ALU.mult, r=['gs0'], w=['gs'])
                hR = Rot(P, ph, 5, [128, D], F32, 'h')
                h3R = Rot(P, ph, 5, [128, D], F32, 'h3')
                h4R = Rot(P, ph, 5, [128, D], F32, 'h4')
                prR = Rot(P, ph, 5, [128, NFC, 128], BF16, 'pr')
                pbR = Rot(P, ph, 5, [128, PLE], BF16, 'pb')
                pTR = Rot(P, ph, 5, [128, 2, 128], BF16, 'pT')
                ssR = Rot(P, ph, 5, [128, 2], F32, 'ss')
                uR = Rot(P, ph, 5, [128, D], BF16, 'u')
                uTR = Rot(P, ph, 5, [128, 8, 128], BF16, 'uT')
                sgR = Rot(P, ph, 2, [128, D], F32, 'sgt')
                tmR = Rot(P, ph, 2, [128, D], F32, 'tm')
                junkA = P.sb(ph, [128, D], BF16)
                psD = P.ps(ph, [128, 2, 512], F32)
                psG = P.ps(ph, [128, 2, 512], F32)
                psP = P.ps(ph, [128, 2, 512], F32)
                ptrR = Rot(P, ph, 2, [128, 8, 128], BF16, 'ptr', psum=True)
                hdst = y if l == DEPTH - 1 else h_s
                def p3b_tile(tt):
                    rows = slice(tt * 128, (tt + 1) * 128)
                    h_t, hk = hR.next()
                    P.dma('sp', h_t[:], h_s[rows, :], w=[hk])
                    pr, prk = prR.next()
                    P.dma('sp', pr[:].rearrange("p q t -> p (q t)"), pr_s[tt, :, :], w=[prk])
                    pb, pbk = pbR.next()
                    P.dma('pool', pb[:], p_in[l, rows, :], w=[pbk])
                    yield
                    h3, h3k = h3R.next()
                    for n2 in range(2):
                        nsl = slice(n2 * 512, (n2 + 1) * 512)
                        for fc in range(NFC):
                            P.mm(psD[:, n2, :], pr[:, fc, :], wD[:, fc, nsl], fc == 0, fc == NFC - 1,
                                 r=[prk, ('wD', fc)], w=[('psD', n2)])
                        P.tt('dve', h3[:, nsl], psD[:, n2, :], h_t[:, nsl], ALU.add, r=[('psD', n2), hk], w=[h3k])
                    ptr, pk = ptrR.next()
                    for k in range(2):
                        P.tr(ptr[:, k, :], pb[:, k * 128:(k + 1) * 128], ident[:], r=[pbk, 'ident'], w=[pk])
                    pT, pTk = pTR.next()
                    P.cp('act', pT[:], ptr[:, 0:2, :], r=[pk], w=[pTk])
                    ss, ssk = ssR.next()
                    P.act(junkA[:], h3[:], AF.Square, accum=ss[:, 0:1], r=[h3k], w=[ssk, 'junkA'])
                    yield
                    P.rsqrt(ss[:, 1:2], ss[:, 0:1], D * EPS, r=[ssk], w=[ssk])
                    u, uk = uR.next()
                    P.stt('dve', u[:], h3[:], ss[:, 1:2], gs_bc[:], ALU.mult, ALU.mult, r=[h3k, ssk, 'gs'], w=[uk])
                    yield
                    ptr, pk = ptrR.next()
                    for k in range(8):
                        P.tr(ptr[:, k, :], u[:, k * 128:(k + 1) * 128], ident[:], r=[uk, 'ident'], w=[pk])
                    uT, uTk = uTR.next()
                    P.cp('act', uT[:], ptr[:], r=[pk], w=[uTk])
                    yield
                    sgt, sgk = sgR.next()
                    tm, tmk = tmR.next()
                    h4, h4k = h4R.next()
                    for n2 in range(2):
                        nsl = slice(n2 * 512, (n2 + 1) * 512)
                        for k in range(8):
                            P.mm(psG[:, n2, :], uT[:, k, :], wG[:, k, nsl], k == 0, k == 7, r=[uTk, ('wG', k)], w=[('psG', n2)])
                        P.act(sgt[:, nsl], psG[:, n2, :], AF.Sigmoid, r=[('psG', n2)], w=[sgk])
                        for k in range(2):
                            P.mm(psP[:, n2, :], pT[:, k, :], wPp[:, k, nsl], k == 0, k == 1, r=[pTk, ('wPp', k)], w=[('psP', n2)])
                        P.tt('dve', tm[:, nsl], psP[:, n2, :], sgt[:, nsl], ALU.mult, r=[('psP', n2), sgk], w=[tmk])
                        P.tt('dve', h4[:, nsl], tm[:, nsl], h3[:, nsl], ALU.add, r=[tmk, h3k], w=[h4k])
                    P.dma('sp', hdst[rows, :], h4[:], r=[h4k])
                    yield

                run_pipelined([(lambda tt=tt: p3b_tile(tt)) for tt in range(NTT)], 5, 1)
                P.barrier()
                P.emit()
    return nc


def rope_table(L):
    NT = L // 128
    inv = 1.0 / (10000.0 ** (np.arange(0, 64, 2, dtype=np.float32) / np.float32(64.0)))
    ang = np.arange(L, dtype=np.float32)[:, None] * inv[None, :].astype(np.float32)
    cs = np.concatenate([np.cos(ang), np.sin(ang)], axis=1).astype(np.float32)
    return np.ascontiguousarray(cs.reshape(NT, 128, 64).transpose(1, 0, 2))


_CACHE = {}


def run(inputs, L, NSEQ, DEPTH, ncores, debug=False):
    key = (L, NSEQ, DEPTH, debug)
    if key not in _CACHE:
        _CACHE[key] = build(L, NSEQ, DEPTH, debug)
    nc = _CACHE[key]
    T = NSEQ * L
    xs = np.ascontiguousarray(inputs['x'], dtype=np.float32).reshape(ncores, T, D)
    ps = np.ascontiguousarray(inputs['p'], dtype=np.float32).reshape(DEPTH, ncores, T, PLE)
    cs = rope_table(L)
    in_maps = []
    for c in range(ncores):
        m = {k: np.ascontiguousarray(v, dtype=np.float32) for k, v in inputs.items() if k not in ('x', 'p')}
        m['x'] = xs[c]
        m['p'] = np.ascontiguousarray(ps[:, c])
        m['cs_tab'] = cs
        in_maps.append(m)
    res = run_bass_kernel_spmd(nc, in_maps, core_ids=list(range(ncores)))
    return res


def kernel(**inputs):
    B, L, _ = inputs['x'].shape
    DEPTH = inputs['p'].shape[0]
    ncores = 8
    NSEQ = B // ncores
    res = run(inputs, L, NSEQ, DEPTH, ncores)
    out = np.stack([np.asarray(r['y'], dtype=np.float32) for r in res.results], axis=0)
    return out.reshape(B, L, D)
```
